# Optimizing a Trainium2 kernel written in Bass

```python
import math
import jax, jax.numpy as jnp
from jax import lax
import numpy as np

D_MODEL = 1024
BATCH = 8
SEQ = 2048
DEPTH = 1
DEC_BATCH = 16
DEC_SEQ = 64
PAST_LEN = 2048

CHUNK = 64
Q_BLOCK = 128
HG_KEY_DIM = 128
N_HG_HEADS = D_MODEL // HG_KEY_DIM
HG_VAL_DIM = D_MODEL // N_HG_HEADS
DA_HEAD_DIM = 64
N_DA_HEADS = D_MODEL // (2 * DA_HEAD_DIM)
DA_V_DIM = 2 * DA_HEAD_DIM
D_FF = 128 * ((8 * D_MODEL // 3 + 127) // 128)
CONV_WIDTH = 3
LN_EPS = 1e-5
DEEPNORM_ALPHA = (2 * DEPTH) ** 0.25
DEEPNORM_BETA = (8 * DEPTH) ** -0.25
DA_SCALE = DA_HEAD_DIM ** -0.5

HG_QK = N_HG_HEADS * HG_KEY_DIM
HG_V = N_HG_HEADS * HG_VAL_DIM
DA_QK = N_DA_HEADS * 2 * DA_HEAD_DIM
DA_V = N_DA_HEADS * DA_V_DIM
IN_SPLITS = (HG_QK, HG_QK, HG_V, HG_V, DA_QK, DA_QK, DA_V, D_MODEL, D_MODEL)
IN_WIDTH = sum(IN_SPLITS)

kernel_name = "hgrn2_diffattn_streaming_encoder_step"


def _layer_norm(x, g, b):
    xf = x.astype(jnp.float32)
    mu = jnp.mean(xf, axis=-1, keepdims=True)
    var = jnp.mean(jnp.square(xf - mu), axis=-1, keepdims=True)
    return ((xf - mu) * lax.rsqrt(var + LN_EPS) * g + b).astype(x.dtype)


def _rms_norm(x, g):
    xf = x.astype(jnp.float32)
    return (xf * lax.rsqrt(jnp.mean(jnp.square(xf), axis=-1, keepdims=True) + LN_EPS) * g).astype(x.dtype)


def _split_in(u):
    outs, off = [], 0
    for w in IN_SPLITS:
        outs.append(u[..., off:off + w])
        off += w
    return outs


def _hgrn2(q, k, v, logf, s0):
    B, T, H, K = q.shape
    V = v.shape[-1]
    C = min(CHUNK, T)
    n = T // C

    def to_chunks(a):
        return a.reshape(B, n, C, H, a.shape[-1]).transpose(1, 0, 3, 2, 4)

    causal = jnp.tril(jnp.ones((C, C), dtype=bool))[:, :, None]

    def step(S, inp):
        qc, kc, vc, gc = inp
        b = jnp.cumsum(gc, axis=2)
        diff = b[:, :, :, None, :] - b[:, :, None, :, :]
        dec = jnp.exp(jnp.where(causal, diff, -jnp.inf))
        att = jnp.einsum('bhtk,bhtsk,bhsk->bhts', qc, dec, kc)
        o = jnp.einsum('bhts,bhsv->bhtv', att, vc) + jnp.einsum('bhtk,bhkv->bhtv', qc * jnp.exp(b), S)
        b_last = b[:, :, -1]
        S_new = jnp.exp(b_last)[..., None] * S + jnp.einsum(
            'bhsk,bhsv->bhkv', kc * jnp.exp(b_last[:, :, None, :] - b), vc)
        return S_new, o

    S, o = lax.scan(step, s0, (to_chunks(q), to_chunks(k), to_chunks(v), to_chunks(logf)))
    o = o.transpose(1, 0, 3, 2, 4).reshape(B, T, H, V)
    return S, o


def _diff_attend(q, k, v, lam, mask):
    s = jnp.einsum('bqhcd,bshcd->bhcqs', q.astype(jnp.float32), k.astype(jnp.float32)) * DA_SCALE
    s = jnp.where(mask, s, -jnp.inf)
    p = jax.nn.softmax(s, axis=-1)
    w = p[:, :, 0] - lam * p[:, :, 1]
    return jnp.einsum('bhqs,bshe->bqhe', w.astype(v.dtype), v)


def _diff_attn_prompt(q, k, v, lam):
    B, T = q.shape[:2]
    nb = T // Q_BLOCK
    qb = q.reshape(B, nb, Q_BLOCK, N_DA_HEADS, 2, DA_HEAD_DIM).transpose(1, 0, 2, 3, 4, 5)
    k_chunk = jnp.arange(T) // CHUNK

    def block(args):
        qi, i = args
        q_chunk = (i * Q_BLOCK + jnp.arange(Q_BLOCK)) // CHUNK
        mask = k_chunk[None, :] <= q_chunk[:, None]
        return _diff_attend(qi, k, v, lam, mask)

    o = lax.map(block, (qb, jnp.arange(nb)))
    return o.transpose(1, 0, 2, 3, 4).reshape(B, T, N_DA_HEADS, DA_V_DIM)


def _layer(x, l, lb, hg_s0, past_k, past_v, ffn_buf0, w_in, hg_norm_g, lq1, lk1, lq2, lk2,
           subln_g, w_br_hg, w_br_da, w_out, ln1_g, ln1_b, w_up, conv_w, conv_b, w_down,
           ln2_g, ln2_b):
    B, T, _ = x.shape
    hq, hf, hi, hog, dq, dk, dv, g_hg, g_da = _split_in(x @ w_in)

    lb_hk = lb.reshape(N_HG_HEADS, HG_KEY_DIM)
    q = hq.reshape(B, T, N_HG_HEADS, HG_KEY_DIM).astype(jnp.float32)
    f = lb_hk + (1.0 - lb_hk) * jax.nn.sigmoid(hf.reshape(B, T, N_HG_HEADS, HG_KEY_DIM).astype(jnp.float32))
    logf = jnp.log(f)
    kk = 1.0 - f
    vv = hi.reshape(B, T, N_HG_HEADS, HG_VAL_DIM).astype(jnp.float32)
    s_new, o = _hgrn2(q, kk, vv, logf, hg_s0.astype(jnp.float32))
    o = _rms_norm(o, hg_norm_g.astype(jnp.float32)) * jax.nn.sigmoid(
        hog.reshape(B, T, N_HG_HEADS, HG_VAL_DIM).astype(jnp.float32))
    y_hg = o.reshape(B, T, HG_V).astype(x.dtype)

    lam_init = 0.8 - 0.6 * math.exp(-0.3 * l)
    lam = (jnp.exp(jnp.sum(lq1.astype(jnp.float32) * lk1.astype(jnp.float32)))
           - jnp.exp(jnp.sum(lq2.astype(jnp.float32) * lk2.astype(jnp.float32))) + lam_init)
    q5 = dq.reshape(B, T, N_DA_HEADS, 2, DA_HEAD_DIM)
    k5 = dk.reshape(B, T, N_DA_HEADS, 2, DA_HEAD_DIM)
    v4 = dv.reshape(B, T, N_DA_HEADS, DA_V_DIM)
    if past_k is None:
        a = _diff_attn_prompt(q5, k5, v4, lam)
    else:
        P = past_k.shape[1]
        k_all = jnp.concatenate([past_k.reshape(B, P, N_DA_HEADS, 2, DA_HEAD_DIM).astype(k5.dtype), k5], axis=1)
        v_all = jnp.concatenate([past_v.astype(v4.dtype), v4], axis=1)
        mask = jnp.ones((T, P + T), dtype=bool)
        a = _diff_attend(q5, k_all, v_all, lam, mask)
    a = _rms_norm(a, subln_g) * (1.0 - lam_init)
    y_da = a.reshape(B, T, DA_V)

    m = jax.nn.sigmoid(g_hg) * (y_hg @ w_br_hg) + jax.nn.sigmoid(g_da) * (y_da @ w_br_da)
    h = _layer_norm(DEEPNORM_ALPHA * x + m @ w_out, ln1_g, ln1_b)

    up = h @ w_up
    buf = jnp.concatenate([ffn_buf0.astype(up.dtype), up], axis=1)
    c = conv_b
    for j in range(CONV_WIDTH):
        c = c + conv_w[j] * buf[:, j:j + T]
    val, gate = jnp.split(c, 2, axis=-1)
    ffn = (jax.nn.silu(gate) * val) @ w_down
    out = _layer_norm(DEEPNORM_ALPHA * h + ffn, ln2_g, ln2_b)

    new_k = k5.reshape(B, T, N_DA_HEADS, 2 * DA_HEAD_DIM)
    return out, new_k, v4, s_new.astype(x.dtype), buf[:, -(CONV_WIDTH - 1):]


def setup_inputs(seed: int = 0) -> dict:
    key = jax.random.key(seed)
    ks = jax.random.split(key, 26)

    def nrm(k, shape, scale):
        return jax.random.normal(k, shape, jnp.float32) * scale

    off_hi = 2 * HG_QK
    off_dv = 2 * HG_QK + 2 * HG_V + 2 * DA_QK
    col_scale = jnp.concatenate([
        jnp.ones((off_hi,), jnp.float32), jnp.full((HG_V,), DEEPNORM_BETA, jnp.float32),
        jnp.ones((HG_V + 2 * DA_QK,), jnp.float32), jnp.full((DA_V,), DEEPNORM_BETA, jnp.float32),
        jnp.ones((2 * D_MODEL,), jnp.float32)])
    assert off_dv == off_hi + 2 * HG_V + 2 * DA_QK - HG_V + HG_V
    return {
        "x_prompt": nrm(ks[0], (BATCH, SEQ, D_MODEL), 1.0),
        "x_sample": nrm(ks[1], (DEC_BATCH, DEC_SEQ, D_MODEL), 1.0),
        "cache_k": nrm(ks[2], (DEPTH, DEC_BATCH, PAST_LEN, N_DA_HEADS, 2 * DA_HEAD_DIM), 1.0),
        "cache_v": nrm(ks[3], (DEPTH, DEC_BATCH, PAST_LEN, N_DA_HEADS, DA_V_DIM), DEEPNORM_BETA),
        "state_hgrn": nrm(ks[4], (DEPTH, DEC_BATCH, N_HG_HEADS, HG_KEY_DIM, HG_VAL_DIM), 0.5),
        "state_ffn_conv": nrm(ks[5], (DEPTH, DEC_BATCH, CONV_WIDTH - 1, 2 * D_FF), 0.5),
        "w_in": nrm(ks[6], (DEPTH, D_MODEL, IN_WIDTH), D_MODEL ** -0.5) * col_scale,
        "hg_lb_logits": nrm(ks[7], (DEPTH + 1, HG_QK), 0.5),
        "hg_norm_g": 1.0 + nrm(ks[8], (DEPTH, HG_VAL_DIM), 0.02),
        "da_lambda_q1": nrm(ks[9], (DEPTH, DA_HEAD_DIM), 0.1),
        "da_lambda_k1": nrm(ks[10], (DEPTH, DA_HEAD_DIM), 0.1),
        "da_lambda_q2": nrm(ks[11], (DEPTH, DA_HEAD_DIM), 0.1),
        "da_lambda_k2": nrm(ks[12], (DEPTH, DA_HEAD_DIM), 0.1),
        "da_subln_g": 1.0 + nrm(ks[13], (DEPTH, DA_V_DIM), 0.02),
        "w_br_hg": nrm(ks[14], (DEPTH, HG_V, D_MODEL), HG_V ** -0.5 * DEEPNORM_BETA),
        "w_br_da": nrm(ks[15], (DEPTH, DA_V, D_MODEL), DA_V ** -0.5 * DEEPNORM_BETA),
        "w_out": nrm(ks[16], (DEPTH, D_MODEL, D_MODEL), D_MODEL ** -0.5 * DEEPNORM_BETA),
        "ln1_g": 1.0 + nrm(ks[17], (DEPTH, D_MODEL), 0.02),
        "ln1_b": nrm(ks[18], (DEPTH, D_MODEL), 0.02),
        "w_up": nrm(ks[19], (DEPTH, D_MODEL, 2 * D_FF), D_MODEL ** -0.5 * DEEPNORM_BETA),
        "conv_w": nrm(ks[20], (DEPTH, CONV_WIDTH, 2 * D_FF), CONV_WIDTH ** -0.5),
        "conv_b": nrm(ks[21], (DEPTH, 2 * D_FF), 0.02),
        "w_down": nrm(ks[22], (DEPTH, D_FF, D_MODEL), D_FF ** -0.5 * DEEPNORM_BETA),
        "ln2_g": 1.0 + nrm(ks[23], (DEPTH, D_MODEL), 0.02),
        "ln2_b": nrm(ks[24], (DEPTH, D_MODEL), 0.02),
    }


def reference(x_prompt, x_sample, cache_k, cache_v, state_hgrn, state_ffn_conv, w_in, hg_lb_logits,
              hg_norm_g, da_lambda_q1, da_lambda_k1, da_lambda_q2, da_lambda_k2, da_subln_g,
              w_br_hg, w_br_da, w_out, ln1_g, ln1_b, w_up, conv_w, conv_b, w_down, ln2_g, ln2_b):
    lb_all = jnp.cumsum(jax.nn.softmax(hg_lb_logits.astype(jnp.float32), axis=0), axis=0)[:DEPTH]
    xp, xs = x_prompt, x_sample
    kp, vp, sp, cp, kd, vd, sd, cd = [], [], [], [], [], [], [], []
    for l in range(DEPTH):
        w = (w_in[l], hg_norm_g[l], da_lambda_q1[l], da_lambda_k1[l], da_lambda_q2[l], da_lambda_k2[l],
             da_subln_g[l], w_br_hg[l], w_br_da[l], w_out[l], ln1_g[l], ln1_b[l], w_up[l], conv_w[l],
             conv_b[l], w_down[l], ln2_g[l], ln2_b[l])
        hg0 = jnp.zeros((xp.shape[0], N_HG_HEADS, HG_KEY_DIM, HG_VAL_DIM), jnp.float32)
        buf0 = jnp.zeros((xp.shape[0], CONV_WIDTH - 1, 2 * D_FF), xp.dtype)
        xp, k1, v1, s1, c1 = _layer(xp, l, lb_all[l], hg0, None, None, buf0, *w)
        xs, k2, v2, s2, c2 = _layer(xs, l, lb_all[l], state_hgrn[l], cache_k[l], cache_v[l],
                                    state_ffn_conv[l], *w)
        kp.append(k1); vp.append(v1); sp.append(s1); cp.append(c1)
        kd.append(k2); vd.append(v2); sd.append(s2); cd.append(c2)
    k_prompt = jnp.stack(kp); v_prompt = jnp.stack(vp)
    hgrn_prompt = jnp.stack(sp); conv_prompt = jnp.stack(cp)
    k_sample = jnp.stack(kd); v_sample = jnp.stack(vd)
    hgrn_sample = jnp.stack(sd); conv_sample = jnp.stack(cd)
    return (xp, xs, k_prompt, v_prompt, hgrn_prompt, conv_prompt, k_sample, v_sample, hgrn_sample, conv_sample)
```

```python
import math
from collections import defaultdict
from contextlib import ExitStack

import numpy as np
import concourse.bass as bass
import concourse.mybir as mybir
from concourse.bass_utils import run_bass_kernel_spmd

F32 = mybir.dt.float32
BF16 = mybir.dt.bfloat16
ALU = mybir.AluOpType
AF = mybir.ActivationFunctionType
AX = mybir.AxisListType

P = 128
D = 1024
T = 2048
SQ = 64
TS = 2 * SQ
NTOK = T + TS
H = 8
DFF = 2816
NJ = DFF // P
PAST = 2048
LN_EPS = 1e-5
ALPHA = 2.0 ** 0.25
LAM_INIT = 0.8 - 0.6 * math.exp(-0.3 * 0)
DA_SCALE = 64 ** -0.5
HTW = 2184
AW = 2180
JG = 3

SAME_ENGINE_SYNC = True
EXCL_PSUM = True


class Tok:
    __slots__ = ("w", "r", "excl")

    def __init__(self):
        self.w = None
        self.r = {}
        self.excl = False


class _Rec:
    def __getattr__(self, name):
        return lambda *a, **k: (name, a, k)


_REC = _Rec()


class Op:
    __slots__ = ("fn", "deps", "inc", "dma")

    def __init__(self, fn, deps, dma):
        self.fn = fn(_REC) if fn is not None else None
        self.deps = deps
        self.inc = False
        self.dma = dma


class Prog:
    ENGS = ("pe", "act", "dve", "pool", "sp")

    def __init__(self):
        self.ops = {e: [] for e in self.ENGS}
        self.slots = {}
        self.tk = defaultdict(Tok)

    def t(self, *key):
        t = self.tk[key]
        if key[0] in ("ps", "pt"):
            t.excl = True
        return t

    def add(self, eng, fn, r=(), w=(), dma=None):
        if EXCL_PSUM:
            xr = [t for t in r if t.excl]
            if xr:
                r = [t for t in r if not t.excl]
                w = list(w) + [t for t in xr if t not in w]
        ops = self.ops[eng]
        idx = len(ops)
        deps = {}

        def need(k, v):
            if k[0] == "e" and k[1] == eng and (eng == "pe" or not SAME_ENGINE_SYNC) and dma is None:
                return
            if deps.get(k, -1) < v:
                deps[k] = v

        for t in r:
            if t.w is not None:
                need(*t.w)
        for t in w:
            if t.w is not None:
                need(*t.w)
            for k, v in t.r.items():
                need(k, v)
        for k, v in deps.items():
            if k[0] == "e":
                self.ops[k[1]][v].inc = True
        if dma is not None:
            cnt = self.slots.get(dma, 0) + 16
            self.slots[dma] = cnt
            ev = (("d", dma), cnt)
        else:
            ev = (("e", eng), idx)
        for t in w:
            t.w = ev
            t.r = {}
        for t in r:
            if t.w is not ev:
                if t.r.get(ev[0], -1) < ev[1]:
                    t.r[ev[0]] = ev[1]
        ops.append(Op(fn, deps, dma))
        return ev

    def barrier(self):
        last = {}
        for e in self.ENGS:
            for idx in range(len(self.ops[e]) - 1, -1, -1):
                op = self.ops[e][idx]
                if op.fn is not None and op.dma is None:
                    last[e] = idx
                    break
        for e in self.ENGS:
            deps = {}
            for e2, idx in last.items():
                if e2 != e:
                    deps[("e", e2)] = idx
                    self.ops[e2][idx].inc = True
            for k, cnt in self.slots.items():
                deps[("d", k)] = cnt
            self.ops[e].append(Op(None, deps, None))
        for t in self.tk.values():
            t.w = None
            t.r = {}

    def emit(self, nc, st):
        sems = {e: st.enter_context(nc.semaphore("s_" + e)) for e in self.ENGS}
        dsem = {k: st.enter_context(nc.semaphore("d%d" % i)) for i, k in enumerate(self.slots)}
        incval = {}
        for e in self.ENGS:
            c = 0
            vals = []
            for op in self.ops[e]:
                if op.inc:
                    c += 1
                vals.append(c)
            incval[e] = vals
        block = st.enter_context(nc.Block())

        def run(e, engine):
            waited = {}
            for op in self.ops[e]:
                for k, v in op.deps.items():
                    if k[0] == "e":
                        sem = sems[k[1]]
                        val = incval[k[1]][v]
                    else:
                        sem = dsem[k[1]]
                        val = v
                    if waited.get(k, 0) < val:
                        engine.wait_ge(sem, val)
                        waited[k] = val
                if op.fn is None:
                    continue
                ins = getattr(engine, op.fn[0])(*op.fn[1], **op.fn[2])
                if op.dma is not None:
                    ins.then_inc(dsem[op.dma], 16)
                elif op.inc:
                    ins.then_inc(sems[e], 1)

        @block.tensor
        def _(eng):
            run("pe", eng)

        @block.scalar
        def _(eng):
            run("act", eng)

        @block.vector
        def _(eng):
            run("dve", eng)

        @block.gpsimd
        def _(eng):
            run("pool", eng)

        @block.sync
        def _(eng):
            run("sp", eng)


class _Stop(Exception):
    pass


STOP = None
EXPT = 0


def build_nc():
    nc = bass.Bass("TRN2", target_bir_lowering=False, dynamic_dma_scratch_size=12288)
    pg = Prog()
    tk = pg.t
    st = ExitStack()

    def chk(n):
        if STOP is not None and STOP == n:
            raise _Stop()

    try:
        _build_body(nc, pg, tk, st, chk)
    except _Stop:
        pg.barrier()
    pg.emit(nc, st)
    st.close()
    return nc


def _build_body(nc, pg, tk, st, chk):

    def din(name, shape):
        return nc.dram_tensor(name, list(shape), F32, kind="ExternalInput").ap()

    def dout(name, shape):
        return nc.dram_tensor(name, list(shape), F32, kind="ExternalOutput").ap()

    x_p = din("x_p", [T, D])
    x_s = din("x_s", [TS, D])
    cache_k = din("cache_k", [2, PAST, H, P])
    cache_v = din("cache_v", [2, PAST, H, P])
    state_hgrn = din("state_hgrn", [2, H, P, P])
    state_conv = din("state_conv", [4, 2 * DFF])
    w_in = din("w_in", [D, 9 * D])
    lb_logits = din("lb_logits", [16, P])
    hg_norm_g = din("hg_norm_g", [P, 1])
    lq1 = din("lq1", [1, 64])
    lk1 = din("lk1", [1, 64])
    lq2 = din("lq2", [1, 64])
    lk2 = din("lk2", [1, 64])
    subln_g = din("subln_g", [P, 1])
    w_br_hg = din("w_br_hg", [D, D])
    w_br_da = din("w_br_da", [D, D])
    w_out = din("w_out", [D, D])
    ln1_g = din("ln1_g", [1, D])
    ln1_b = din("ln1_b", [1, D])
    w_up = din("w_up", [D, 2 * DFF])
    conv_wb = din("conv_wb", [4, 2 * DFF])
    w_down = din("w_down", [DFF, D])
    ln2_g = din("ln2_g", [1, D])
    ln2_b = din("ln2_b", [1, D])

    y_p = dout("y_p", [T, D])
    y_s = dout("y_s", [TS, D])
    k_p = dout("k_p", [T, H, P])
    v_p = dout("v_p", [T, H, P])
    hg_p = dout("hg_p", [H, P, P])
    conv_p = dout("conv_p", [2, 2 * DFF])
    k_s = dout("k_s", [TS, H, P])
    v_s = dout("v_s", [TS, H, P])
    hg_s = dout("hg_s", [2, H, P, P])
    conv_s = dout("conv_s", [2, 2, 2 * DFF])

    MAIN_BYTES = 216896
    MAIN = st.enter_context(nc.sbuf_tensor("main", [P, MAIN_BYTES // 4], F32))
    arena = {"off": 0, "peak": 0}

    def sb(name, shape, dt=F32):
        n = 1
        for s_ in shape[1:]:
            n *= s_
        nbytes = n * (4 if dt is F32 else 2)
        nb_al = (nbytes + 31) // 32 * 32
        off = arena["off"]
        assert off + nb_al <= MAIN_BYTES, ("SBUF arena overflow", name, off, nb_al)
        arena["off"] = off + nb_al
        arena["peak"] = max(arena["peak"], arena["off"])
        v = MAIN[:, off // 4:(off + nb_al) // 4]
        if dt is not F32:
            v = v.bitcast(dt)
        v = v[0:shape[0], 0:n]
        if len(shape) == 3:
            v = v.rearrange("p (a b) -> p a b", b=shape[2])
        elif len(shape) == 4:
            v = v.rearrange("p (a b c) -> p a b c", b=shape[2], c=shape[3])
        return v

    def ps(name, shape, dt=F32):
        return st.enter_context(nc.psum_tensor(name, list(shape), dt))

    out_events = []
    scratch = {"off": 160000}

    def sbs(name, shape, dt=F32):
        save = arena["off"]
        arena["off"] = scratch["off"]
        v = sb(name, shape, dt)
        scratch["off"] = arena["off"]
        arena["off"] = save
        return v

    PS = [ps("ps%d" % i, [P, 512], F32) for i in range(7)]
    PS.append(ps("ps7", [P, 512], F32))
    PT = PS[7][:].bitcast(BF16)

    ones = sbs("ones", [P, 132])
    ident = sb("ident", [P, P])
    identb = sb("identb", [P, P], BF16)
    hmask = sb("hmask", [P, P])
    rmask = sb("rmask", [P, 512], BF16)
    pg.add("pool", lambda e: e.memset(ones[:], 1.0), w=[tk("ones")])
    pg.add("pool", lambda e: e.affine_select(out=ident[:], in_=ones[:, 0:P], pattern=[[-1, P]],
                                              compare_op=ALU.is_equal, fill=0.0, base=0,
                                              channel_multiplier=1),
           r=[tk("ones")], w=[tk("ident")])
    pg.add("dve", lambda e: e.tensor_copy(out=identb[:], in_=ident[:]), r=[tk("ident")], w=[tk("identb")])
    pg.add("pool", lambda e: e.affine_select(out=hmask[:], in_=ones[:, 0:P], pattern=[[1, P]],
                                              compare_op=ALU.is_ge, fill=0.0, base=0,
                                              channel_multiplier=-1),
           r=[tk("ones")], w=[tk("hmask")])
    pg.add("pool", lambda e: e.memset(hmask[0:64, 64:128], 0.0), w=[tk("hmask")])
    pg.add("pool", lambda e: e.memset(rmask[:], 1.0), w=[tk("rmask")])
    pg.add("pool", lambda e: e.memset(rmask[:].rearrange("p (a b) -> p a b", b=64)[:, :, 0:1], 0.0),
           w=[tk("rmask")])

    lbrows = sbs("lbrows", [16, P])
    lgc = sbs("lgc", [P, 16])
    lbv = sb("lbv", [P, H])
    omlv = sb("omlv", [P, H])
    tmp8 = sbs("tmp8", [P, H])
    pg.add("sp", lambda e: e.dma_start(out=lbrows[:], in_=lb_logits), w=[tk("lbrows")], dma="lbrows")
    pg.add("pe", lambda e: e.transpose(out=PS[0][:, 0:16], in_=lbrows[:], identity=ident[0:16, 0:16]),
           r=[tk("lbrows"), tk("ident")], w=[tk("ps", 0)])
    pg.add("dve", lambda e: e.tensor_copy(out=lgc[:], in_=PS[0][:, 0:16]), r=[tk("ps", 0)], w=[tk("lgc")])
    pg.add("dve", lambda e: e.tensor_tensor(out=tmp8[:], in0=lgc[:, 8:16], in1=lgc[:, 0:8], op=ALU.subtract),
           r=[tk("lgc")], w=[tk("tmp8")])
    pg.add("act", lambda e: e.activation(out=tmp8[:], in_=tmp8[:], func=AF.Exp), r=[tk("tmp8")], w=[tk("tmp8")])
    pg.add("dve", lambda e: e.tensor_scalar(out=tmp8[:], in0=tmp8[:], scalar1=1.0, scalar2=None, op0=ALU.add),
           r=[tk("tmp8")], w=[tk("tmp8")])
    pg.add("dve", lambda e: e.reciprocal(out=lbv[:], in_=tmp8[:]), r=[tk("tmp8")], w=[tk("lbv")])
    pg.add("dve", lambda e: e.tensor_scalar(out=omlv[:], in0=lbv[:], scalar1=-1.0, scalar2=1.0,
                                             op0=ALU.mult, op1=ALU.add),
           r=[tk("lbv")], w=[tk("omlv")])

    lamin = sbs("lamin", [P, 4, 64])
    lamt = sbs("lamt", [P, 2, 64])
    lams = sbs("lams", [P, 2])
    neglam = sb("neglam", [P, 1])
    for i, src in enumerate((lq1, lk1, lq2, lk2)):
        pg.add("sp", lambda e, i=i, src=src: e.dma_start(out=lamin[:, i, :], in_=src.to_broadcast([P, 64])),
               w=[tk("lamin", i)], dma=("lamin", i))
    pg.add("dve", lambda e: e.tensor_tensor(out=lamt[:], in0=lamin[:, 0::2, :], in1=lamin[:, 1::2, :], op=ALU.mult),
           r=[tk("lamin", i) for i in range(4)], w=[tk("lamt")])
    pg.add("dve", lambda e: e.tensor_reduce(out=lams[:], in_=lamt[:], axis=AX.X, op=ALU.add),
           r=[tk("lamt")], w=[tk("lams")])
    pg.add("act", lambda e: e.activation(out=lams[:], in_=lams[:], func=AF.Exp), r=[tk("lams")], w=[tk("lams")])
    pg.add("dve", lambda e: e.tensor_tensor(out=neglam[:], in0=lams[:, 1:2], in1=lams[:, 0:1], op=ALU.subtract),
           r=[tk("lams")], w=[tk("neglam")])
    pg.add("dve", lambda e: e.tensor_scalar(out=neglam[:], in0=neglam[:], scalar1=-LAM_INIT, scalar2=None, op0=ALU.add),
           r=[tk("neglam")], w=[tk("neglam")])

    hgg = sb("hgg", [P, 1])
    sgg = sb("sgg", [P, 1])
    pg.add("sp", lambda e: e.dma_start(out=hgg[:], in_=hg_norm_g), w=[tk("hgg")], dma="hgg")
    pg.add("sp", lambda e: e.dma_start(out=sgg[:], in_=subln_g), w=[tk("sgg")], dma="sgg")
    pg.add("dve", lambda e: e.tensor_scalar(out=sgg[:], in0=sgg[:], scalar1=1.0 - LAM_INIT, scalar2=None, op0=ALU.mult),
           r=[tk("sgg")], w=[tk("sgg")])

    cvrows = sbs("cvrows", [44, 8, P])
    cv = sb("cv", [P, 8, 44])
    for r_ in range(8):
        srow = conv_wb[r_, :] if r_ < 4 else state_conv[r_ - 4, :]
        pg.add("sp", lambda e, r_=r_, srow=srow: e.dma_start(out=cvrows[:, r_, :],
                                                   in_=srow.rearrange("(b p) -> b p", p=P)),
               w=[tk("cvrows", r_)], dma=("cvrows", r_))
        pg.add("pe", lambda e, r_=r_: e.transpose(out=PS[1][:, r_ * 44:(r_ + 1) * 44], in_=cvrows[:, r_, :],
                                                   identity=ident[0:44, 0:44]),
               r=[tk("cvrows", r_), tk("ident")], w=[tk("ps", 1)])
    pg.add("dve", lambda e: e.tensor_copy(out=cv[:].rearrange("p a b -> p (a b)"), in_=PS[1][:, 0:352]),
           r=[tk("ps", 1)], w=[tk("cv")])

    hT2 = sb("hT2", [P, 8, HTW], BF16)
    xT = hT2
    big1_off = arena["off"]
    yT = sb("yT", [P, 16, NTOK], BF16)
    arena["off"] = big1_off
    ACCT = sb("acc", [P, 17, D], F32)
    epsc = sb("epsc", [P, 1])
    stt = [sb("stt%d" % i, [P, 12]) for i in range(3)]
    mv = [sb("mv%d" % i, [P, 4]) for i in range(3)]
    pg.add("pool", lambda e: e.memset(epsc[:], LN_EPS), w=[tk("epsc")])
    mark = arena["off"]

    xb = [sb("xb%d" % i, [P, D], BF16) for i in range(2)]
    for i in range(17):
        src = x_p[i * P:(i + 1) * P, :] if i < 16 else x_s
        b = i % 2
        pg.add("pool", lambda e, b=b, src=src: e.dma_start(out=xb[b][:], in_=src), w=[tk("xb", b)], dma=("xb", b))
        for dc in range(8):
            pg.add("pe", lambda e, b=b, dc=dc: e.transpose(out=PT[:, dc * P:(dc + 1) * P],
                                                            in_=xb[b][:, dc * P:(dc + 1) * P], identity=identb[:]),
                   r=[tk("xb", b), tk("identb")], w=[tk("pt")])
        eng = "act" if i % 2 else "dve"
        if eng == "act":
            pg.add("act", lambda e, i=i: e.copy(out=xT[:, :, i * P:(i + 1) * P],
                                                 in_=PT[:].rearrange("p (a b) -> p a b", b=P)),
                   r=[tk("pt")], w=[tk("xT", i)])
        else:
            pg.add("dve", lambda e, i=i: e.tensor_copy(out=xT[:, :, i * P:(i + 1) * P],
                                                        in_=PT[:].rearrange("p (a b) -> p a b", b=P)),
                   r=[tk("pt")], w=[tk("xT", i)])

    chk(0)
    pg.barrier()
    arena["off"] = mark
    wh = [sb("wh%d" % i, [P, 8, 7 * P], BF16) for i in range(2)]
    WOFF = [0, 1, 4, 5, 2, 3, 6]
    hq_sb = [sb("hq%d" % i, [P, 512]) for i in range(2)]
    sigp = [sb("sigp%d" % i, [P, 512]) for i in range(2)]
    sign = [sb("sign%d" % i, [P, 512]) for i in range(2)]
    fbuf = sigp
    bbuf = [sb("bbuf%d" % i, [P, 512]) for i in range(2)]
    ebuf = [sb("ebuf%d" % i, [P, 512]) for i in range(2)]
    enbuf = bbuf
    qdz = [sb("qdz%d" % i, [P, 4, 3, 64], BF16) for i in range(2)]
    kdT = [sb("kdT%d" % i, [P, 512], BF16) for i in range(2)]
    qT = [sb("qT%d" % i, [P, 2, 512], BF16) for i in range(2)]
    vh = [sb("vh%d" % i, [P, 4, P], BF16) for i in range(2)]
    sog = [sb("sog%d" % i, [P, 4, P]) for i in range(2)]
    tm32 = [sb("tm32_%d" % i, [P, 512]) for i in range(2)]
    qTs = [sb("qTs%d" % i, [P, 512], BF16) for i in range(2)]
    kdtm = [sb("kdtm%d" % i, [P, 2, P], BF16) for i in range(2)]
    atm = [sb("atm%d" % i, [P, P], BF16) for i in range(2)]
    s32 = [sb("s32_%d" % i, [P, P]) for i in range(3)]
    sbf = [sb("sbf%d" % i, [P, P], BF16) for i in range(3)]
    t1b = [sb("t1b%d" % i, [P, P]) for i in range(2)]
    junk = sb("junk", [P, P])
    ssb = [sb("ssb%d" % i, [P, 2]) for i in range(4)]
    yh = [sb("yh%d" % i, [P, P], BF16) for i in range(2)]
    ktp = sb("ktp", [P, T], BF16)
    vp = sb("vp", [P, 16, 136], BF16)
    kts1 = sb("kts", [P, PAST], BF16)
    vs1_ = sb("vs", [P, 16, 136], BF16)
    ktn = [sb("ktn%d" % i, [P, SQ], BF16) for i in range(2)]
    vn = [sb("vn%d" % i, [SQ, 136], BF16) for i in range(2)]
    kc1 = sb("kc", [P, 16, P], BF16)
    kc = [kc1, kc1]
    pT = [sb("pT%d" % i, [P, 2, 256], BF16) for i in range(2)]
    rden = [sb("rden%d" % i, [P, 4]) for i in range(2)]
    t1a = [sb("t1a%d" % i, [P, P]) for i in range(2)]
    abuf = [sb("abuf%d" % i, [P, P]) for i in range(2)]
    ya = [sb("ya%d" % i, [P, P], BF16) for i in range(2)]

    for i in range(2):
        pg.add("pool", lambda e, i=i: e.memset(qT[i][:], 0.0), w=[tk("qT", i)])
        pg.add("pool", lambda e, i=i: e.memset(qdz[i][:, :, 1, :], 0.0), w=[tk("qdz", i)])
        pg.add("pool", lambda e, i=i: e.memset(vn[i][:, 128:130], 1.0), w=[tk("vn1", i)])
    pg.add("pool", lambda e: e.memset(vs1_[:, :, 128:130], 1.0), w=[tk("vs1", 0)])
    pg.add("pool", lambda e: e.memset(vp[:, :, 128:130], 1.0), w=[tk("vp1")])

    class Ring:
        def __init__(self, n):
            self.n = n
            self.i = -1

        def next(self):
            self.i = (self.i + 1) % self.n
            return self.i

    r_kvst = Ring(2)
    r_kdtm = Ring(2)
    r_atm = Ring(2)
    r_s = Ring(3)
    r_t1b = Ring(2)
    r_ss = Ring(4)
    r_yh = Ring(2)
    r_pT = Ring(2)
    r_sc = Ring(2)
    r_rden = Ring(2)
    r_t1a = Ring(2)
    r_ab = Ring(2)
    r_ya = Ring(2)
    r_pa = Ring(2)

    PA = [PS[0], PS[1]]
    PH = PS[2]
    SC = [PS[3], PS[4]]
    AC = [PS[5], PS[6]]
    PAI = [0, 1]
    SCI = [3, 4]
    ACI = [5, 6]

    def evac_eng(i):
        return "act" if i % 2 else "dve"

    groups = []
    for g in range(4):
        groups.append((g * 512, 512, [(g * 512 + k * P, P, "p", None) for k in range(4)]))
    groups.append((T, TS, [(T, SQ, "s", 0), (T + SQ, SQ, "s", 1)]))

    def xtoks(t0, n):
        return [tk("xT", i) for i in range(t0 // P, (t0 + n + P - 1) // P)]

    def run_interleaved(gens):
        gens = [g for g in gens if g is not None]
        while gens:
            for g in list(gens):
                try:
                    next(g)
                except StopIteration:
                    gens.remove(g)

    def rstd_ops(si, R):
        pg.add("act", lambda e: e.activation(out=ssb[si][0:R, 1:2], in_=ssb[si][0:R, 0:1], func=AF.Ln,
                                             scale=1.0 / P, bias=epsc[0:R, :]),
               r=[tk("ssb", si), tk("epsc")], w=[tk("ssb", si)])
        pg.add("act", lambda e: e.activation(out=ssb[si][0:R, 1:2], in_=ssb[si][0:R, 1:2], func=AF.Exp, scale=-0.5),
               r=[tk("ssb", si)], w=[tk("ssb", si)])

    def gen_attend(h, qbuf, qc0, nq, qrows, keytiles, out_cols):
        nqt = nq // qrows
        acc_v = [AC[i][:].rearrange("p (m c) -> p m c", m=2) for i in range(nqt)]
        last_for_qt = {}
        first_for_qt = {}
        for ki, kt in enumerate(keytiles):
            for iq in range(nqt):
                if iq * qrows >= kt[3]:
                    last_for_qt[iq] = ki
                    first_for_qt.setdefault(iq, ki)

        def scores(ki):
            kt_ap, v_ap, ns, q0, diag, rtoks = keytiles[ki]
            sci = r_sc.next()
            scv = SC[sci][:].rearrange("p (m c) -> p m c", m=2)
            for m in range(2):
                pg.add("pe", lambda e, m=m: e.matmul(
                    scv[0:ns, m, q0:nq], lhsT=kt_ap, rhs=qT[qbuf][:, m, qc0 + q0:qc0 + nq],
                    start=(m == 0), stop=(m == 1)),
                    r=rtoks + [tk("qT", qbuf)], w=[tk("ps", SCI[sci])])
            return sci, scv

        nxt = scores(0)
        for ki, (kt_ap, v_ap, ns, q0, diag, rtoks) in enumerate(keytiles):
            sci, scv = nxt
            if ki + 1 < len(keytiles):
                nxt = scores(ki + 1)
            pi = r_pT.next()
            pg.add("act", lambda e: e.activation(out=pT[pi][0:ns, :, q0:nq], in_=scv[0:ns, :, q0:nq], func=AF.Exp),
                   r=[tk("ps", SCI[sci])], w=[tk("pT", pi)])
            if diag is not None:
                pg.add("pool", lambda e: e.memset(pT[pi][64:128, :, diag * P:diag * P + 64], 0.0), w=[tk("pT", pi)])
            yield
            for iq in range(nqt):
                if iq * qrows < q0:
                    continue
                for m in range(2):
                    pg.add("pe", lambda e, iq=iq, m=m: e.matmul(
                        acc_v[iq][0:qrows, m, 0:129], lhsT=pT[pi][0:ns, m, iq * qrows:(iq + 1) * qrows],
                        rhs=v_ap[:, 0:129], start=(ki == first_for_qt[iq] and m == 0),
                        stop=(ki == last_for_qt[iq] and m == 1)),
                        r=rtoks + [tk("pT", pi)], w=[tk("ps", ACI[iq])])
            yield
        for iq in range(nqt):
            yield from attend_finalize(h, acc_v[iq], tk("ps", ACI[iq]), qrows, out_cols[iq])

    def attend_finalize(h, av, atok, R, c0):
        if True:
            ri = r_rden.next()
            pg.add("dve", lambda e: e.reciprocal(out=rden[ri][0:R, 0:2], in_=av[0:R, :, 128]),
                   r=[atok], w=[tk("rden", ri)])
            pg.add("dve", lambda e: e.tensor_tensor(out=rden[ri][0:R, 2:3], in0=rden[ri][0:R, 1:2],
                                                    in1=neglam[0:R, :], op=ALU.mult),
                   r=[tk("rden", ri), tk("neglam")], w=[tk("rden", ri)])
            ti = r_t1a.next()
            pg.add("act", lambda e: e.activation(out=t1a[ti][0:R, :], in_=av[0:R, 1, 0:128], func=AF.Copy,
                                                 scale=rden[ri][0:R, 2:3]),
                   r=[atok, tk("rden", ri)], w=[tk("t1a", ti)])
            ai = r_ab.next()
            pg.add("dve", lambda e: e.scalar_tensor_tensor(
                out=abuf[ai][0:R, :], in0=av[0:R, 0, 0:128], scalar=rden[ri][0:R, 0:1], in1=t1a[ti][0:R, :],
                op0=ALU.mult, op1=ALU.add),
                r=[atok, tk("rden", ri), tk("t1a", ti)], w=[tk("abuf", ai)])
            yield
            si = r_ss.next()
            pg.add("act", lambda e: e.activation(out=junk[0:R, :], in_=abuf[ai][0:R, :], func=AF.Square,
                                                 accum_out=ssb[si][0:R, 0:1]),
                   r=[tk("abuf", ai)], w=[tk("junk"), tk("ssb", si)])
            rstd_ops(si, R)
            yi = r_ya.next()
            pg.add("dve", lambda e: e.tensor_scalar(out=ya[yi][0:R, :], in0=abuf[ai][0:R, :],
                                                    scalar1=ssb[si][0:R, 1:2], scalar2=None, op0=ALU.mult),
                   r=[tk("abuf", ai), tk("ssb", si)], w=[tk("ya", yi)])
            yield
            pg.add("pe", lambda e: e.transpose(out=PT[:, 0:R], in_=ya[yi][0:R, :], identity=identb[0:R, 0:R]),
                   r=[tk("ya", yi), tk("identb")], w=[tk("pt")])
            pg.add("act", lambda e: e.activation(out=yT[:, 8 + h, c0:c0 + R], in_=PT[:, 0:R], func=AF.Copy,
                                                 scale=sgg[:, 0:1]),
                   r=[tk("pt"), tk("sgg")], w=[tk("yT", 8 + h, c0 // P)])
            yield

    def gen_attend_sample(h, qbuf, qc0, keytiles, out_col):
        R = SQ
        av = AC[0][:].rearrange("p (m c) -> p m c", m=2)
        atok = tk("ps", ACI[0])
        steps = [keytiles[i:i + 2] for i in range(0, 16, 2)] + [keytiles[16:17]]
        nsteps = len(steps)

        def scores(si_):
            tiles_ = steps[si_]
            sci = r_sc.next()
            scv = SC[sci][:, 0:256].rearrange("p (a c) -> p a c", c=64)
            for t_, (kt_ap, v_ap, ns, q0, diag, rtoks) in enumerate(tiles_):
                for m in range(2):
                    pg.add("pe", lambda e, t_=t_, m=m, kt_ap=kt_ap, ns=ns: e.matmul(
                        scv[0:ns, t_ * 2 + m, :], lhsT=kt_ap, rhs=qT[qbuf][:, m, qc0:qc0 + R],
                        start=(t_ == 0 and m == 0), stop=(t_ == len(tiles_) - 1 and m == 1)),
                        r=rtoks + [tk("qT", qbuf)], w=[tk("ps", SCI[sci])])
            return sci, scv

        nxt = scores(0)
        for si_, tiles_ in enumerate(steps):
            sci, scv = nxt
            if si_ + 1 < nsteps:
                nxt = scores(si_ + 1)
            ns = tiles_[0][2]
            nb_ = 2 * len(tiles_)
            pi = r_pT.next()
            ptv = pT[pi][:].rearrange("p m c -> p (m c)")[:, 0:256].rearrange("p (a c) -> p a c", c=64)
            pg.add("act", lambda e: e.activation(out=ptv[0:ns, 0:nb_, :], in_=scv[0:ns, 0:nb_, :], func=AF.Exp),
                   r=[tk("ps", SCI[sci])], w=[tk("pT", pi)])
            yield
            for t_, (kt_ap, v_ap, ns_, q0, diag, rtoks) in enumerate(tiles_):
                for m in range(2):
                    first = (si_ == 0 and t_ == 0 and m == 0)
                    last = (si_ == nsteps - 1 and t_ == len(tiles_) - 1 and m == 1)
                    pg.add("pe", lambda e, t_=t_, m=m, v_ap=v_ap, ns_=ns_, first=first, last=last: e.matmul(
                        av[0:R, m, 0:129], lhsT=ptv[0:ns_, t_ * 2 + m, :], rhs=v_ap[:, 0:129], start=first, stop=last),
                        r=rtoks + [tk("pT", pi)], w=[atok])
            yield
        yield from attend_finalize(h, av, atok, R, out_col)

    def load_cache(h, sq):
        pg.add("pool", lambda e: e.dma_start(out=kc1[:], in_=cache_k[sq, :, h, :].rearrange("(j p) e -> p j e", p=P)),
               w=[tk("kc", 0)], dma=("kc", 0))
        pg.add("pool", lambda e: e.dma_start(out=vs1_[:, 0:16, 0:128],
                                             in_=cache_v[sq, :, h, :].rearrange("(j p) e -> p j e", p=P)),
               w=[tk("vs", 0)], dma=("vs", 0))

    def gen_cache_T():
        for half in range(2):
            for jj in range(8):
                j = half * 8 + jj
                pg.add("pe", lambda e, j=j, jj=jj: e.transpose(out=PT[:, jj * P:(jj + 1) * P], in_=kc1[:, j, :],
                                                               identity=identb[:]),
                       r=[tk("kc", 0), tk("identb")], w=[tk("pt")])
            pg.add("dve", lambda e, half=half: e.tensor_copy(out=kts1[:, half * 1024:(half + 1) * 1024], in_=PT[:]),
                   r=[tk("pt")], w=[tk("kts", 0)])
            yield

    def load_weights(h):
        wb = h % 2
        for j in range(7):
            off = WOFF[j] * D + h * P
            pg.add("pool", lambda e, j=j, off=off: e.dma_start(
                out=wh[wb][:, :, j * P:(j + 1) * P], in_=w_in[:, off:off + P].rearrange("(c p) n -> p c n", p=P)),
                w=[tk("wh", wb, j)], dma=("wh", wb, j))

    def gen_inproj(h, gi):
        wb = h % 2
        W = wh[wb]
        gb = (5 * h + gi) % 2
        t0, NT, tiles = groups[gi]
        xt = xtoks(t0, NT)
        for j in range(4):
            pi = r_pa.next()
            ptok = tk("ps", PAI[pi])
            for dc in range(8):
                pg.add("pe", lambda e, dc=dc: e.matmul(
                    PA[pi][:, 0:NT], lhsT=W[:, dc, j * P:(j + 1) * P], rhs=xT[:, dc, t0:t0 + NT],
                    start=(dc == 0), stop=(dc == 7)),
                    r=[tk("wh", wb, j)] + xt, w=[ptok])
                if dc == 99:
                    yield
            src = PA[pi][:, 0:NT]
            if j == 0:
                pg.add("dve", lambda e: e.tensor_copy(out=hq_sb[gb][:, 0:NT], in_=src), r=[ptok], w=[tk("hq", gb)])
            elif j == 1:
                pg.add("act", lambda e: e.activation(out=sign[gb][:, 0:NT], in_=src, func=AF.Exp, scale=-1.0),
                       r=[ptok], w=[tk("sign", gb)])
            elif j == 2:
                pg.add("act", lambda e: e.activation(out=qTs[gb][:, 0:NT], in_=src, func=AF.Copy, scale=DA_SCALE),
                       r=[ptok], w=[tk("qTs", gb)])
                for m in range(2):
                    pg.add("pool", lambda e, m=m: e.tensor_copy(
                        out=qT[gb][m * 64:(m + 1) * 64, m, 0:NT], in_=qTs[gb][m * 64:(m + 1) * 64, 0:NT]),
                        r=[tk("qTs", gb)], w=[tk("qT", gb)])
            else:
                if gi < 4:
                    pg.add("dve", lambda e: e.tensor_copy(out=ktp[:, t0:t0 + NT], in_=src), r=[ptok], w=[tk("ktp", gi)])
                else:
                    for sq in range(2):
                        pg.add("dve", lambda e, sq=sq: e.tensor_copy(out=ktn[sq][:, :], in_=PA[pi][:, sq * SQ:(sq + 1) * SQ]),
                               r=[ptok], w=[tk("ktn", sq)])
            yield
        sp_, sn_ = sigp[gb][:, 0:NT], sign[gb][:, 0:NT]
        pg.add("dve", lambda e: e.tensor_scalar(out=sp_, in0=sn_, scalar1=1.0, scalar2=None, op0=ALU.add),
               r=[tk("sign", gb)], w=[tk("sigp", gb)])
        pg.add("dve", lambda e: e.reciprocal(out=sp_, in_=sp_), r=[tk("sigp", gb)], w=[tk("sigp", gb)])
        pg.add("dve", lambda e: e.tensor_tensor(out=sn_, in0=sn_, in1=sp_, op=ALU.mult),
               r=[tk("sign", gb), tk("sigp", gb)], w=[tk("sign", gb)])
        pg.add("dve", lambda e: e.tensor_scalar(out=sp_, in0=sp_, scalar1=omlv[:, h:h + 1], scalar2=lbv[:, h:h + 1],
                                                op0=ALU.mult, op1=ALU.add),
               r=[tk("sigp", gb), tk("omlv"), tk("lbv")], w=[tk("sigp", gb)])
        yield
        pg.add("act", lambda e: e.activation(out=sp_, in_=sp_, func=AF.Ln), r=[tk("sigp", gb)], w=[tk("sigp", gb)])
        pg.add("dve", lambda e: e.tensor_tensor_scan(out=bbuf[gb][:, 0:NT], data0=rmask[:, 0:NT], data1=sp_, initial=0.0,
                                                     op0=ALU.mult, op1=ALU.add),
               r=[tk("sigp", gb), tk("rmask")], w=[tk("bbuf", gb)])
        pg.add("act", lambda e: e.activation(out=ebuf[gb][:, 0:NT], in_=bbuf[gb][:, 0:NT], func=AF.Exp),
               r=[tk("bbuf", gb)], w=[tk("ebuf", gb)])
        pg.add("act", lambda e: e.activation(out=bbuf[gb][:, 0:NT], in_=bbuf[gb][:, 0:NT], func=AF.Exp, scale=-1.0),
               r=[tk("bbuf", gb)], w=[tk("bbuf", gb)])
        yield
        nch_g = NT // 64
        pg.add("dve", lambda e: e.tensor_tensor(
            out=qdz[gb][:, 0:nch_g // 2, 0::2, :],
            in0=hq_sb[gb][:, 0:NT].rearrange("p (a b c) -> p a b c", b=2, c=64),
            in1=ebuf[gb][:, 0:NT].rearrange("p (a b c) -> p a b c", b=2, c=64), op=ALU.mult),
            r=[tk("hq", gb), tk("ebuf", gb)], w=[tk("qdz", gb)])
        pg.add("dve", lambda e: e.scalar_tensor_tensor(
            out=kdT[gb][:, 0:NT], in0=sn_, scalar=omlv[:, h:h + 1], in1=bbuf[gb][:, 0:NT],
            op0=ALU.mult, op1=ALU.mult),
            r=[tk("sign", gb), tk("omlv"), tk("bbuf", gb)], w=[tk("kdT", gb)])
        yield
        for li, (tt0, R, kind, sq) in enumerate(tiles):
            pi = r_pa.next()
            ptok = tk("ps", PAI[pi])
            for dc in range(8):
                pg.add("pe", lambda e, dc=dc: e.matmul(
                    PA[pi][0:R, :], lhsT=xT[:, dc, tt0:tt0 + R], rhs=W[:, dc, 3 * P:7 * P],
                    start=(dc == 0), stop=(dc == 7)),
                    r=[tk("wh", wb, j) for j in (3, 4, 5, 6)] + xtoks(tt0, R), w=[ptok])
                if dc == 99:
                    yield
            ki = r_kvst.next()
            ttok = tk("tm32", ki)
            tmv = tm32[ki][:].rearrange("p (a b) -> p a b", b=P)
            pg.add("act", lambda e: e.copy(out=tm32[ki][0:R, :], in_=PA[pi][0:R, :]), r=[ptok], w=[ttok])
            if kind == "p":
                kd, vd = k_p[tt0:tt0 + R, h, :], v_p[tt0:tt0 + R, h, :]
            else:
                kd, vd = k_s[sq * SQ:(sq + 1) * SQ, h, :], v_s[sq * SQ:(sq + 1) * SQ, h, :]
            pg.add("sp", lambda e: e.dma_start(out=kd, in_=tmv[0:R, 0, :]), r=[ttok], dma=("tm32", ki))
            pg.add("sp", lambda e: e.dma_start(out=vd, in_=tmv[0:R, 3, :]), r=[ttok], dma=("tm32", ki))
            pg.add("pool", lambda e: e.tensor_copy(out=vh[gb][0:R, li, :], in_=tmv[0:R, 1, :]), r=[ttok], w=[tk("vh", gb, li)])
            pg.add("act", lambda e: e.activation(out=sog[gb][0:R, li, :], in_=tmv[0:R, 2, :], func=AF.Exp, scale=-1.0),
                   r=[ttok], w=[tk("sog", gb, li)])
            if kind == "p":
                ti = tt0 // P
                pg.add("pool", lambda e: e.tensor_copy(out=vp[:, ti, 0:128], in_=tmv[:, 3, :]), r=[ttok], w=[tk("vp", ti)])
            else:
                pg.add("pool", lambda e: e.tensor_copy(out=vn[sq][0:SQ, 0:128], in_=tmv[0:SQ, 3, :]), r=[ttok], w=[tk("vsn", sq)])
            so_ = sog[gb][0:R, li, :]
            pg.add("dve", lambda e: e.tensor_scalar(out=so_, in0=so_, scalar1=1.0, scalar2=None, op0=ALU.add),
                   r=[tk("sog", gb, li)], w=[tk("sog", gb, li)])
            pg.add("dve", lambda e: e.reciprocal(out=so_, in_=so_), r=[tk("sog", gb, li)], w=[tk("sog", gb, li)])
            yield

    hstate = {"s_cur": None}

    def gen_hgrn(h, gi):
        gb = (5 * h + gi) % 2
        t0, NT, tiles = groups[gi]
        phv = PH[:].rearrange("p (a b) -> p a b", b=P)
        ptok = tk("ps", 2)
        for li, (tt0, R, kind, sq) in enumerate(tiles):
            nch = R // 64
            c0 = tt0 - t0
            lt = c0 // P
            if kind == "p" and tt0 == 0:
                s_cur = r_s.next()
                pg.add("pool", lambda e: e.memset(s32[s_cur][:], 0.0), w=[tk("s32", s_cur)])
                pg.add("pool", lambda e: e.memset(sbf[s_cur][:], 0.0), w=[tk("sbf", s_cur)])
                hstate["s_cur"] = s_cur
            if kind == "s":
                s_cur = r_s.next()
                pg.add("sp", lambda e: e.dma_start(out=s32[s_cur][:], in_=state_hgrn[sq, h, :, :]),
                       w=[tk("s32", s_cur)], dma=("s32", s_cur))
                pg.add("pool", lambda e: e.tensor_copy(out=sbf[s_cur][:], in_=s32[s_cur][:]),
                       r=[tk("s32", s_cur)], w=[tk("sbf", s_cur)])
                hstate["s_cur"] = s_cur
            s_cur = hstate["s_cur"]
            pg.add("pe", lambda e: e.transpose(out=PT[0:R, 0:P], in_=kdT[gb][:, c0:c0 + R], identity=identb[:]),
                   r=[tk("kdT", gb), tk("identb")], w=[tk("pt")])
            kmi = r_kdtm.next()
            for c in range(nch):
                pg.add("dve", lambda e, c=c: e.tensor_copy(out=kdtm[kmi][c * 64:(c + 1) * 64, c, :],
                                                           in_=PT[c * 64:(c + 1) * 64, 0:P]),
                       r=[tk("pt")], w=[tk("kdtm", kmi)])
            if nch == 2:
                qd_mov = qdz[gb][:, lt, 0::2, :]
                att_out = phv[0:R, 0, 0:R].rearrange("p (a b) -> p a b", b=64)
            else:
                qd_mov = qdz[gb][:, 0, 2 * sq, :]
                att_out = phv[0:R, 0, 0:R]
            pg.add("pe", lambda e: e.matmul(att_out, lhsT=kdT[gb][:, c0:c0 + R], rhs=qd_mov, start=True, stop=True),
                   r=[tk("kdT", gb), tk("qdz", gb)], w=[ptok])
            ai = r_atm.next()
            pg.add("dve", lambda e: e.tensor_tensor(out=atm[ai][0:R, 0:R], in0=phv[0:R, 0, 0:R], in1=hmask[0:R, 0:R],
                                                    op=ALU.mult),
                   r=[ptok, tk("hmask")], w=[tk("atm", ai)])
            yield
            for c in range(nch):
                if nch == 2:
                    lhs = kdtm[kmi][:, c, :]
                    rhs = vh[gb][:, li, :]
                else:
                    lhs = kdtm[kmi][0:64, 0, :]
                    rhs = vh[gb][0:64, li, :]
                pg.add("pe", lambda e, c=c, lhs=lhs, rhs=rhs: e.matmul(phv[:, 1 + c, :], lhsT=lhs, rhs=rhs,
                                                                     start=True, stop=True),
                       r=[tk("kdtm", kmi), tk("vh", gb, li)], w=[ptok])
            s_before = [s_cur]
            for c in range(nch):
                ti = r_t1b.next()
                sc_ = s_cur
                pg.add("dve", lambda e, c=c, ti=ti, sc_=sc_: e.tensor_tensor(out=t1b[ti][:], in0=phv[:, 1 + c, :],
                                                                           in1=s32[sc_][:], op=ALU.add),
                       r=[ptok, tk("s32", sc_)], w=[tk("t1b", ti)])
                s_new = r_s.next()
                acol = c0 + c * 64 + 63
                pg.add("dve", lambda e, ti=ti, s_new=s_new, acol=acol: e.tensor_scalar(
                    out=s32[s_new][:], in0=t1b[ti][:], scalar1=ebuf[gb][:, acol:acol + 1], scalar2=None, op0=ALU.mult),
                    r=[tk("t1b", ti), tk("ebuf", gb)], w=[tk("s32", s_new)])
                pg.add("pool", lambda e, ti=ti, s_new=s_new, acol=acol: e.tensor_scalar(
                    out=sbf[s_new][:], in0=t1b[ti][:], scalar1=ebuf[gb][:, acol:acol + 1], scalar2=1.0,
                    op0=ALU.mult, op1=ALU.mult),
                    r=[tk("t1b", ti), tk("ebuf", gb)], w=[tk("sbf", s_new)])
                s_cur = s_new
                s_before.append(s_cur)
            hstate["s_cur"] = s_cur
            if kind == "p" and tt0 == T - P:
                pg.add("sp", lambda e: e.dma_start(out=hg_p[h, :, :], in_=s32[s_cur][:]),
                       r=[tk("s32", s_cur)], dma=("s32", s_cur))
            if kind == "s":
                pg.add("sp", lambda e: e.dma_start(out=hg_s[sq, h, :, :], in_=s32[s_cur][:]),
                       r=[tk("s32", s_cur)], dma=("s32", s_cur))
            yield
            pg.add("pe", lambda e: e.matmul(phv[0:R, 3, :], lhsT=atm[ai][0:R, 0:R], rhs=vh[gb][0:R, li, :],
                                            start=True, stop=False),
                   r=[tk("atm", ai), tk("vh", gb, li)], w=[ptok])
            for c in range(nch):
                if nch == 2:
                    lhs = qdz[gb][:, lt, :, :].rearrange("p a b -> p (a b)")[:, c * 64:c * 64 + P]
                else:
                    lhs = qdz[gb][:, 0, 2 * sq, :]
                sb_i = s_before[c]
                pg.add("pe", lambda e, lhs=lhs, sb_i=sb_i, c=c: e.matmul(
                    phv[0:R, 3, :], lhsT=lhs, rhs=sbf[sb_i][:], start=False, stop=(c == nch - 1)),
                    r=[tk("qdz", gb), tk("sbf", sb_i)], w=[ptok])
            si = r_ss.next()
            pg.add("act", lambda e: e.activation(out=junk[0:R, :], in_=phv[0:R, 3, :], func=AF.Square,
                                                 accum_out=ssb[si][0:R, 0:1]),
                   r=[ptok], w=[tk("junk"), tk("ssb", si)])
            rstd_ops(si, R)
            yield
            yi = r_yh.next()
            pg.add("dve", lambda e: e.scalar_tensor_tensor(
                out=yh[yi][0:R, :], in0=phv[0:R, 3, :], scalar=ssb[si][0:R, 1:2], in1=sog[gb][0:R, li, :],
                op0=ALU.mult, op1=ALU.mult),
                r=[ptok, tk("ssb", si), tk("sog", gb, li)], w=[tk("yh", yi)])
            yield
            pg.add("pe", lambda e: e.transpose(out=PT[:, 0:R], in_=yh[yi][0:R, :], identity=identb[0:R, 0:R]),
                   r=[tk("yh", yi), tk("identb")], w=[tk("pt")])
            pg.add("act", lambda e: e.activation(out=yT[:, h, tt0:tt0 + R], in_=PT[:, 0:R], func=AF.Copy,
                                                 scale=hgg[:, 0:1]),
                   r=[tk("pt"), tk("hgg")], w=[tk("yT", h, tt0 // P)])
            yield

    def gen_attn(h, gi):
        gb = (5 * h + gi) % 2
        t0, NT, tiles = groups[gi]
        if gi < 4:
            for qg in range(2):
                qt0 = t0 + qg * 256
                nkt = qt0 // P + 2
                keytiles = []
                for j in range(nkt):
                    if j < qt0 // P:
                        q0, diag = 0, None
                    else:
                        q0, diag = (j - qt0 // P) * P, j - qt0 // P
                    keytiles.append((ktp[:, j * P:(j + 1) * P], vp[:, j, :], P, q0, diag,
                                     [tk("ktp", j // 4), tk("vp", j), tk("vp1")]))
                yield from gen_attend(h, gb, qg * 256, 256, P, keytiles, [qt0, qt0 + P])
        else:
            for sq in range(2):
                if sq == 1:
                    load_cache(h, 1)
                yield from gen_cache_T()
                keytiles = []
                for j in range(16):
                    keytiles.append((kts1[:, j * P:(j + 1) * P], vs1_[:, j, :], P, 0, None,
                                     [tk("kts", 0), tk("vs", 0), tk("vs1", 0)]))
                keytiles.append((ktn[sq][:, :], vn[sq][0:SQ, :], SQ, 0, None,
                                 [tk("ktn", sq), tk("vsn", sq), tk("vn1", sq)]))
                yield from gen_attend_sample(h, gb, sq * SQ, keytiles, T + sq * SQ)

    for i in range(2):
        pg.add("pool", lambda e, i=i: e.memset(kdtm[i][:], 0.0), w=[tk("kdtm", i)])
    load_weights(0)
    tail = []
    for h in range(H):
        if h + 1 < H:
            load_weights(h + 1)
        run_interleaved(tail + [gen_inproj(h, 0)])
        load_cache(h, 0)
        for gi in range(4):
            run_interleaved([gen_attn(h, gi), gen_hgrn(h, gi), gen_inproj(h, gi + 1)])
        tail = [gen_attn(h, 4), gen_hgrn(h, 4)]
    run_interleaved(tail)
    pg.barrier()
    chk(200)

    arena["off"] = mark
    mT = sb("mT", [P, 8, NTOK], BF16)
    mark2 = arena["off"]
    w2 = [sb("w2_%d" % i, [P, 8, 4, P], BF16) for i in range(2)]
    s1b = [sb("s1b%d" % i, [P, 512]) for i in range(2)]
    s2b = [sb("s2b%d" % i, [P, 512]) for i in range(2)]
    m1b = [sb("m1b%d" % i, [P, 512]) for i in range(2)]
    m2b = [sb("m2b%d" % i, [P, 512]) for i in range(2)]
    bankset = [[0, 1, 3, 4], [5, 6, 2, 0]]
    it = 0
    for nb in range(8):
        b2 = nb % 2
        srcs = [w_in[:, 7 * D + nb * P:7 * D + (nb + 1) * P], w_in[:, 8 * D + nb * P:8 * D + (nb + 1) * P],
                w_br_hg[:, nb * P:(nb + 1) * P], w_br_da[:, nb * P:(nb + 1) * P]]
        for j in range(4):
            pg.add("pool", lambda e, b2=b2, j=j, src=srcs[j]: e.dma_start(
                out=w2[b2][:, :, j, :], in_=src.rearrange("(c p) n -> p c n", p=P)),
                w=[tk("w2", b2, j)], dma=("w2", b2, j))
        for gi, (t0, NT, tiles) in enumerate(groups):
            banks = [0, 1, 3, 4] if it % 2 == 0 else [5, 6, 2, 1]
            if it % 2 == 1:
                banks = [5, 6, 2, 0]
            it += 1
            gb = gi % 2
            xt = xtoks(t0, NT)
            ytk_hg = [tk("yT", hh, i) for hh in range(8) for i in range(t0 // P, (t0 + NT + P - 1) // P)]
            ytk_da = [tk("yT", 8 + hh, i) for hh in range(8) for i in range(t0 // P, (t0 + NT + P - 1) // P)]
            for j in range(4):
                bk = banks[j]
                for c in range(8):
                    if j < 2:
                        rhs = xT[:, c, t0:t0 + NT]
                        rt = xt
                    elif j == 2:
                        rhs = yT[:, c, t0:t0 + NT]
                        rt = ytk_hg
                    else:
                        rhs = yT[:, 8 + c, t0:t0 + NT]
                        rt = ytk_da
                    pg.add("pe", lambda e, bk=bk, b2=b2, j=j, c=c, rhs=rhs, NT=NT: e.matmul(
                        PS[bk][:, 0:NT], lhsT=w2[b2][:, c, j, :], rhs=rhs, start=(c == 0), stop=(c == 7)),
                        r=[tk("w2", b2, j)] + rt, w=[tk("ps", bk)])
            pg.add("act", lambda e, gb=gb, bk=banks[0], NT=NT: e.activation(out=s1b[gb][:, 0:NT], in_=PS[bk][:, 0:NT], func=AF.Sigmoid),
                   r=[tk("ps", banks[0])], w=[tk("s1b", gb)])
            pg.add("act", lambda e, gb=gb, bk=banks[1], NT=NT: e.activation(out=s2b[gb][:, 0:NT], in_=PS[bk][:, 0:NT], func=AF.Sigmoid),
                   r=[tk("ps", banks[1])], w=[tk("s2b", gb)])
            pg.add("dve", lambda e, gb=gb, bk=banks[2], NT=NT: e.tensor_tensor(out=m1b[gb][:, 0:NT], in0=PS[bk][:, 0:NT],
                                                                               in1=s1b[gb][:, 0:NT], op=ALU.mult),
                   r=[tk("ps", banks[2]), tk("s1b", gb)], w=[tk("m1b", gb)])
            pg.add("dve", lambda e, gb=gb, bk=banks[3], NT=NT: e.tensor_tensor(out=m2b[gb][:, 0:NT], in0=PS[bk][:, 0:NT],
                                                                               in1=s2b[gb][:, 0:NT], op=ALU.mult),
                   r=[tk("ps", banks[3]), tk("s2b", gb)], w=[tk("m2b", gb)])
            pg.add("pool", lambda e, gb=gb, nb=nb, t0=t0, NT=NT: e.tensor_tensor(
                out=mT[:, nb, t0:t0 + NT], in0=m1b[gb][:, 0:NT], in1=m2b[gb][:, 0:NT], op=ALU.add),
                r=[tk("m1b", gb), tk("m2b", gb)],
                w=[tk("mT", i) for i in range(t0 // P, (t0 + NT + P - 1) // P)])

    chk(201)
    pg.barrier()
    arena["off"] = mark2
    wout = sb("wout", [P, 8, D], BF16)
    g1b = sb("g1b", [P, D])
    b1b = sb("b1b", [P, D])
    xf = [sb("xf%d" % i, [P, D]) for i in range(3)]
    rb = [sb("rb%d" % i, [P, D]) for i in range(3)]
    hb = [sb("hb%d" % i, [P, D], BF16) for i in range(3)]

    pg.add("pool", lambda e: e.dma_start(out=wout[:], in_=w_out.rearrange("(c p) n -> p c n", p=P)),
           w=[tk("wout")], dma="wout")
    pg.add("sp", lambda e: e.dma_start(out=g1b[:], in_=ln1_g.to_broadcast([P, D])), w=[tk("g1b")], dma="g1b")
    pg.add("sp", lambda e: e.dma_start(out=b1b[:], in_=ln1_b.to_broadcast([P, D])), w=[tk("b1b")], dma="b1b")
    pg.add("dve", lambda e: e.tensor_scalar(out=g1b[:], in0=g1b[:], scalar1=ALPHA, scalar2=None, op0=ALU.mult),
           r=[tk("g1b")], w=[tk("g1b")])
    pg.add("dve", lambda e: e.tensor_scalar(out=b1b[:], in0=b1b[:], scalar1=ALPHA, scalar2=None, op0=ALU.mult),
           r=[tk("b1b")], w=[tk("b1b")])
    pg.add("pool", lambda e: e.memset(hT2[:, :, 0:2], 0.0), w=[tk("hTz")])
    pg.add("pool", lambda e: e.memset(hT2[:, :, 2050:2052], 0.0), w=[tk("hTz")])
    pg.add("pool", lambda e: e.memset(hT2[:, :, 2116:2118], 0.0), w=[tk("hTz")])

    def layer_norm_tile(src_ap, bi, gtile, btile, gtok, btok, out_ap, out_tok, src_tok):
        for hf_ in range(2):
            pg.add("dve", lambda e, hf_=hf_: e.bn_stats(out=stt[bi][:, hf_ * 6:(hf_ + 1) * 6], in_=src_ap[:, hf_ * 512:(hf_ + 1) * 512]),
                   r=[src_tok], w=[tk("stt", bi)])
        pg.add("dve", lambda e: e.bn_aggr(out=mv[bi][:, 0:2], in_=stt[bi][:]), r=[tk("stt", bi)], w=[tk("mv", bi)])
        pg.add("act", lambda e: e.activation(out=mv[bi][:, 2:3], in_=mv[bi][:, 1:2], func=AF.Ln, bias=epsc[:, :]),
               r=[tk("mv", bi), tk("epsc")], w=[tk("mv", bi)])
        pg.add("act", lambda e: e.activation(out=mv[bi][:, 2:3], in_=mv[bi][:, 2:3], func=AF.Exp, scale=-0.5),
               r=[tk("mv", bi)], w=[tk("mv", bi)])
        pg.add("dve", lambda e: e.tensor_scalar(out=src_ap, in0=src_ap, scalar1=mv[bi][:, 0:1], scalar2=mv[bi][:, 2:3],
                                                op0=ALU.subtract, op1=ALU.mult),
               r=[src_tok, tk("mv", bi)], w=[src_tok])
        pg.add("dve", lambda e: e.tensor_tensor(out=src_ap, in0=src_ap, in1=gtile[:], op=ALU.mult),
               r=[src_tok, gtok], w=[src_tok])
        pg.add("pool", lambda e: e.tensor_tensor(out=out_ap, in0=src_ap, in1=btile[:], op=ALU.add),
               r=[src_tok, btok], w=[out_tok])

    for i in range(17):
        bi = i % 3
        src = x_p[i * P:(i + 1) * P, :] if i < 16 else x_s
        pg.add("sp", lambda e, bi=bi, src=src: e.dma_start(out=xf[bi][:], in_=src), w=[tk("xf", bi)], dma=("xf", bi))
        for hf_ in range(2):
            bk = ([3, 4] if i % 2 == 0 else [5, 6])[hf_]
            for c in range(8):
                pg.add("pe", lambda e, bk=bk, c=c, i=i, hf_=hf_: e.matmul(
                    PS[bk][:, :], lhsT=mT[:, c, i * P:(i + 1) * P], rhs=wout[:, c, hf_ * 512:(hf_ + 1) * 512],
                    start=(c == 0), stop=(c == 7)),
                    r=[tk("mT", i), tk("wout")], w=[tk("ps", bk)])
            pg.add("dve", lambda e, bk=bk, bi=bi, hf_=hf_: e.scalar_tensor_tensor(
                out=rb[bi][:, hf_ * 512:(hf_ + 1) * 512], in0=xf[bi][:, hf_ * 512:(hf_ + 1) * 512], scalar=ALPHA,
                in1=PS[bk][:, :], op0=ALU.mult, op1=ALU.add),
                r=[tk("xf", bi), tk("ps", bk)], w=[tk("rb", bi)])
        layer_norm_tile(rb[bi][:], bi, g1b, b1b, tk("g1b"), tk("b1b"), ACCT[:, i, :], tk("acc", i), tk("rb", bi))
        pg.add("act", lambda e, bi=bi, i=i: e.activation(out=hb[bi][:], in_=ACCT[:, i, :], func=AF.Copy, scale=1.0 / ALPHA),
               r=[tk("acc", i)], w=[tk("hb", bi)])
        for c in range(8):
            pg.add("pe", lambda e, bi=bi, c=c: e.transpose(out=PT[:, c * P:(c + 1) * P], in_=hb[bi][:, c * P:(c + 1) * P],
                                                           identity=identb[:]),
                   r=[tk("hb", bi), tk("identb")], w=[tk("pt")])
        if i < 16:
            pg.add("dve", lambda e, i=i: e.tensor_copy(out=hT2[:, :, 2 + i * P:2 + (i + 1) * P],
                                                       in_=PT[:].rearrange("p (a b) -> p a b", b=P)),
                   r=[tk("pt")], w=[tk("hT", i)])
        else:
            pg.add("dve", lambda e: e.tensor_copy(
                out=hT2[:, :, 2052:2184].rearrange("p a (s c) -> p a s c", c=66)[:, :, :, 0:64],
                in_=PT[:].rearrange("p (a s c) -> p a s c", s=2, c=64)),
                r=[tk("pt")], w=[tk("hT", 16)])

    chk(202)
    pg.barrier()
    arena["off"] = mark
    wup = [sb("wup%d" % i, [P, 8, 2, P], BF16) for i in range(2 * JG)]
    wdn = [sb("wdn%d" % i, [P, D], BF16) for i in range(2 * JG)]
    Ab = [sb("A%d" % i, [P, AW], BF16) for i in range(2 * JG)]
    cwork = [[sb("cw%d_%d" % (i, k), [P, 256]) for k in range(5)] for i in range(3)]
    co = sb("co", [P, 3, 2, 44])
    cot = sb("cot", [88, 3, P])
    r_cw = Ring(3)

    cgroups = [(256 * g, 258, 256 * g, "p") for g in range(8)] + [(2050, 132, 2048, "s")]

    def hT_toks(c0, n):
        toks = [tk("hTz")]
        for i in range(17):
            lo = 2 + i * P if i < 16 else 2052
            hi = lo + P if i < 16 else 2184
            if c0 < hi and c0 + n > lo:
                toks.append(tk("hT", i))
        return toks

    njg = (NJ + JG - 1) // JG
    jbufs = {}

    def gen_up(jg):
        js = list(range(jg * JG, min(NJ, (jg + 1) * JG)))
        bufs = []
        jbufs[jg] = bufs
        for jj, j in enumerate(js):
            bidx = (jg % 2) * JG + jj
            bufs.append(bidx)
            for vg in range(2):
                off = vg * DFF + j * P
                pg.add("pool", lambda e, bidx=bidx, vg=vg, off=off: e.dma_start(
                    out=wup[bidx][:, :, vg, :], in_=w_up[:, off:off + P].rearrange("(c p) n -> p c n", p=P)),
                    w=[tk("wup", bidx, vg)], dma=("wup", bidx, vg))
            pg.add("pool", lambda e, bidx=bidx, j=j: e.dma_start(out=wdn[bidx][:], in_=w_down[j * P:(j + 1) * P, :]),
                   w=[tk("wdn", bidx)], dma=("wdn", bidx))

            for (hc0, NC, ac0, kind) in cgroups:
                NO = NC - 2
                ht = hT_toks(hc0, NC)
                cwi = r_cw.next()
                cw = cwork[cwi]
                res = []
                for vg in range(2):
                    bk = ([3, 4], [5, 6], [2, 7])[cwi][vg]
                    for c in range(8):
                        pg.add("pe", lambda e, bk=bk, bidx=bidx, vg=vg, c=c, hc0=hc0, NC=NC, kind=kind: e.matmul(
                            PS[bk][:, 0:NC], lhsT=wup[bidx][:, c, vg, :], rhs=hT2[:, c, hc0:hc0 + NC],
                            start=(c == 0), stop=(c == 7)),
                            r=[tk("wup", bidx, vg)] + ht, w=[tk("ps", bk)])
                    blk = vg * NJ + j
                    if kind == "s":
                        pg.add("dve", lambda e, bk=bk, blk=blk: e.tensor_copy(
                            out=PS[bk][:, 0:132].rearrange("p (s c) -> p s c", c=66)[:, :, 0:2],
                            in_=cv[:, 4:8, blk].rearrange("p (s r) -> p s r", r=2)),
                            r=[tk("cv")], w=[tk("ps", bk)])
                    t1 = cw[vg * 2]
                    t2 = cw[vg * 2 + 1]
                    pg.add("act", lambda e, bk=bk, t1=t1, blk=blk, NC=NC, NO=NO: e.activation(
                        out=t1[:, 0:NO], in_=PS[bk][:, 2:NC], func=AF.Identity, scale=cv[:, 2, blk:blk + 1],
                        bias=cv[:, 3, blk:blk + 1]),
                        r=[tk("ps", bk), tk("cv")], w=[tk("cw", cwi, vg * 2)])
                    pg.add("dve", lambda e, bk=bk, t1=t1, t2=t2, blk=blk, NC=NC, NO=NO: e.scalar_tensor_tensor(
                        out=t2[:, 0:NO], in0=PS[bk][:, 1:NC - 1], scalar=cv[:, 1, blk:blk + 1], in1=t1[:, 0:NO],
                        op0=ALU.mult, op1=ALU.add),
                        r=[tk("ps", bk), tk("cv"), tk("cw", cwi, vg * 2)], w=[tk("cw", cwi, vg * 2 + 1)])
                    pg.add("dve", lambda e, bk=bk, t1=t1, t2=t2, blk=blk, NC=NC, NO=NO: e.scalar_tensor_tensor(
                        out=t1[:, 0:NO], in0=PS[bk][:, 0:NO], scalar=cv[:, 0, blk:blk + 1], in1=t2[:, 0:NO],
                        op0=ALU.mult, op1=ALU.add),
                        r=[tk("ps", bk), tk("cv"), tk("cw", cwi, vg * 2 + 1)], w=[tk("cw", cwi, vg * 2)])
                    res.append(t1)
                    if kind == "p" and hc0 == 256 * 7:
                        pg.add("act", lambda e, bk=bk, blk=blk, NC=NC: e.copy(out=co[:, 0, :, blk], in_=PS[bk][:, NC - 2:NC]),
                               r=[tk("ps", bk)], w=[tk("co")])
                    if kind == "s":
                        pg.add("act", lambda e, bk=bk, blk=blk: e.copy(
                            out=co[:, 1:3, :, blk], in_=PS[bk][:, 0:132].rearrange("p (s c) -> p s c", c=66)[:, :, 64:66]),
                            r=[tk("ps", bk)], w=[tk("co")])
                sg = cw[4]
                pg.add("act", lambda e, sg=sg, g_=res[1], NO=NO: e.activation(out=sg[:, 0:NO], in_=g_[:, 0:NO], func=AF.Silu),
                       r=[tk("cw", cwi, 2)], w=[tk("cw", cwi, 4)])
                if kind == "p":
                    atoks = [tk("A", bidx, ac0 // P), tk("A", bidx, ac0 // P + 1)]
                else:
                    atoks = [tk("A", bidx, 16)]
                if kind == "p":
                    pg.add("pool", lambda e, bidx=bidx, ac0=ac0, NO=NO, sg=sg, v_=res[0]: e.tensor_tensor(
                        out=Ab[bidx][:, ac0:ac0 + NO], in0=v_[:, 0:NO], in1=sg[:, 0:NO], op=ALU.mult),
                        r=[tk("cw", cwi, 0), tk("cw", cwi, 4)], w=atoks)
                else:
                    pg.add("pool", lambda e, bidx=bidx, sg=sg, v_=res[0]: e.tensor_tensor(
                        out=Ab[bidx][:, 2048:2176].rearrange("p (s c) -> p s c", c=64),
                        in0=v_[:, 0:132].rearrange("p (s c) -> p s c", c=66)[:, :, 0:64],
                        in1=sg[:, 0:132].rearrange("p (s c) -> p s c", c=66)[:, :, 0:64], op=ALU.mult),
                        r=[tk("cw", cwi, 0), tk("cw", cwi, 4)], w=atoks)
                yield

    def gen_down(jg):
        bufs = jbufs[jg]
        for i in range(17):
            for hf_ in range(2):
                bk = [0, 1][hf_]
                for jj, bidx in enumerate(bufs):
                    lhs = Ab[bidx][:, i * P:(i + 1) * P]
                    pg.add("pe", lambda e, bk=bk, lhs=lhs, bidx=bidx, hf_=hf_, jj=jj, n=len(bufs): e.matmul(
                        PS[bk][:, :], lhsT=lhs, rhs=wdn[bidx][:, hf_ * 512:(hf_ + 1) * 512],
                        start=(jj == 0), stop=(jj == n - 1)),
                        r=[tk("A", bidx, i), tk("wdn", bidx)], w=[tk("ps", bk)])
                pg.add("dve", lambda e, bk=bk, i=i, hf_=hf_: e.tensor_tensor(
                    out=ACCT[:, i, hf_ * 512:(hf_ + 1) * 512], in0=PS[bk][:, :], in1=ACCT[:, i, hf_ * 512:(hf_ + 1) * 512],
                    op=ALU.add),
                    r=[tk("ps", bk), tk("acc", i)], w=[tk("acc", i)])
                yield

    run_interleaved([gen_up(0)])
    for jg in range(njg):
        run_interleaved([gen_down(jg), gen_up(jg + 1) if jg + 1 < njg else None])

    chk(203)
    for inst in range(3):
        pg.add("pe", lambda e, inst=inst: e.transpose(out=PS[2][0:88, 0:P], in_=co[:, inst, :, :].rearrange("p a b -> p (a b)"),
                                                      identity=ident[:]),
               r=[tk("co"), tk("ident")], w=[tk("ps", 2)])
        pg.add("dve", lambda e, inst=inst: e.tensor_copy(out=cot[:, inst, :], in_=PS[2][0:88, 0:P]),
               r=[tk("ps", 2)], w=[tk("cot", inst)])
        for r_ in range(2):
            dst = conv_p[r_, :] if inst == 0 else conv_s[inst - 1, r_, :]
            out_events.append(pg.add("sp", lambda e, inst=inst, r_=r_, dst=dst: e.dma_start(
                out=dst.rearrange("(b p) -> b p", p=P), in_=cot[r_ * 44:(r_ + 1) * 44, inst, :]),
                r=[tk("cot", inst)], dma=("cot", inst)))

    g2b = sb("g2b", [P, D])
    b2b = sb("b2b", [P, D])
    yo = [sb("yo%d" % i, [P, D]) for i in range(2)]
    pg.add("sp", lambda e: e.dma_start(out=g2b[:], in_=ln2_g.to_broadcast([P, D])), w=[tk("g2b")], dma="g2b")
    pg.add("sp", lambda e: e.dma_start(out=b2b[:], in_=ln2_b.to_broadcast([P, D])), w=[tk("b2b")], dma="b2b")
    for i in range(17):
        bi = i % 2
        layer_norm_tile(ACCT[:, i, :], bi, g2b, b2b, tk("g2b"), tk("b2b"), yo[bi][:], tk("yo", bi), tk("acc", i))
        dst = y_p[i * P:(i + 1) * P, :] if i < 16 else y_s
        out_events.append(pg.add("sp", lambda e, bi=bi, dst=dst: e.dma_start(out=dst, in_=yo[bi][:]),
                                 r=[tk("yo", bi)], dma=("yo", bi)))

    pg.barrier()


_NC_CACHE = {}


def kernel(x_prompt, x_sample, cache_k, cache_v, state_hgrn, state_ffn_conv, w_in, hg_lb_logits,
           hg_norm_g, da_lambda_q1, da_lambda_k1, da_lambda_q2, da_lambda_k2, da_subln_g,
           w_br_hg, w_br_da, w_out, ln1_g, ln1_b, w_up, conv_w, conv_b, w_down, ln2_g, ln2_b):
    f = lambda a: np.ascontiguousarray(np.asarray(a, dtype=np.float32))
    if "nc" not in _NC_CACHE:
        _NC_CACHE["nc"] = build_nc()
    nc = _NC_CACHE["nc"]
    shared = {
        "w_in": f(w_in[0]),
        "lb_logits": f(hg_lb_logits).reshape(16, 128),
        "hg_norm_g": f(hg_norm_g).reshape(128, 1),
        "lq1": f(da_lambda_q1), "lk1": f(da_lambda_k1), "lq2": f(da_lambda_q2), "lk2": f(da_lambda_k2),
        "subln_g": f(da_subln_g).reshape(128, 1),
        "w_br_hg": f(w_br_hg[0]), "w_br_da": f(w_br_da[0]), "w_out": f(w_out[0]),
        "ln1_g": f(ln1_g), "ln1_b": f(ln1_b),
        "w_up": f(w_up[0]),
        "conv_wb": f(np.concatenate([np.asarray(conv_w[0]), np.asarray(conv_b)], axis=0)),
        "w_down": f(w_down[0]),
        "ln2_g": f(ln2_g), "ln2_b": f(ln2_b),
    }
    in_maps = []
    for c in range(8):
        m = dict(shared)
        m["x_p"] = f(x_prompt[c])
        m["x_s"] = f(x_sample[2 * c:2 * c + 2]).reshape(TS, D)
        m["cache_k"] = f(cache_k[0, 2 * c:2 * c + 2])
        m["cache_v"] = f(cache_v[0, 2 * c:2 * c + 2])
        m["state_hgrn"] = f(state_hgrn[0, 2 * c:2 * c + 2])
        m["state_conv"] = f(state_ffn_conv[0, 2 * c:2 * c + 2]).reshape(4, 2 * DFF)
        in_maps.append(m)
    res = run_bass_kernel_spmd(nc, in_maps, core_ids=list(range(8)))
    R = res.results
    y_prompt = np.stack([R[c]["y_p"] for c in range(8)], axis=0)
    y_sample = np.concatenate([R[c]["y_s"].reshape(2, SQ, D) for c in range(8)], axis=0)
    k_prompt = np.stack([R[c]["k_p"] for c in range(8)], axis=0)[None]
    v_prompt = np.stack([R[c]["v_p"] for c in range(8)], axis=0)[None]
    hgrn_prompt = np.stack([R[c]["hg_p"] for c in range(8)], axis=0)[None]
    conv_prompt = np.stack([R[c]["conv_p"] for c in range(8)], axis=0)[None]
    k_sample = np.concatenate([R[c]["k_s"].reshape(2, SQ, H, P) for c in range(8)], axis=0)[None]
    v_sample = np.concatenate([R[c]["v_s"].reshape(2, SQ, H, P) for c in range(8)], axis=0)[None]
    hgrn_sample = np.concatenate([R[c]["hg_s"] for c in range(8)], axis=0)[None]
    conv_sample = np.concatenate([R[c]["conv_s"] for c in range(8)], axis=0)[None]
    return (y_prompt.astype(np.float32), y_sample.astype(np.float32), k_prompt.astype(np.float32),
            v_prompt.astype(np.float32), hgrn_prompt.astype(np.float32), conv_prompt.astype(np.float32),
            k_sample.astype(np.float32), v_sample.astype(np.float32), hgrn_sample.astype(np.float32),
            conv_sample.astype(np.float32))
```

```python
import math
from collections import defaultdict
from contextlib import ExitStack

import numpy as np
import concourse.bass as bass
import concourse.mybir as mybir
from concourse.bass_utils import run_bass_kernel_spmd

F32 = mybir.dt.float32
BF16 = mybir.dt.bfloat16
ALU = mybir.AluOpType
AF = mybir.ActivationFunctionType
AX = mybir.AxisListType

P = 128
D = 1024
T = 2048
SQ = 64
TS = 2 * SQ
NTOK = T + TS
H = 8
DFF = 2816
NJ = DFF // P
PAST = 2048
LN_EPS = 1e-5
ALPHA = 2.0 ** 0.25
LAM_INIT = 0.8 - 0.6 * math.exp(-0.3 * 0)
DA_SCALE = 64 ** -0.5
HTW = 2184
AW = 2180
JG = 3

SAME_ENGINE_SYNC = True
EXCL_PSUM = True


class Tok:
    __slots__ = ("w", "r", "excl")

    def __init__(self):
        self.w = None
        self.r = {}
        self.excl = False


class _Rec:
    def __getattr__(self, name):
        return lambda *a, **k: (name, a, k)


_REC = _Rec()


class Op:
    __slots__ = ("fn", "deps", "inc", "dma")

    def __init__(self, fn, deps, dma):
        self.fn = fn(_REC) if fn is not None else None
        self.deps = deps
        self.inc = False
        self.dma = dma


class Prog:
    ENGS = ("pe", "act", "dve", "pool", "sp")

    def __init__(self):
        self.ops = {e: [] for e in self.ENGS}
        self.slots = {}
        self.tk = defaultdict(Tok)

    def t(self, *key):
        t = self.tk[key]
        if key[0] in ("ps", "pt"):
            t.excl = True
        return t

    def add(self, eng, fn, r=(), w=(), dma=None):
        if EXCL_PSUM:
            xr = [t for t in r if t.excl]
            if xr:
                r = [t for t in r if not t.excl]
                w = list(w) + [t for t in xr if t not in w]
        ops = self.ops[eng]
        idx = len(ops)
        deps = {}

        def need(k, v):
            if k[0] == "e" and k[1] == eng and (eng == "pe" or not SAME_ENGINE_SYNC) and dma is None:
                return
            if deps.get(k, -1) < v:
                deps[k] = v

        for t in r:
            if t.w is not None:
                need(*t.w)
        for t in w:
            if t.w is not None:
                need(*t.w)
            for k, v in t.r.items():
                need(k, v)
        for k, v in deps.items():
            if k[0] == "e":
                self.ops[k[1]][v].inc = True
        if dma is not None:
            cnt = self.slots.get(dma, 0) + 16
            self.slots[dma] = cnt
            ev = (("d", dma), cnt)
        else:
            ev = (("e", eng), idx)
        for t in w:
            t.w = ev
            t.r = {}
        for t in r:
            if t.w is not ev:
                if t.r.get(ev[0], -1) < ev[1]:
                    t.r[ev[0]] = ev[1]
        ops.append(Op(fn, deps, dma))
        return ev

    def barrier(self):
        last = {}
        for e in self.ENGS:
            for idx in range(len(self.ops[e]) - 1, -1, -1):
                op = self.ops[e][idx]
                if op.fn is not None and op.dma is None:
                    last[e] = idx
                    break
        for e in self.ENGS:
            deps = {}
            for e2, idx in last.items():
                if e2 != e:
                    deps[("e", e2)] = idx
                    self.ops[e2][idx].inc = True
            for k, cnt in self.slots.items():
                deps[("d", k)] = cnt
            self.ops[e].append(Op(None, deps, None))
        for t in self.tk.values():
            t.w = None
            t.r = {}

    def emit(self, nc, st):
        sems = {e: st.enter_context(nc.semaphore("s_" + e)) for e in self.ENGS}
        dsem = {k: st.enter_context(nc.semaphore("d%d" % i)) for i, k in enumerate(self.slots)}
        incval = {}
        for e in self.ENGS:
            c = 0
            vals = []
            for op in self.ops[e]:
                if op.inc:
                    c += 1
                vals.append(c)
            incval[e] = vals
        block = st.enter_context(nc.Block())

        def run(e, engine):
            waited = {}
            for op in self.ops[e]:
                for k, v in op.deps.items():
                    if k[0] == "e":
                        sem = sems[k[1]]
                        val = incval[k[1]][v]
                    else:
                        sem = dsem[k[1]]
                        val = v
                    if waited.get(k, 0) < val:
                        engine.wait_ge(sem, val)
                        waited[k] = val
                if op.fn is None:
                    continue
                ins = getattr(engine, op.fn[0])(*op.fn[1], **op.fn[2])
                if op.dma is not None:
                    ins.then_inc(dsem[op.dma], 16)
                elif op.inc:
                    ins.then_inc(sems[e], 1)

        @block.tensor
        def _(eng):
            run("pe", eng)

        @block.scalar
        def _(eng):
            run("act", eng)

        @block.vector
        def _(eng):
            run("dve", eng)

        @block.gpsimd
        def _(eng):
            run("pool", eng)

        @block.sync
        def _(eng):
            run("sp", eng)


class _Stop(Exception):
    pass


STOP = None
EXPT = 0


def build_nc():
    nc = bass.Bass("TRN2", target_bir_lowering=False, dynamic_dma_scratch_size=12288)
    pg = Prog()
    tk = pg.t
    st = ExitStack()

    def chk(n):
        if STOP is not None and STOP == n:
            raise _Stop()

    try:
        _build_body(nc, pg, tk, st, chk)
    except _Stop:
        pg.barrier()
    pg.emit(nc, st)
    st.close()
    return nc


def _build_body(nc, pg, tk, st, chk):

    def din(name, shape):
        return nc.dram_tensor(name, list(shape), F32, kind="ExternalInput").ap()

    def dout(name, shape):
        return nc.dram_tensor(name, list(shape), F32, kind="ExternalOutput").ap()

    x_p = din("x_p", [T, D])
    x_s = din("x_s", [TS, D])
    cache_k = din("cache_k", [2, PAST, H, P])
    cache_v = din("cache_v", [2, PAST, H, P])
    state_hgrn = din("state_hgrn", [2, H, P, P])
    state_conv = din("state_conv", [4, 2 * DFF])
    w_in = din("w_in", [D, 9 * D])
    lb_logits = din("lb_logits", [16, P])
    hg_norm_g = din("hg_norm_g", [P, 1])
    lq1 = din("lq1", [1, 64])
    lk1 = din("lk1", [1, 64])
    lq2 = din("lq2", [1, 64])
    lk2 = din("lk2", [1, 64])
    subln_g = din("subln_g", [P, 1])
    w_br_hg = din("w_br_hg", [D, D])
    w_br_da = din("w_br_da", [D, D])
    w_out = din("w_out", [D, D])
    ln1_g = din("ln1_g", [1, D])
    ln1_b = din("ln1_b", [1, D])
    w_up = din("w_up", [D, 2 * DFF])
    conv_wb = din("conv_wb", [4, 2 * DFF])
    w_down = din("w_down", [DFF, D])
    ln2_g = din("ln2_g", [1, D])
    ln2_b = din("ln2_b", [1, D])

    y_p = dout("y_p", [T, D])
    y_s = dout("y_s", [TS, D])
    k_p = dout("k_p", [T, H, P])
    v_p = dout("v_p", [T, H, P])
    hg_p = dout("hg_p", [H, P, P])
    conv_p = dout("conv_p", [2, 2 * DFF])
    k_s = dout("k_s", [TS, H, P])
    v_s = dout("v_s", [TS, H, P])
    hg_s = dout("hg_s", [2, H, P, P])
    conv_s = dout("conv_s", [2, 2, 2 * DFF])

    MAIN_BYTES = 216896
    MAIN = st.enter_context(nc.sbuf_tensor("main", [P, MAIN_BYTES // 4], F32))
    arena = {"off": 0, "peak": 0}

    def sb(name, shape, dt=F32):
        n = 1
        for s_ in shape[1:]:
            n *= s_
        nbytes = n * (4 if dt is F32 else 2)
        nb_al = (nbytes + 31) // 32 * 32
        off = arena["off"]
        assert off + nb_al <= MAIN_BYTES, ("SBUF arena overflow", name, off, nb_al)
        arena["off"] = off + nb_al
        arena["peak"] = max(arena["peak"], arena["off"])
        v = MAIN[:, off // 4:(off + nb_al) // 4]
        if dt is not F32:
            v = v.bitcast(dt)
        v = v[0:shape[0], 0:n]
        if len(shape) == 3:
            v = v.rearrange("p (a b) -> p a b", b=shape[2])
        elif len(shape) == 4:
            v = v.rearrange("p (a b c) -> p a b c", b=shape[2], c=shape[3])
        return v

    def ps(name, shape, dt=F32):
        return st.enter_context(nc.psum_tensor(name, list(shape), dt))

    out_events = []
    scratch = {"off": 160000}

    def sbs(name, shape, dt=F32):
        save = arena["off"]
        arena["off"] = scratch["off"]
        v = sb(name, shape, dt)
        scratch["off"] = arena["off"]
        arena["off"] = save
        return v

    PS = [ps("ps%d" % i, [P, 512], F32) for i in range(7)]
    PT = ps("pt", [P, 1024], BF16)

    ones = sbs("ones", [P, 132])
    ident = sb("ident", [P, P])
    identb = sb("identb", [P, P], BF16)
    hmask = sb("hmask", [P, P])
    rmask = sb("rmask", [P, 512], BF16)
    pg.add("pool", lambda e: e.memset(ones[:], 1.0), w=[tk("ones")])
    pg.add("pool", lambda e: e.affine_select(out=ident[:], in_=ones[:, 0:P], pattern=[[-1, P]],
                                              compare_op=ALU.is_equal, fill=0.0, base=0,
                                              channel_multiplier=1),
           r=[tk("ones")], w=[tk("ident")])
    pg.add("dve", lambda e: e.tensor_copy(out=identb[:], in_=ident[:]), r=[tk("ident")], w=[tk("identb")])
    pg.add("pool", lambda e: e.affine_select(out=hmask[:], in_=ones[:, 0:P], pattern=[[1, P]],
                                              compare_op=ALU.is_ge, fill=0.0, base=0,
                                              channel_multiplier=-1),
           r=[tk("ones")], w=[tk("hmask")])
    pg.add("pool", lambda e: e.memset(hmask[0:64, 64:128], 0.0), w=[tk("hmask")])
    pg.add("pool", lambda e: e.memset(rmask[:], 1.0), w=[tk("rmask")])
    pg.add("pool", lambda e: e.memset(rmask[:].rearrange("p (a b) -> p a b", b=64)[:, :, 0:1], 0.0),
           w=[tk("rmask")])

    lbrows = sbs("lbrows", [16, P])
    lgc = sbs("lgc", [P, 16])
    lbv = sb("lbv", [P, H])
    omlv = sb("omlv", [P, H])
    tmp8 = sbs("tmp8", [P, H])
    pg.add("sp", lambda e: e.dma_start(out=lbrows[:], in_=lb_logits), w=[tk("lbrows")], dma="lbrows")
    pg.add("pe", lambda e: e.transpose(out=PS[0][:, 0:16], in_=lbrows[:], identity=ident[0:16, 0:16]),
           r=[tk("lbrows"), tk("ident")], w=[tk("ps", 0)])
    pg.add("dve", lambda e: e.tensor_copy(out=lgc[:], in_=PS[0][:, 0:16]), r=[tk("ps", 0)], w=[tk("lgc")])
    pg.add("dve", lambda e: e.tensor_tensor(out=tmp8[:], in0=lgc[:, 8:16], in1=lgc[:, 0:8], op=ALU.subtract),
           r=[tk("lgc")], w=[tk("tmp8")])
    pg.add("act", lambda e: e.activation(out=tmp8[:], in_=tmp8[:], func=AF.Exp), r=[tk("tmp8")], w=[tk("tmp8")])
    pg.add("dve", lambda e: e.tensor_scalar(out=tmp8[:], in0=tmp8[:], scalar1=1.0, scalar2=None, op0=ALU.add),
           r=[tk("tmp8")], w=[tk("tmp8")])
    pg.add("dve", lambda e: e.reciprocal(out=lbv[:], in_=tmp8[:]), r=[tk("tmp8")], w=[tk("lbv")])
    pg.add("dve", lambda e: e.tensor_scalar(out=omlv[:], in0=lbv[:], scalar1=-1.0, scalar2=1.0,
                                             op0=ALU.mult, op1=ALU.add),
           r=[tk("lbv")], w=[tk("omlv")])

    lamin = sbs("lamin", [P, 4, 64])
    lamt = sbs("lamt", [P, 2, 64])
    lams = sbs("lams", [P, 2])
    neglam = sb("neglam", [P, 1])
    for i, src in enumerate((lq1, lk1, lq2, lk2)):
        pg.add("sp", lambda e, i=i, src=src: e.dma_start(out=lamin[:, i, :], in_=src.to_broadcast([P, 64])),
               w=[tk("lamin", i)], dma=("lamin", i))
    pg.add("dve", lambda e: e.tensor_tensor(out=lamt[:], in0=lamin[:, 0::2, :], in1=lamin[:, 1::2, :], op=ALU.mult),
           r=[tk("lamin", i) for i in range(4)], w=[tk("lamt")])
    pg.add("dve", lambda e: e.tensor_reduce(out=lams[:], in_=lamt[:], axis=AX.X, op=ALU.add),
           r=[tk("lamt")], w=[tk("lams")])
    pg.add("act", lambda e: e.activation(out=lams[:], in_=lams[:], func=AF.Exp), r=[tk("lams")], w=[tk("lams")])
    pg.add("dve", lambda e: e.tensor_tensor(out=neglam[:], in0=lams[:, 1:2], in1=lams[:, 0:1], op=ALU.subtract),
           r=[tk("lams")], w=[tk("neglam")])
    pg.add("dve", lambda e: e.tensor_scalar(out=neglam[:], in0=neglam[:], scalar1=-LAM_INIT, scalar2=None, op0=ALU.add),
           r=[tk("neglam")], w=[tk("neglam")])

    hgg = sb("hgg", [P, 1])
    sgg = sb("sgg", [P, 1])
    pg.add("sp", lambda e: e.dma_start(out=hgg[:], in_=hg_norm_g), w=[tk("hgg")], dma="hgg")
    pg.add("sp", lambda e: e.dma_start(out=sgg[:], in_=subln_g), w=[tk("sgg")], dma="sgg")
    pg.add("dve", lambda e: e.tensor_scalar(out=sgg[:], in0=sgg[:], scalar1=1.0 - LAM_INIT, scalar2=None, op0=ALU.mult),
           r=[tk("sgg")], w=[tk("sgg")])

    cvrows = sbs("cvrows", [44, 8, P])
    cv = sb("cv", [P, 8, 44])
    for r_ in range(8):
        srow = conv_wb[r_, :] if r_ < 4 else state_conv[r_ - 4, :]
        pg.add("sp", lambda e, r_=r_, srow=srow: e.dma_start(out=cvrows[:, r_, :],
                                                   in_=srow.rearrange("(b p) -> b p", p=P)),
               w=[tk("cvrows", r_)], dma=("cvrows", r_))
        pg.add("pe", lambda e, r_=r_: e.transpose(out=PS[1][:, r_ * 44:(r_ + 1) * 44], in_=cvrows[:, r_, :],
                                                   identity=ident[0:44, 0:44]),
               r=[tk("cvrows", r_), tk("ident")], w=[tk("ps", 1)])
    pg.add("dve", lambda e: e.tensor_copy(out=cv[:].rearrange("p a b -> p (a b)"), in_=PS[1][:, 0:352]),
           r=[tk("ps", 1)], w=[tk("cv")])

    hT2 = sb("hT2", [P, 8, HTW], BF16)
    xT = hT2
    big1_off = arena["off"]
    yT = sb("yT", [P, 16, NTOK], BF16)
    arena["off"] = big1_off
    ACCT = sb("acc", [P, 17, D], F32)
    epsc = sb("epsc", [P, 1])
    stt = [sb("stt%d" % i, [P, 12]) for i in range(3)]
    mv = [sb("mv%d" % i, [P, 4]) for i in range(3)]
    pg.add("pool", lambda e: e.memset(epsc[:], LN_EPS), w=[tk("epsc")])
    mark = arena["off"]

    xb = [sb("xb%d" % i, [P, D], BF16) for i in range(2)]
    for i in range(17):
        src = x_p[i * P:(i + 1) * P, :] if i < 16 else x_s
        b = i % 2
        pg.add("pool", lambda e, b=b, src=src: e.dma_start(out=xb[b][:], in_=src), w=[tk("xb", b)], dma=("xb", b))
        for dc in range(8):
            pg.add("pe", lambda e, b=b, dc=dc: e.transpose(out=PT[:, dc * P:(dc + 1) * P],
                                                            in_=xb[b][:, dc * P:(dc + 1) * P], identity=identb[:]),
                   r=[tk("xb", b), tk("identb")], w=[tk("pt")])
        eng = "act" if i % 2 else "dve"
        if eng == "act":
            pg.add("act", lambda e, i=i: e.copy(out=xT[:, :, i * P:(i + 1) * P],
                                                 in_=PT[:].rearrange("p (a b) -> p a b", b=P)),
                   r=[tk("pt")], w=[tk("xT", i)])
        else:
            pg.add("dve", lambda e, i=i: e.tensor_copy(out=xT[:, :, i * P:(i + 1) * P],
                                                        in_=PT[:].rearrange("p (a b) -> p a b", b=P)),
                   r=[tk("pt")], w=[tk("xT", i)])

    chk(0)
    pg.barrier()
    arena["off"] = mark
    wh = [sb("wh%d" % i, [P, 8, 7 * P], BF16) for i in range(2)]
    WOFF = [0, 1, 4, 5, 2, 3, 6]
    hq_sb = [sb("hq%d" % i, [P, 512]) for i in range(2)]
    sigp = [sb("sigp%d" % i, [P, 512]) for i in range(2)]
    sign = [sb("sign%d" % i, [P, 512]) for i in range(2)]
    fbuf = sigp
    bbuf = [sb("bbuf%d" % i, [P, 512]) for i in range(2)]
    ebuf = [sb("ebuf%d" % i, [P, 512]) for i in range(2)]
    enbuf = bbuf
    qdz = [sb("qdz%d" % i, [P, 4, 3, 64], BF16) for i in range(2)]
    kdT = [sb("kdT%d" % i, [P, 512], BF16) for i in range(2)]
    qT = [sb("qT%d" % i, [P, 2, 512], BF16) for i in range(2)]
    vh = [sb("vh%d" % i, [P, 4, P], BF16) for i in range(2)]
    sog = [sb("sog%d" % i, [P, 4, P]) for i in range(2)]
    tm32 = [sb("tm32_%d" % i, [P, 512]) for i in range(2)]
    qTs = [sb("qTs%d" % i, [P, 512], BF16) for i in range(2)]
    kdtm = [sb("kdtm%d" % i, [P, 2, P], BF16) for i in range(2)]
    atm = [sb("atm%d" % i, [P, P], BF16) for i in range(2)]
    s32 = [sb("s32_%d" % i, [P, P]) for i in range(3)]
    sbf = [sb("sbf%d" % i, [P, P], BF16) for i in range(3)]
    t1b = [sb("t1b%d" % i, [P, P]) for i in range(2)]
    junk = sb("junk", [P, P])
    ssb = [sb("ssb%d" % i, [P, 2]) for i in range(4)]
    yh = [sb("yh%d" % i, [P, P], BF16) for i in range(2)]
    ktp = sb("ktp", [P, T], BF16)
    vp = sb("vp", [P, 16, 136], BF16)
    kts1 = sb("kts", [P, PAST], BF16)
    vs1_ = sb("vs", [P, 16, 136], BF16)
    ktn = [sb("ktn%d" % i, [P, SQ], BF16) for i in range(2)]
    vn = [sb("vn%d" % i, [SQ, 136], BF16) for i in range(2)]
    kc1 = sb("kc", [P, 16, P], BF16)
    kc = [kc1, kc1]
    pT = [sb("pT%d" % i, [P, 2, 256], BF16) for i in range(2)]
    rden = [sb("rden%d" % i, [P, 4]) for i in range(2)]
    t1a = [sb("t1a%d" % i, [P, P]) for i in range(2)]
    abuf = [sb("abuf%d" % i, [P, P]) for i in range(2)]
    ya = [sb("ya%d" % i, [P, P], BF16) for i in range(2)]

    for i in range(2):
        pg.add("pool", lambda e, i=i: e.memset(qT[i][:], 0.0), w=[tk("qT", i)])
        pg.add("pool", lambda e, i=i: e.memset(qdz[i][:, :, 1, :], 0.0), w=[tk("qdz", i)])
        pg.add("pool", lambda e, i=i: e.memset(vn[i][:, 128:130], 1.0), w=[tk("vn1", i)])
    pg.add("pool", lambda e: e.memset(vs1_[:, :, 128:130], 1.0), w=[tk("vs1", 0)])
    pg.add("pool", lambda e: e.memset(vp[:, :, 128:130], 1.0), w=[tk("vp1")])

    class Ring:
        def __init__(self, n):
            self.n = n
            self.i = -1

        def next(self):
            self.i = (self.i + 1) % self.n
            return self.i

    r_kvst = Ring(2)
    r_kdtm = Ring(2)
    r_atm = Ring(2)
    r_s = Ring(3)
    r_t1b = Ring(2)
    r_ss = Ring(4)
    r_yh = Ring(2)
    r_pT = Ring(2)
    r_sc = Ring(2)
    r_rden = Ring(2)
    r_t1a = Ring(2)
    r_ab = Ring(2)
    r_ya = Ring(2)
    r_pa = Ring(2)

    PA = [PS[0], PS[1]]
    PH = PS[2]
    SC = [PS[3], PS[4]]
    AC = [PS[5], PS[6]]
    PAI = [0, 1]
    SCI = [3, 4]
    ACI = [5, 6]

    def evac_eng(i):
        return "act" if i % 2 else "dve"

    groups = []
    for g in range(4):
        groups.append((g * 512, 512, [(g * 512 + k * P, P, "p", None) for k in range(4)]))
    groups.append((T, TS, [(T, SQ, "s", 0), (T + SQ, SQ, "s", 1)]))

    def xtoks(t0, n):
        return [tk("xT", i) for i in range(t0 // P, (t0 + n + P - 1) // P)]

    def run_interleaved(gens):
        gens = [g for g in gens if g is not None]
        while gens:
            for g in list(gens):
                try:
                    next(g)
                except StopIteration:
                    gens.remove(g)

    def rstd_ops(si, R):
        pg.add("act", lambda e: e.activation(out=ssb[si][0:R, 1:2], in_=ssb[si][0:R, 0:1], func=AF.Ln,
                                             scale=1.0 / P, bias=epsc[0:R, :]),
               r=[tk("ssb", si), tk("epsc")], w=[tk("ssb", si)])
        pg.add("act", lambda e: e.activation(out=ssb[si][0:R, 1:2], in_=ssb[si][0:R, 1:2], func=AF.Exp, scale=-0.5),
               r=[tk("ssb", si)], w=[tk("ssb", si)])

    def gen_attend(h, qbuf, qc0, nq, qrows, keytiles, out_cols):
        nqt = nq // qrows
        acc_v = [AC[i][:].rearrange("p (m c) -> p m c", m=2) for i in range(nqt)]
        last_for_qt = {}
        first_for_qt = {}
        for ki, kt in enumerate(keytiles):
            for iq in range(nqt):
                if iq * qrows >= kt[3]:
                    last_for_qt[iq] = ki
                    first_for_qt.setdefault(iq, ki)

        def scores(ki):
            kt_ap, v_ap, ns, q0, diag, rtoks = keytiles[ki]
            sci = r_sc.next()
            scv = SC[sci][:].rearrange("p (m c) -> p m c", m=2)
            for m in range(2):
                pg.add("pe", lambda e, m=m: e.matmul(
                    scv[0:ns, m, q0:nq], lhsT=kt_ap, rhs=qT[qbuf][:, m, qc0 + q0:qc0 + nq],
                    start=(m == 0), stop=(m == 1)),
                    r=rtoks + [tk("qT", qbuf)], w=[tk("ps", SCI[sci])])
            return sci, scv

        nxt = scores(0)
        for ki, (kt_ap, v_ap, ns, q0, diag, rtoks) in enumerate(keytiles):
            sci, scv = nxt
            if ki + 1 < len(keytiles):
                nxt = scores(ki + 1)
            pi = r_pT.next()
            pg.add("act", lambda e: e.activation(out=pT[pi][0:ns, :, q0:nq], in_=scv[0:ns, :, q0:nq], func=AF.Exp),
                   r=[tk("ps", SCI[sci])], w=[tk("pT", pi)])
            if diag is not None:
                pg.add("pool", lambda e: e.memset(pT[pi][64:128, :, diag * P:diag * P + 64], 0.0), w=[tk("pT", pi)])
            yield
            for iq in range(nqt):
                if iq * qrows < q0:
                    continue
                for m in range(2):
                    pg.add("pe", lambda e, iq=iq, m=m: e.matmul(
                        acc_v[iq][0:qrows, m, 0:129], lhsT=pT[pi][0:ns, m, iq * qrows:(iq + 1) * qrows],
                        rhs=v_ap[:, 0:129], start=(ki == first_for_qt[iq] and m == 0),
                        stop=(ki == last_for_qt[iq] and m == 1)),
                        r=rtoks + [tk("pT", pi)], w=[tk("ps", ACI[iq])])
            yield
        for iq in range(nqt):
            yield from attend_finalize(h, acc_v[iq], tk("ps", ACI[iq]), qrows, out_cols[iq])

    def attend_finalize(h, av, atok, R, c0):
        if True:
            ri = r_rden.next()
            pg.add("dve", lambda e: e.reciprocal(out=rden[ri][0:R, 0:2], in_=av[0:R, :, 128]),
                   r=[atok], w=[tk("rden", ri)])
            pg.add("dve", lambda e: e.tensor_tensor(out=rden[ri][0:R, 2:3], in0=rden[ri][0:R, 1:2],
                                                    in1=neglam[0:R, :], op=ALU.mult),
                   r=[tk("rden", ri), tk("neglam")], w=[tk("rden", ri)])
            ti = r_t1a.next()
            pg.add("act", lambda e: e.activation(out=t1a[ti][0:R, :], in_=av[0:R, 1, 0:128], func=AF.Copy,
                                                 scale=rden[ri][0:R, 2:3]),
                   r=[atok, tk("rden", ri)], w=[tk("t1a", ti)])
            ai = r_ab.next()
            pg.add("dve", lambda e: e.scalar_tensor_tensor(
                out=abuf[ai][0:R, :], in0=av[0:R, 0, 0:128], scalar=rden[ri][0:R, 0:1], in1=t1a[ti][0:R, :],
                op0=ALU.mult, op1=ALU.add),
                r=[atok, tk("rden", ri), tk("t1a", ti)], w=[tk("abuf", ai)])
            yield
            si = r_ss.next()
            pg.add("act", lambda e: e.activation(out=junk[0:R, :], in_=abuf[ai][0:R, :], func=AF.Square,
                                                 accum_out=ssb[si][0:R, 0:1]),
                   r=[tk("abuf", ai)], w=[tk("junk"), tk("ssb", si)])
            rstd_ops(si, R)
            yi = r_ya.next()
            pg.add("dve", lambda e: e.tensor_scalar(out=ya[yi][0:R, :], in0=abuf[ai][0:R, :],
                                                    scalar1=ssb[si][0:R, 1:2], scalar2=None, op0=ALU.mult),
                   r=[tk("abuf", ai), tk("ssb", si)], w=[tk("ya", yi)])
            yield
            pg.add("pe", lambda e: e.transpose(out=PT[:, 0:R], in_=ya[yi][0:R, :], identity=identb[0:R, 0:R]),
                   r=[tk("ya", yi), tk("identb")], w=[tk("pt")])
            pg.add("act", lambda e: e.activation(out=yT[:, 8 + h, c0:c0 + R], in_=PT[:, 0:R], func=AF.Copy,
                                                 scale=sgg[:, 0:1]),
                   r=[tk("pt"), tk("sgg")], w=[tk("yT", 8 + h, c0 // P)])
            yield

    def gen_attend_sample(h, qbuf, qc0, keytiles, out_col):
        R = SQ
        av = AC[0][:].rearrange("p (m c) -> p m c", m=2)
        atok = tk("ps", ACI[0])
        steps = [keytiles[i:i + 2] for i in range(0, 16, 2)] + [keytiles[16:17]]
        nsteps = len(steps)

        def scores(si_):
            tiles_ = steps[si_]
            sci = r_sc.next()
            scv = SC[sci][:, 0:256].rearrange("p (a c) -> p a c", c=64)
            for t_, (kt_ap, v_ap, ns, q0, diag, rtoks) in enumerate(tiles_):
                for m in range(2):
                    pg.add("pe", lambda e, t_=t_, m=m, kt_ap=kt_ap, ns=ns: e.matmul(
                        scv[0:ns, t_ * 2 + m, :], lhsT=kt_ap, rhs=qT[qbuf][:, m, qc0:qc0 + R],
                        start=(t_ == 0 and m == 0), stop=(t_ == len(tiles_) - 1 and m == 1)),
                        r=rtoks + [tk("qT", qbuf)], w=[tk("ps", SCI[sci])])
            return sci, scv

        nxt = scores(0)
        for si_, tiles_ in enumerate(steps):
            sci, scv = nxt
            if si_ + 1 < nsteps:
                nxt = scores(si_ + 1)
            ns = tiles_[0][2]
            nb_ = 2 * len(tiles_)
            pi = r_pT.next()
            ptv = pT[pi][:].rearrange("p m c -> p (m c)")[:, 0:256].rearrange("p (a c) -> p a c", c=64)
            pg.add("act", lambda e: e.activation(out=ptv[0:ns, 0:nb_, :], in_=scv[0:ns, 0:nb_, :], func=AF.Exp),
                   r=[tk("ps", SCI[sci])], w=[tk("pT", pi)])
            yield
            for t_, (kt_ap, v_ap, ns_, q0, diag, rtoks) in enumerate(tiles_):
                for m in range(2):
                    first = (si_ == 0 and t_ == 0 and m == 0)
                    last = (si_ == nsteps - 1 and t_ == len(tiles_) - 1 and m == 1)
                    pg.add("pe", lambda e, t_=t_, m=m, v_ap=v_ap, ns_=ns_, first=first, last=last: e.matmul(
                        av[0:R, m, 0:129], lhsT=ptv[0:ns_, t_ * 2 + m, :], rhs=v_ap[:, 0:129], start=first, stop=last),
                        r=rtoks + [tk("pT", pi)], w=[atok])
            yield
        yield from attend_finalize(h, av, atok, R, out_col)

    def load_cache(h, sq):
        pg.add("pool", lambda e: e.dma_start(out=kc1[:], in_=cache_k[sq, :, h, :].rearrange("(j p) e -> p j e", p=P)),
               w=[tk("kc", 0)], dma=("kc", 0))
        pg.add("pool", lambda e: e.dma_start(out=vs1_[:, 0:16, 0:128],
                                             in_=cache_v[sq, :, h, :].rearrange("(j p) e -> p j e", p=P)),
               w=[tk("vs", 0)], dma=("vs", 0))

    def gen_cache_T():
        for half in range(2):
            for jj in range(8):
                j = half * 8 + jj
                pg.add("pe", lambda e, j=j, jj=jj: e.transpose(out=PT[:, jj * P:(jj + 1) * P], in_=kc1[:, j, :],
                                                               identity=identb[:]),
                       r=[tk("kc", 0), tk("identb")], w=[tk("pt")])
            pg.add("dve", lambda e, half=half: e.tensor_copy(out=kts1[:, half * 1024:(half + 1) * 1024], in_=PT[:]),
                   r=[tk("pt")], w=[tk("kts", 0)])
            yield

    def load_weights(h):
        wb = h % 2
        for j in range(7):
            off = WOFF[j] * D + h * P
            pg.add("pool", lambda e, j=j, off=off: e.dma_start(
                out=wh[wb][:, :, j * P:(j + 1) * P], in_=w_in[:, off:off + P].rearrange("(c p) n -> p c n", p=P)),
                w=[tk("wh", wb, j)], dma=("wh", wb, j))

    def gen_inproj(h, gi):
        wb = h % 2
        W = wh[wb]
        gb = (5 * h + gi) % 2
        t0, NT, tiles = groups[gi]
        xt = xtoks(t0, NT)
        for j in range(4):
            pi = r_pa.next()
            ptok = tk("ps", PAI[pi])
            for dc in range(8):
                pg.add("pe", lambda e, dc=dc: e.matmul(
                    PA[pi][:, 0:NT], lhsT=W[:, dc, j * P:(j + 1) * P], rhs=xT[:, dc, t0:t0 + NT],
                    start=(dc == 0), stop=(dc == 7)),
                    r=[tk("wh", wb, j)] + xt, w=[ptok])
                if dc == 99:
                    yield
            src = PA[pi][:, 0:NT]
            if j == 0:
                pg.add("dve", lambda e: e.tensor_copy(out=hq_sb[gb][:, 0:NT], in_=src), r=[ptok], w=[tk("hq", gb)])
            elif j == 1:
                pg.add("act", lambda e: e.activation(out=sign[gb][:, 0:NT], in_=src, func=AF.Exp, scale=-1.0),
                       r=[ptok], w=[tk("sign", gb)])
            elif j == 2:
                pg.add("act", lambda e: e.activation(out=qTs[gb][:, 0:NT], in_=src, func=AF.Copy, scale=DA_SCALE),
                       r=[ptok], w=[tk("qTs", gb)])
                for m in range(2):
                    pg.add("pool", lambda e, m=m: e.tensor_copy(
                        out=qT[gb][m * 64:(m + 1) * 64, m, 0:NT], in_=qTs[gb][m * 64:(m + 1) * 64, 0:NT]),
                        r=[tk("qTs", gb)], w=[tk("qT", gb)])
            else:
                if gi < 4:
                    pg.add("dve", lambda e: e.tensor_copy(out=ktp[:, t0:t0 + NT], in_=src), r=[ptok], w=[tk("ktp", gi)])
                else:
                    for sq in range(2):
                        pg.add("dve", lambda e, sq=sq: e.tensor_copy(out=ktn[sq][:, :], in_=PA[pi][:, sq * SQ:(sq + 1) * SQ]),
                               r=[ptok], w=[tk("ktn", sq)])
            yield
        sp_, sn_ = sigp[gb][:, 0:NT], sign[gb][:, 0:NT]
        pg.add("dve", lambda e: e.tensor_scalar(out=sp_, in0=sn_, scalar1=1.0, scalar2=None, op0=ALU.add),
               r=[tk("sign", gb)], w=[tk("sigp", gb)])
        for q_ in range(0, NT, 128):
            pg.add("dve", lambda e, q_=q_: e.reciprocal(out=sigp[gb][:, q_:q_ + 128], in_=sigp[gb][:, q_:q_ + 128]),
                   r=[tk("sigp", gb)], w=[tk("sigp", gb)])
            yield
        pg.add("dve", lambda e: e.tensor_tensor(out=sn_, in0=sn_, in1=sp_, op=ALU.mult),
               r=[tk("sign", gb), tk("sigp", gb)], w=[tk("sign", gb)])
        pg.add("dve", lambda e: e.tensor_scalar(out=sp_, in0=sp_, scalar1=omlv[:, h:h + 1], scalar2=lbv[:, h:h + 1],
                                                op0=ALU.mult, op1=ALU.add),
               r=[tk("sigp", gb), tk("omlv"), tk("lbv")], w=[tk("sigp", gb)])
        yield
        pg.add("act", lambda e: e.activation(out=sp_, in_=sp_, func=AF.Ln), r=[tk("sigp", gb)], w=[tk("sigp", gb)])
        pg.add("dve", lambda e: e.tensor_tensor_scan(out=bbuf[gb][:, 0:NT], data0=rmask[:, 0:NT], data1=sp_, initial=0.0,
                                                     op0=ALU.mult, op1=ALU.add),
               r=[tk("sigp", gb), tk("rmask")], w=[tk("bbuf", gb)])
        pg.add("act", lambda e: e.activation(out=ebuf[gb][:, 0:NT], in_=bbuf[gb][:, 0:NT], func=AF.Exp),
               r=[tk("bbuf", gb)], w=[tk("ebuf", gb)])
        pg.add("act", lambda e: e.activation(out=bbuf[gb][:, 0:NT], in_=bbuf[gb][:, 0:NT], func=AF.Exp, scale=-1.0),
               r=[tk("bbuf", gb)], w=[tk("bbuf", gb)])
        yield
        nch_g = NT // 64
        pg.add("dve", lambda e: e.tensor_tensor(
            out=qdz[gb][:, 0:nch_g // 2, 0::2, :],
            in0=hq_sb[gb][:, 0:NT].rearrange("p (a b c) -> p a b c", b=2, c=64),
            in1=ebuf[gb][:, 0:NT].rearrange("p (a b c) -> p a b c", b=2, c=64), op=ALU.mult),
            r=[tk("hq", gb), tk("ebuf", gb)], w=[tk("qdz", gb)])
        pg.add("dve", lambda e: e.scalar_tensor_tensor(
            out=kdT[gb][:, 0:NT], in0=sn_, scalar=omlv[:, h:h + 1], in1=bbuf[gb][:, 0:NT],
            op0=ALU.mult, op1=ALU.mult),
            r=[tk("sign", gb), tk("omlv"), tk("bbuf", gb)], w=[tk("kdT", gb)])
        yield
        for li, (tt0, R, kind, sq) in enumerate(tiles):
            pi = r_pa.next()
            ptok = tk("ps", PAI[pi])
            for dc in range(8):
                pg.add("pe", lambda e, dc=dc: e.matmul(
                    PA[pi][0:R, :], lhsT=xT[:, dc, tt0:tt0 + R], rhs=W[:, dc, 3 * P:7 * P],
                    start=(dc == 0), stop=(dc == 7)),
                    r=[tk("wh", wb, j) for j in (3, 4, 5, 6)] + xtoks(tt0, R), w=[ptok])
                if dc == 99:
                    yield
            ki = r_kvst.next()
            ttok = tk("tm32", ki)
            tmv = tm32[ki][:].rearrange("p (a b) -> p a b", b=P)
            pg.add("act", lambda e: e.copy(out=tm32[ki][0:R, :], in_=PA[pi][0:R, :]), r=[ptok], w=[ttok])
            if kind == "p":
                kd, vd = k_p[tt0:tt0 + R, h, :], v_p[tt0:tt0 + R, h, :]
            else:
                kd, vd = k_s[sq * SQ:(sq + 1) * SQ, h, :], v_s[sq * SQ:(sq + 1) * SQ, h, :]
            pg.add("sp", lambda e: e.dma_start(out=kd, in_=tmv[0:R, 0, :]), r=[ttok], dma=("tm32", ki))
            pg.add("sp", lambda e: e.dma_start(out=vd, in_=tmv[0:R, 3, :]), r=[ttok], dma=("tm32", ki))
            pg.add("pool", lambda e: e.tensor_copy(out=vh[gb][0:R, li, :], in_=tmv[0:R, 1, :]), r=[ttok], w=[tk("vh", gb, li)])
            pg.add("act", lambda e: e.activation(out=sog[gb][0:R, li, :], in_=tmv[0:R, 2, :], func=AF.Exp, scale=-1.0),
                   r=[ttok], w=[tk("sog", gb, li)])
            if kind == "p":
                ti = tt0 // P
                pg.add("pool", lambda e: e.tensor_copy(out=vp[:, ti, 0:128], in_=tmv[:, 3, :]), r=[ttok], w=[tk("vp", ti)])
            else:
                pg.add("pool", lambda e: e.tensor_copy(out=vn[sq][0:SQ, 0:128], in_=tmv[0:SQ, 3, :]), r=[ttok], w=[tk("vsn", sq)])
            so_ = sog[gb][0:R, li, :]
            pg.add("dve", lambda e: e.tensor_scalar(out=so_, in0=so_, scalar1=1.0, scalar2=None, op0=ALU.add),
                   r=[tk("sog", gb, li)], w=[tk("sog", gb, li)])
            pg.add("dve", lambda e: e.reciprocal(out=so_, in_=so_), r=[tk("sog", gb, li)], w=[tk("sog", gb, li)])
            yield

    hstate = {"s_cur": None}

    def gen_hgrn(h, gi):
        gb = (5 * h + gi) % 2
        t0, NT, tiles = groups[gi]
        phv = PH[:].rearrange("p (a b) -> p a b", b=P)
        ptok = tk("ps", 2)
        for li, (tt0, R, kind, sq) in enumerate(tiles):
            nch = R // 64
            c0 = tt0 - t0
            lt = c0 // P
            if kind == "p" and tt0 == 0:
                s_cur = r_s.next()
                pg.add("pool", lambda e: e.memset(s32[s_cur][:], 0.0), w=[tk("s32", s_cur)])
                pg.add("pool", lambda e: e.memset(sbf[s_cur][:], 0.0), w=[tk("sbf", s_cur)])
                hstate["s_cur"] = s_cur
            if kind == "s":
                s_cur = r_s.next()
                pg.add("sp", lambda e: e.dma_start(out=s32[s_cur][:], in_=state_hgrn[sq, h, :, :]),
                       w=[tk("s32", s_cur)], dma=("s32", s_cur))
                pg.add("pool", lambda e: e.tensor_copy(out=sbf[s_cur][:], in_=s32[s_cur][:]),
                       r=[tk("s32", s_cur)], w=[tk("sbf", s_cur)])
                hstate["s_cur"] = s_cur
            s_cur = hstate["s_cur"]
            pg.add("pe", lambda e: e.transpose(out=PT[0:R, 0:P], in_=kdT[gb][:, c0:c0 + R], identity=identb[:]),
                   r=[tk("kdT", gb), tk("identb")], w=[tk("pt")])
            kmi = r_kdtm.next()
            for c in range(nch):
                pg.add("dve", lambda e, c=c: e.tensor_copy(out=kdtm[kmi][c * 64:(c + 1) * 64, c, :],
                                                           in_=PT[c * 64:(c + 1) * 64, 0:P]),
                       r=[tk("pt")], w=[tk("kdtm", kmi)])
            if nch == 2:
                qd_mov = qdz[gb][:, lt, 0::2, :]
                att_out = phv[0:R, 0, 0:R].rearrange("p (a b) -> p a b", b=64)
            else:
                qd_mov = qdz[gb][:, 0, 2 * sq, :]
                att_out = phv[0:R, 0, 0:R]
            pg.add("pe", lambda e: e.matmul(att_out, lhsT=kdT[gb][:, c0:c0 + R], rhs=qd_mov, start=True, stop=True),
                   r=[tk("kdT", gb), tk("qdz", gb)], w=[ptok])
            ai = r_atm.next()
            pg.add("dve", lambda e: e.tensor_tensor(out=atm[ai][0:R, 0:R], in0=phv[0:R, 0, 0:R], in1=hmask[0:R, 0:R],
                                                    op=ALU.mult),
                   r=[ptok, tk("hmask")], w=[tk("atm", ai)])
            yield
            for c in range(nch):
                if nch == 2:
                    lhs = kdtm[kmi][:, c, :]
                    rhs = vh[gb][:, li, :]
                else:
                    lhs = kdtm[kmi][0:64, 0, :]
                    rhs = vh[gb][0:64, li, :]
                pg.add("pe", lambda e, c=c, lhs=lhs, rhs=rhs: e.matmul(phv[:, 1 + c, :], lhsT=lhs, rhs=rhs,
                                                                     start=True, stop=True),
                       r=[tk("kdtm", kmi), tk("vh", gb, li)], w=[ptok])
            s_before = [s_cur]
            for c in range(nch):
                ti = r_t1b.next()
                sc_ = s_cur
                pg.add("dve", lambda e, c=c, ti=ti, sc_=sc_: e.tensor_tensor(out=t1b[ti][:], in0=phv[:, 1 + c, :],
                                                                           in1=s32[sc_][:], op=ALU.add),
                       r=[ptok, tk("s32", sc_)], w=[tk("t1b", ti)])
                s_new = r_s.next()
                acol = c0 + c * 64 + 63
                pg.add("dve", lambda e, ti=ti, s_new=s_new, acol=acol: e.tensor_scalar(
                    out=s32[s_new][:], in0=t1b[ti][:], scalar1=ebuf[gb][:, acol:acol + 1], scalar2=None, op0=ALU.mult),
                    r=[tk("t1b", ti), tk("ebuf", gb)], w=[tk("s32", s_new)])
                pg.add("pool", lambda e, ti=ti, s_new=s_new, acol=acol: e.tensor_scalar(
                    out=sbf[s_new][:], in0=t1b[ti][:], scalar1=ebuf[gb][:, acol:acol + 1], scalar2=1.0,
                    op0=ALU.mult, op1=ALU.mult),
                    r=[tk("t1b", ti), tk("ebuf", gb)], w=[tk("sbf", s_new)])
                s_cur = s_new
                s_before.append(s_cur)
            hstate["s_cur"] = s_cur
            if kind == "p" and tt0 == T - P:
                pg.add("sp", lambda e: e.dma_start(out=hg_p[h, :, :], in_=s32[s_cur][:]),
                       r=[tk("s32", s_cur)], dma=("s32", s_cur))
            if kind == "s":
                pg.add("sp", lambda e: e.dma_start(out=hg_s[sq, h, :, :], in_=s32[s_cur][:]),
                       r=[tk("s32", s_cur)], dma=("s32", s_cur))
            yield
            pg.add("pe", lambda e: e.matmul(phv[0:R, 3, :], lhsT=atm[ai][0:R, 0:R], rhs=vh[gb][0:R, li, :],
                                            start=True, stop=False),
                   r=[tk("atm", ai), tk("vh", gb, li)], w=[ptok])
            for c in range(nch):
                if nch == 2:
                    lhs = qdz[gb][:, lt, :, :].rearrange("p a b -> p (a b)")[:, c * 64:c * 64 + P]
                else:
                    lhs = qdz[gb][:, 0, 2 * sq, :]
                sb_i = s_before[c]
                pg.add("pe", lambda e, lhs=lhs, sb_i=sb_i, c=c: e.matmul(
                    phv[0:R, 3, :], lhsT=lhs, rhs=sbf[sb_i][:], start=False, stop=(c == nch - 1)),
                    r=[tk("qdz", gb), tk("sbf", sb_i)], w=[ptok])
            si = r_ss.next()
            pg.add("act", lambda e: e.activation(out=junk[0:R, :], in_=phv[0:R, 3, :], func=AF.Square,
                                                 accum_out=ssb[si][0:R, 0:1]),
                   r=[ptok], w=[tk("junk"), tk("ssb", si)])
            rstd_ops(si, R)
            yield
            yi = r_yh.next()
            pg.add("dve", lambda e: e.scalar_tensor_tensor(
                out=yh[yi][0:R, :], in0=phv[0:R, 3, :], scalar=ssb[si][0:R, 1:2], in1=sog[gb][0:R, li, :],
                op0=ALU.mult, op1=ALU.mult),
                r=[ptok, tk("ssb", si), tk("sog", gb, li)], w=[tk("yh", yi)])
            yield
            pg.add("pe", lambda e: e.transpose(out=PT[:, 0:R], in_=yh[yi][0:R, :], identity=identb[0:R, 0:R]),
                   r=[tk("yh", yi), tk("identb")], w=[tk("pt")])
            pg.add("act", lambda e: e.activation(out=yT[:, h, tt0:tt0 + R], in_=PT[:, 0:R], func=AF.Copy,
                                                 scale=hgg[:, 0:1]),
                   r=[tk("pt"), tk("hgg")], w=[tk("yT", h, tt0 // P)])
            yield

    def gen_attn(h, gi):
        gb = (5 * h + gi) % 2
        t0, NT, tiles = groups[gi]
        if gi < 4:
            for qg in range(2):
                qt0 = t0 + qg * 256
                nkt = qt0 // P + 2
                keytiles = []
                for j in range(nkt):
                    if j < qt0 // P:
                        q0, diag = 0, None
                    else:
                        q0, diag = (j - qt0 // P) * P, j - qt0 // P
                    keytiles.append((ktp[:, j * P:(j + 1) * P], vp[:, j, :], P, q0, diag,
                                     [tk("ktp", j // 4), tk("vp", j), tk("vp1")]))
                yield from gen_attend(h, gb, qg * 256, 256, P, keytiles, [qt0, qt0 + P])
        else:
            for sq in range(2):
                if sq == 1:
                    load_cache(h, 1)
                yield from gen_cache_T()
                keytiles = []
                for j in range(16):
                    keytiles.append((kts1[:, j * P:(j + 1) * P], vs1_[:, j, :], P, 0, None,
                                     [tk("kts", 0), tk("vs", 0), tk("vs1", 0)]))
                keytiles.append((ktn[sq][:, :], vn[sq][0:SQ, :], SQ, 0, None,
                                 [tk("ktn", sq), tk("vsn", sq), tk("vn1", sq)]))
                yield from gen_attend_sample(h, gb, sq * SQ, keytiles, T + sq * SQ)

    for i in range(2):
        pg.add("pool", lambda e, i=i: e.memset(kdtm[i][:], 0.0), w=[tk("kdtm", i)])
    load_weights(0)
    tail = []
    for h in range(H):
        if h + 1 < H:
            load_weights(h + 1)
        run_interleaved(tail + [gen_inproj(h, 0)])
        load_cache(h, 0)
        for gi in range(4):
            run_interleaved([gen_attn(h, gi), gen_hgrn(h, gi), gen_inproj(h, gi + 1)])
        tail = [gen_attn(h, 4), gen_hgrn(h, 4)]
    run_interleaved(tail)
    pg.barrier()
    chk(200)

    arena["off"] = mark
    mT = sb("mT", [P, 8, NTOK], BF16)
    mark2 = arena["off"]
    w2 = [sb("w2_%d" % i, [P, 8, 4, P], BF16) for i in range(2)]
    s1b = [sb("s1b%d" % i, [P, 512]) for i in range(2)]
    s2b = [sb("s2b%d" % i, [P, 512]) for i in range(2)]
    m1b = [sb("m1b%d" % i, [P, 512]) for i in range(2)]
    m2b = [sb("m2b%d" % i, [P, 512]) for i in range(2)]
    bankset = [[0, 1, 3, 4], [5, 6, 2, 0]]
    it = 0
    for nb in range(8):
        b2 = nb % 2
        srcs = [w_in[:, 7 * D + nb * P:7 * D + (nb + 1) * P], w_in[:, 8 * D + nb * P:8 * D + (nb + 1) * P],
                w_br_hg[:, nb * P:(nb + 1) * P], w_br_da[:, nb * P:(nb + 1) * P]]
        for j in range(4):
            pg.add("pool", lambda e, b2=b2, j=j, src=srcs[j]: e.dma_start(
                out=w2[b2][:, :, j, :], in_=src.rearrange("(c p) n -> p c n", p=P)),
                w=[tk("w2", b2, j)], dma=("w2", b2, j))
        for gi, (t0, NT, tiles) in enumerate(groups):
            banks = [0, 1, 3, 4] if it % 2 == 0 else [5, 6, 2, 1]
            if it % 2 == 1:
                banks = [5, 6, 2, 0]
            it += 1
            gb = gi % 2
            xt = xtoks(t0, NT)
            ytk_hg = [tk("yT", hh, i) for hh in range(8) for i in range(t0 // P, (t0 + NT + P - 1) // P)]
            ytk_da = [tk("yT", 8 + hh, i) for hh in range(8) for i in range(t0 // P, (t0 + NT + P - 1) // P)]
            for j in range(4):
                bk = banks[j]
                for c in range(8):
                    if j < 2:
                        rhs = xT[:, c, t0:t0 + NT]
                        rt = xt
                    elif j == 2:
                        rhs = yT[:, c, t0:t0 + NT]
                        rt = ytk_hg
                    else:
                        rhs = yT[:, 8 + c, t0:t0 + NT]
                        rt = ytk_da
                    pg.add("pe", lambda e, bk=bk, b2=b2, j=j, c=c, rhs=rhs, NT=NT: e.matmul(
                        PS[bk][:, 0:NT], lhsT=w2[b2][:, c, j, :], rhs=rhs, start=(c == 0), stop=(c == 7)),
                        r=[tk("w2", b2, j)] + rt, w=[tk("ps", bk)])
            pg.add("act", lambda e, gb=gb, bk=banks[0], NT=NT: e.activation(out=s1b[gb][:, 0:NT], in_=PS[bk][:, 0:NT], func=AF.Sigmoid),
                   r=[tk("ps", banks[0])], w=[tk("s1b", gb)])
            pg.add("act", lambda e, gb=gb, bk=banks[1], NT=NT: e.activation(out=s2b[gb][:, 0:NT], in_=PS[bk][:, 0:NT], func=AF.Sigmoid),
                   r=[tk("ps", banks[1])], w=[tk("s2b", gb)])
            pg.add("dve", lambda e, gb=gb, bk=banks[2], NT=NT: e.tensor_tensor(out=m1b[gb][:, 0:NT], in0=PS[bk][:, 0:NT],
                                                                               in1=s1b[gb][:, 0:NT], op=ALU.mult),
                   r=[tk("ps", banks[2]), tk("s1b", gb)], w=[tk("m1b", gb)])
            pg.add("dve", lambda e, gb=gb, bk=banks[3], NT=NT: e.tensor_tensor(out=m2b[gb][:, 0:NT], in0=PS[bk][:, 0:NT],
                                                                               in1=s2b[gb][:, 0:NT], op=ALU.mult),
                   r=[tk("ps", banks[3]), tk("s2b", gb)], w=[tk("m2b", gb)])
            pg.add("pool", lambda e, gb=gb, nb=nb, t0=t0, NT=NT: e.tensor_tensor(
                out=mT[:, nb, t0:t0 + NT], in0=m1b[gb][:, 0:NT], in1=m2b[gb][:, 0:NT], op=ALU.add),
                r=[tk("m1b", gb), tk("m2b", gb)],
                w=[tk("mT", i) for i in range(t0 // P, (t0 + NT + P - 1) // P)])

    chk(201)
    pg.barrier()
    arena["off"] = mark2
    wout = sb("wout", [P, 8, D], BF16)
    g1b = sb("g1b", [P, D])
    b1b = sb("b1b", [P, D])
    xf = [sb("xf%d" % i, [P, D]) for i in range(3)]
    rb = [sb("rb%d" % i, [P, D]) for i in range(3)]
    hb = [sb("hb%d" % i, [P, D], BF16) for i in range(3)]

    pg.add("pool", lambda e: e.dma_start(out=wout[:], in_=w_out.rearrange("(c p) n -> p c n", p=P)),
           w=[tk("wout")], dma="wout")
    pg.add("sp", lambda e: e.dma_start(out=g1b[:], in_=ln1_g.to_broadcast([P, D])), w=[tk("g1b")], dma="g1b")
    pg.add("sp", lambda e: e.dma_start(out=b1b[:], in_=ln1_b.to_broadcast([P, D])), w=[tk("b1b")], dma="b1b")
    pg.add("dve", lambda e: e.tensor_scalar(out=g1b[:], in0=g1b[:], scalar1=ALPHA, scalar2=None, op0=ALU.mult),
           r=[tk("g1b")], w=[tk("g1b")])
    pg.add("dve", lambda e: e.tensor_scalar(out=b1b[:], in0=b1b[:], scalar1=ALPHA, scalar2=None, op0=ALU.mult),
           r=[tk("b1b")], w=[tk("b1b")])
    pg.add("pool", lambda e: e.memset(hT2[:, :, 0:2], 0.0), w=[tk("hTz")])
    pg.add("pool", lambda e: e.memset(hT2[:, :, 2050:2052], 0.0), w=[tk("hTz")])
    pg.add("pool", lambda e: e.memset(hT2[:, :, 2116:2118], 0.0), w=[tk("hTz")])

    def layer_norm_tile(src_ap, bi, gtile, btile, gtok, btok, out_ap, out_tok, src_tok):
        for hf_ in range(2):
            pg.add("dve", lambda e, hf_=hf_: e.bn_stats(out=stt[bi][:, hf_ * 6:(hf_ + 1) * 6], in_=src_ap[:, hf_ * 512:(hf_ + 1) * 512]),
                   r=[src_tok], w=[tk("stt", bi)])
        pg.add("dve", lambda e: e.bn_aggr(out=mv[bi][:, 0:2], in_=stt[bi][:]), r=[tk("stt", bi)], w=[tk("mv", bi)])
        pg.add("act", lambda e: e.activation(out=mv[bi][:, 2:3], in_=mv[bi][:, 1:2], func=AF.Ln, bias=epsc[:, :]),
               r=[tk("mv", bi), tk("epsc")], w=[tk("mv", bi)])
        pg.add("act", lambda e: e.activation(out=mv[bi][:, 2:3], in_=mv[bi][:, 2:3], func=AF.Exp, scale=-0.5),
               r=[tk("mv", bi)], w=[tk("mv", bi)])
        pg.add("dve", lambda e: e.tensor_scalar(out=src_ap, in0=src_ap, scalar1=mv[bi][:, 0:1], scalar2=mv[bi][:, 2:3],
                                                op0=ALU.subtract, op1=ALU.mult),
               r=[src_tok, tk("mv", bi)], w=[src_tok])
        pg.add("dve", lambda e: e.tensor_tensor(out=src_ap, in0=src_ap, in1=gtile[:], op=ALU.mult),
               r=[src_tok, gtok], w=[src_tok])
        pg.add("pool", lambda e: e.tensor_tensor(out=out_ap, in0=src_ap, in1=btile[:], op=ALU.add),
               r=[src_tok, btok], w=[out_tok])

    for i in range(17):
        bi = i % 3
        src = x_p[i * P:(i + 1) * P, :] if i < 16 else x_s
        pg.add("sp", lambda e, bi=bi, src=src: e.dma_start(out=xf[bi][:], in_=src), w=[tk("xf", bi)], dma=("xf", bi))
        for hf_ in range(2):
            bk = ([3, 4] if i % 2 == 0 else [5, 6])[hf_]
            for c in range(8):
                pg.add("pe", lambda e, bk=bk, c=c, i=i, hf_=hf_: e.matmul(
                    PS[bk][:, :], lhsT=mT[:, c, i * P:(i + 1) * P], rhs=wout[:, c, hf_ * 512:(hf_ + 1) * 512],
                    start=(c == 0), stop=(c == 7)),
                    r=[tk("mT", i), tk("wout")], w=[tk("ps", bk)])
            pg.add("dve", lambda e, bk=bk, bi=bi, hf_=hf_: e.scalar_tensor_tensor(
                out=rb[bi][:, hf_ * 512:(hf_ + 1) * 512], in0=xf[bi][:, hf_ * 512:(hf_ + 1) * 512], scalar=ALPHA,
                in1=PS[bk][:, :], op0=ALU.mult, op1=ALU.add),
                r=[tk("xf", bi), tk("ps", bk)], w=[tk("rb", bi)])
        layer_norm_tile(rb[bi][:], bi, g1b, b1b, tk("g1b"), tk("b1b"), ACCT[:, i, :], tk("acc", i), tk("rb", bi))
        pg.add("act", lambda e, bi=bi, i=i: e.activation(out=hb[bi][:], in_=ACCT[:, i, :], func=AF.Copy, scale=1.0 / ALPHA),
               r=[tk("acc", i)], w=[tk("hb", bi)])
        for c in range(8):
            pg.add("pe", lambda e, bi=bi, c=c: e.transpose(out=PT[:, c * P:(c + 1) * P], in_=hb[bi][:, c * P:(c + 1) * P],
                                                           identity=identb[:]),
                   r=[tk("hb", bi), tk("identb")], w=[tk("pt")])
        if i < 16:
            pg.add("dve", lambda e, i=i: e.tensor_copy(out=hT2[:, :, 2 + i * P:2 + (i + 1) * P],
                                                       in_=PT[:].rearrange("p (a b) -> p a b", b=P)),
                   r=[tk("pt")], w=[tk("hT", i)])
        else:
            pg.add("dve", lambda e: e.tensor_copy(
                out=hT2[:, :, 2052:2184].rearrange("p a (s c) -> p a s c", c=66)[:, :, :, 0:64],
                in_=PT[:].rearrange("p (a s c) -> p a s c", s=2, c=64)),
                r=[tk("pt")], w=[tk("hT", 16)])

    chk(202)
    pg.barrier()
    arena["off"] = mark
    wup = [sb("wup%d" % i, [P, 8, 2, P], BF16) for i in range(2 * JG)]
    wdn = [sb("wdn%d" % i, [P, D], BF16) for i in range(2 * JG)]
    Ab = [sb("A%d" % i, [P, AW], BF16) for i in range(2 * JG)]
    cwork = [[sb("cw%d_%d" % (i, k), [P, 256]) for k in range(5)] for i in range(2)]
    co = sb("co", [P, 3, 2, 44])
    cot = sb("cot", [88, 3, P])
    r_cw = Ring(2)

    cgroups = [(256 * g, 258, 256 * g, "p") for g in range(8)] + [(2050, 132, 2048, "s")]

    def hT_toks(c0, n):
        toks = [tk("hTz")]
        for i in range(17):
            lo = 2 + i * P if i < 16 else 2052
            hi = lo + P if i < 16 else 2184
            if c0 < hi and c0 + n > lo:
                toks.append(tk("hT", i))
        return toks

    njg = (NJ + JG - 1) // JG
    jbufs = {}

    def gen_up(jg):
        js = list(range(jg * JG, min(NJ, (jg + 1) * JG)))
        bufs = []
        jbufs[jg] = bufs
        for jj, j in enumerate(js):
            bidx = (jg % 2) * JG + jj
            bufs.append(bidx)
            for vg in range(2):
                off = vg * DFF + j * P
                pg.add("pool", lambda e, bidx=bidx, vg=vg, off=off: e.dma_start(
                    out=wup[bidx][:, :, vg, :], in_=w_up[:, off:off + P].rearrange("(c p) n -> p c n", p=P)),
                    w=[tk("wup", bidx, vg)], dma=("wup", bidx, vg))
            pg.add("pool", lambda e, bidx=bidx, j=j: e.dma_start(out=wdn[bidx][:], in_=w_down[j * P:(j + 1) * P, :]),
                   w=[tk("wdn", bidx)], dma=("wdn", bidx))

            for (hc0, NC, ac0, kind) in cgroups:
                NO = NC - 2
                ht = hT_toks(hc0, NC)
                cwi = r_cw.next()
                cw = cwork[cwi]
                res = []
                for vg in range(2):
                    bk = [3, 4][vg] if (cwi == 0) else [5, 6][vg]
                    for c in range(8):
                        pg.add("pe", lambda e, bk=bk, bidx=bidx, vg=vg, c=c, hc0=hc0, NC=NC, kind=kind: e.matmul(
                            PS[bk][:, 0:NC], lhsT=wup[bidx][:, c, vg, :], rhs=hT2[:, c, hc0:hc0 + NC],
                            start=(c == 0), stop=(c == 7)),
                            r=[tk("wup", bidx, vg)] + ht, w=[tk("ps", bk)])
                    blk = vg * NJ + j
                    if kind == "s":
                        pg.add("dve", lambda e, bk=bk, blk=blk: e.tensor_copy(
                            out=PS[bk][:, 0:132].rearrange("p (s c) -> p s c", c=66)[:, :, 0:2],
                            in_=cv[:, 4:8, blk].rearrange("p (s r) -> p s r", r=2)),
                            r=[tk("cv")], w=[tk("ps", bk)])
                    t1 = cw[vg * 2]
                    t2 = cw[vg * 2 + 1]
                    pg.add("act", lambda e, bk=bk, t1=t1, blk=blk, NC=NC, NO=NO: e.activation(
                        out=t1[:, 0:NO], in_=PS[bk][:, 2:NC], func=AF.Identity, scale=cv[:, 2, blk:blk + 1],
                        bias=cv[:, 3, blk:blk + 1]),
                        r=[tk("ps", bk), tk("cv")], w=[tk("cw", cwi, vg * 2)])
                    pg.add("dve", lambda e, bk=bk, t1=t1, t2=t2, blk=blk, NC=NC, NO=NO: e.scalar_tensor_tensor(
                        out=t2[:, 0:NO], in0=PS[bk][:, 1:NC - 1], scalar=cv[:, 1, blk:blk + 1], in1=t1[:, 0:NO],
                        op0=ALU.mult, op1=ALU.add),
                        r=[tk("ps", bk), tk("cv"), tk("cw", cwi, vg * 2)], w=[tk("cw", cwi, vg * 2 + 1)])
                    pg.add("dve", lambda e, bk=bk, t1=t1, t2=t2, blk=blk, NC=NC, NO=NO: e.scalar_tensor_tensor(
                        out=t1[:, 0:NO], in0=PS[bk][:, 0:NO], scalar=cv[:, 0, blk:blk + 1], in1=t2[:, 0:NO],
                        op0=ALU.mult, op1=ALU.add),
                        r=[tk("ps", bk), tk("cv"), tk("cw", cwi, vg * 2 + 1)], w=[tk("cw", cwi, vg * 2)])
                    res.append(t1)
                    if kind == "p" and hc0 == 256 * 7:
                        pg.add("act", lambda e, bk=bk, blk=blk, NC=NC: e.copy(out=co[:, 0, :, blk], in_=PS[bk][:, NC - 2:NC]),
                               r=[tk("ps", bk)], w=[tk("co")])
                    if kind == "s":
                        pg.add("act", lambda e, bk=bk, blk=blk: e.copy(
                            out=co[:, 1:3, :, blk], in_=PS[bk][:, 0:132].rearrange("p (s c) -> p s c", c=66)[:, :, 64:66]),
                            r=[tk("ps", bk)], w=[tk("co")])
                sg = cw[4]
                pg.add("act", lambda e, sg=sg, g_=res[1], NO=NO: e.activation(out=sg[:, 0:NO], in_=g_[:, 0:NO], func=AF.Silu),
                       r=[tk("cw", cwi, 2)], w=[tk("cw", cwi, 4)])
                if kind == "p":
                    atoks = [tk("A", bidx, ac0 // P), tk("A", bidx, ac0 // P + 1)]
                else:
                    atoks = [tk("A", bidx, 16)]
                if kind == "p":
                    pg.add("pool", lambda e, bidx=bidx, ac0=ac0, NO=NO, sg=sg, v_=res[0]: e.tensor_tensor(
                        out=Ab[bidx][:, ac0:ac0 + NO], in0=v_[:, 0:NO], in1=sg[:, 0:NO], op=ALU.mult),
                        r=[tk("cw", cwi, 0), tk("cw", cwi, 4)], w=atoks)
                else:
                    pg.add("pool", lambda e, bidx=bidx, sg=sg, v_=res[0]: e.tensor_tensor(
                        out=Ab[bidx][:, 2048:2176].rearrange("p (s c) -> p s c", c=64),
                        in0=v_[:, 0:132].rearrange("p (s c) -> p s c", c=66)[:, :, 0:64],
                        in1=sg[:, 0:132].rearrange("p (s c) -> p s c", c=66)[:, :, 0:64], op=ALU.mult),
                        r=[tk("cw", cwi, 0), tk("cw", cwi, 4)], w=atoks)
                yield

    def gen_down(jg):
        bufs = jbufs[jg]
        for i in range(17):
            for hf_ in range(2):
                bk = [0, 1][hf_]
                for jj, bidx in enumerate(bufs):
                    lhs = Ab[bidx][:, i * P:(i + 1) * P]
                    pg.add("pe", lambda e, bk=bk, lhs=lhs, bidx=bidx, hf_=hf_, jj=jj, n=len(bufs): e.matmul(
                        PS[bk][:, :], lhsT=lhs, rhs=wdn[bidx][:, hf_ * 512:(hf_ + 1) * 512],
                        start=(jj == 0), stop=(jj == n - 1)),
                        r=[tk("A", bidx, i), tk("wdn", bidx)], w=[tk("ps", bk)])
                pg.add("dve", lambda e, bk=bk, i=i, hf_=hf_: e.tensor_tensor(
                    out=ACCT[:, i, hf_ * 512:(hf_ + 1) * 512], in0=PS[bk][:, :], in1=ACCT[:, i, hf_ * 512:(hf_ + 1) * 512],
                    op=ALU.add),
                    r=[tk("ps", bk), tk("acc", i)], w=[tk("acc", i)])
                yield

    run_interleaved([gen_up(0)])
    for jg in range(njg):
        run_interleaved([gen_down(jg), gen_up(jg + 1) if jg + 1 < njg else None])

    chk(203)
    for inst in range(3):
        pg.add("pe", lambda e, inst=inst: e.transpose(out=PS[2][0:88, 0:P], in_=co[:, inst, :, :].rearrange("p a b -> p (a b)"),
                                                      identity=ident[:]),
               r=[tk("co"), tk("ident")], w=[tk("ps", 2)])
        pg.add("dve", lambda e, inst=inst: e.tensor_copy(out=cot[:, inst, :], in_=PS[2][0:88, 0:P]),
               r=[tk("ps", 2)], w=[tk("cot", inst)])
        for r_ in range(2):
            dst = conv_p[r_, :] if inst == 0 else conv_s[inst - 1, r_, :]
            out_events.append(pg.add("sp", lambda e, inst=inst, r_=r_, dst=dst: e.dma_start(
                out=dst.rearrange("(b p) -> b p", p=P), in_=cot[r_ * 44:(r_ + 1) * 44, inst, :]),
                r=[tk("cot", inst)], dma=("cot", inst)))

    g2b = sb("g2b", [P, D])
    b2b = sb("b2b", [P, D])
    yo = [sb("yo%d" % i, [P, D]) for i in range(2)]
    pg.add("sp", lambda e: e.dma_start(out=g2b[:], in_=ln2_g.to_broadcast([P, D])), w=[tk("g2b")], dma="g2b")
    pg.add("sp", lambda e: e.dma_start(out=b2b[:], in_=ln2_b.to_broadcast([P, D])), w=[tk("b2b")], dma="b2b")
    for i in range(17):
        bi = i % 2
        layer_norm_tile(ACCT[:, i, :], bi, g2b, b2b, tk("g2b"), tk("b2b"), yo[bi][:], tk("yo", bi), tk("acc", i))
        dst = y_p[i * P:(i + 1) * P, :] if i < 16 else y_s
        out_events.append(pg.add("sp", lambda e, bi=bi, dst=dst: e.dma_start(out=dst, in_=yo[bi][:]),
                                 r=[tk("yo", bi)], dma=("yo", bi)))

    pg.barrier()


_NC_CACHE = {}


def kernel(x_prompt, x_sample, cache_k, cache_v, state_hgrn, state_ffn_conv, w_in, hg_lb_logits,
           hg_norm_g, da_lambda_q1, da_lambda_k1, da_lambda_q2, da_lambda_k2, da_subln_g,
           w_br_hg, w_br_da, w_out, ln1_g, ln1_b, w_up, conv_w, conv_b, w_down, ln2_g, ln2_b):
    f = lambda a: np.ascontiguousarray(np.asarray(a, dtype=np.float32))
    if "nc" not in _NC_CACHE:
        _NC_CACHE["nc"] = build_nc()
    nc = _NC_CACHE["nc"]
    shared = {
        "w_in": f(w_in[0]),
        "lb_logits": f(hg_lb_logits).reshape(16, 128),
        "hg_norm_g": f(hg_norm_g).reshape(128, 1),
        "lq1": f(da_lambda_q1), "lk1": f(da_lambda_k1), "lq2": f(da_lambda_q2), "lk2": f(da_lambda_k2),
        "subln_g": f(da_subln_g).reshape(128, 1),
        "w_br_hg": f(w_br_hg[0]), "w_br_da": f(w_br_da[0]), "w_out": f(w_out[0]),
        "ln1_g": f(ln1_g), "ln1_b": f(ln1_b),
        "w_up": f(w_up[0]),
        "conv_wb": f(np.concatenate([np.asarray(conv_w[0]), np.asarray(conv_b)], axis=0)),
        "w_down": f(w_down[0]),
        "ln2_g": f(ln2_g), "ln2_b": f(ln2_b),
    }
    in_maps = []
    for c in range(8):
        m = dict(shared)
        m["x_p"] = f(x_prompt[c])
        m["x_s"] = f(x_sample[2 * c:2 * c + 2]).reshape(TS, D)
        m["cache_k"] = f(cache_k[0, 2 * c:2 * c + 2])
        m["cache_v"] = f(cache_v[0, 2 * c:2 * c + 2])
        m["state_hgrn"] = f(state_hgrn[0, 2 * c:2 * c + 2])
        m["state_conv"] = f(state_ffn_conv[0, 2 * c:2 * c + 2]).reshape(4, 2 * DFF)
        in_maps.append(m)
    res = run_bass_kernel_spmd(nc, in_maps, core_ids=list(range(8)))
    R = res.results
    y_prompt = np.stack([R[c]["y_p"] for c in range(8)], axis=0)
    y_sample = np.concatenate([R[c]["y_s"].reshape(2, SQ, D) for c in range(8)], axis=0)
    k_prompt = np.stack([R[c]["k_p"] for c in range(8)], axis=0)[None]
    v_prompt = np.stack([R[c]["v_p"] for c in range(8)], axis=0)[None]
    hgrn_prompt = np.stack([R[c]["hg_p"] for c in range(8)], axis=0)[None]
    conv_prompt = np.stack([R[c]["conv_p"] for c in range(8)], axis=0)[None]
    k_sample = np.concatenate([R[c]["k_s"].reshape(2, SQ, H, P) for c in range(8)], axis=0)[None]
    v_sample = np.concatenate([R[c]["v_s"].reshape(2, SQ, H, P) for c in range(8)], axis=0)[None]
    hgrn_sample = np.concatenate([R[c]["hg_s"] for c in range(8)], axis=0)[None]
    conv_sample = np.concatenate([R[c]["conv_s"] for c in range(8)], axis=0)[None]
    return (y_prompt.astype(np.float32), y_sample.astype(np.float32), k_prompt.astype(np.float32),
            v_prompt.astype(np.float32), hgrn_prompt.astype(np.float32), conv_prompt.astype(np.float32),
            k_sample.astype(np.float32), v_sample.astype(np.float32), hgrn_sample.astype(np.float32),
            conv_sample.astype(np.float32))
```

```python
import math
from collections import defaultdict
from contextlib import ExitStack

import numpy as np
import concourse.bass as bass
import concourse.mybir as mybir
from concourse.bass_utils import run_bass_kernel_spmd

F32 = mybir.dt.float32
BF16 = mybir.dt.bfloat16
ALU = mybir.AluOpType
AF = mybir.ActivationFunctionType
AX = mybir.AxisListType

P = 128
D = 1024
T = 2048
SQ = 64
TS = 2 * SQ
NTOK = T + TS
H = 8
DFF = 2816
NJ = DFF // P
PAST = 2048
LN_EPS = 1e-5
ALPHA = 2.0 ** 0.25
LAM_INIT = 0.8 - 0.6 * math.exp(-0.3 * 0)
DA_SCALE = 64 ** -0.5
HTW = 2184
AW = 2180
JG = 3

SAME_ENGINE_SYNC = True
EXCL_PSUM = True


class Tok:
    __slots__ = ("w", "r", "excl")

    def __init__(self):
        self.w = None
        self.r = {}
        self.excl = False


class _Rec:
    def __getattr__(self, name):
        return lambda *a, **k: (name, a, k)


_REC = _Rec()


class Op:
    __slots__ = ("fn", "deps", "inc", "dma")

    def __init__(self, fn, deps, dma):
        self.fn = fn(_REC) if fn is not None else None
        self.deps = deps
        self.inc = False
        self.dma = dma


class Prog:
    ENGS = ("pe", "act", "dve", "pool", "sp")

    def __init__(self):
        self.ops = {e: [] for e in self.ENGS}
        self.slots = {}
        self.tk = defaultdict(Tok)

    def t(self, *key):
        t = self.tk[key]
        if key[0] in ("ps", "pt"):
            t.excl = True
        return t

    def add(self, eng, fn, r=(), w=(), dma=None):
        if EXCL_PSUM:
            xr = [t for t in r if t.excl]
            if xr:
                r = [t for t in r if not t.excl]
                w = list(w) + [t for t in xr if t not in w]
        ops = self.ops[eng]
        idx = len(ops)
        deps = {}

        def need(k, v):
            if k[0] == "e" and k[1] == eng and (eng == "pe" or not SAME_ENGINE_SYNC) and dma is None:
                return
            if deps.get(k, -1) < v:
                deps[k] = v

        for t in r:
            if t.w is not None:
                need(*t.w)
        for t in w:
            if t.w is not None:
                need(*t.w)
            for k, v in t.r.items():
                need(k, v)
        for k, v in deps.items():
            if k[0] == "e":
                self.ops[k[1]][v].inc = True
        if dma is not None:
            cnt = self.slots.get(dma, 0) + 16
            self.slots[dma] = cnt
            ev = (("d", dma), cnt)
        else:
            ev = (("e", eng), idx)
        for t in w:
            t.w = ev
            t.r = {}
        for t in r:
            if t.w is not ev:
                if t.r.get(ev[0], -1) < ev[1]:
                    t.r[ev[0]] = ev[1]
        ops.append(Op(fn, deps, dma))
        return ev

    def barrier(self):
        last = {}
        for e in self.ENGS:
            for idx in range(len(self.ops[e]) - 1, -1, -1):
                op = self.ops[e][idx]
                if op.fn is not None and op.dma is None:
                    last[e] = idx
                    break
        for e in self.ENGS:
            deps = {}
            for e2, idx in last.items():
                if e2 != e:
                    deps[("e", e2)] = idx
                    self.ops[e2][idx].inc = True
            for k, cnt in self.slots.items():
                deps[("d", k)] = cnt
            self.ops[e].append(Op(None, deps, None))
        for t in self.tk.values():
            t.w = None
            t.r = {}

    def emit(self, nc, st):
        sems = {e: st.enter_context(nc.semaphore("s_" + e)) for e in self.ENGS}
        dsem = {k: st.enter_context(nc.semaphore("d%d" % i)) for i, k in enumerate(self.slots)}
        incval = {}
        for e in self.ENGS:
            c = 0
            vals = []
            for op in self.ops[e]:
                if op.inc:
                    c += 1
                vals.append(c)
            incval[e] = vals
        block = st.enter_context(nc.Block())

        def run(e, engine):
            waited = {}
            for op in self.ops[e]:
                for k, v in op.deps.items():
                    if k[0] == "e":
                        sem = sems[k[1]]
                        val = incval[k[1]][v]
                    else:
                        sem = dsem[k[1]]
                        val = v
                    if waited.get(k, 0) < val:
                        engine.wait_ge(sem, val)
                        waited[k] = val
                if op.fn is None:
                    continue
                ins = getattr(engine, op.fn[0])(*op.fn[1], **op.fn[2])
                if op.dma is not None:
                    ins.then_inc(dsem[op.dma], 16)
                elif op.inc:
                    ins.then_inc(sems[e], 1)

        @block.tensor
        def _(eng):
            run("pe", eng)

        @block.scalar
        def _(eng):
            run("act", eng)

        @block.vector
        def _(eng):
            run("dve", eng)

        @block.gpsimd
        def _(eng):
            run("pool", eng)

        @block.sync
        def _(eng):
            run("sp", eng)


class _Stop(Exception):
    pass


STOP = None
EXPT = 0


def build_nc():
    nc = bass.Bass("TRN2", target_bir_lowering=False, dynamic_dma_scratch_size=12288)
    pg = Prog()
    tk = pg.t
    st = ExitStack()

    def chk(n):
        if STOP is not None and STOP == n:
            raise _Stop()

    try:
        _build_body(nc, pg, tk, st, chk)
    except _Stop:
        pg.barrier()
    pg.emit(nc, st)
    st.close()
    return nc


def _build_body(nc, pg, tk, st, chk):

    def din(name, shape):
        return nc.dram_tensor(name, list(shape), F32, kind="ExternalInput").ap()

    def dout(name, shape):
        return nc.dram_tensor(name, list(shape), F32, kind="ExternalOutput").ap()

    x_p = din("x_p", [T, D])
    x_s = din("x_s", [TS, D])
    cache_k = din("cache_k", [2, PAST, H, P])
    cache_v = din("cache_v", [2, PAST, H, P])
    state_hgrn = din("state_hgrn", [2, H, P, P])
    state_conv = din("state_conv", [4, 2 * DFF])
    w_in = din("w_in", [D, 9 * D])
    lb_logits = din("lb_logits", [16, P])
    hg_norm_g = din("hg_norm_g", [P, 1])
    lq1 = din("lq1", [1, 64])
    lk1 = din("lk1", [1, 64])
    lq2 = din("lq2", [1, 64])
    lk2 = din("lk2", [1, 64])
    subln_g = din("subln_g", [P, 1])
    w_br_hg = din("w_br_hg", [D, D])
    w_br_da = din("w_br_da", [D, D])
    w_out = din("w_out", [D, D])
    ln1_g = din("ln1_g", [1, D])
    ln1_b = din("ln1_b", [1, D])
    w_up = din("w_up", [D, 2 * DFF])
    conv_wb = din("conv_wb", [4, 2 * DFF])
    w_down = din("w_down", [DFF, D])
    ln2_g = din("ln2_g", [1, D])
    ln2_b = din("ln2_b", [1, D])

    y_p = dout("y_p", [T, D])
    y_s = dout("y_s", [TS, D])
    k_p = dout("k_p", [T, H, P])
    v_p = dout("v_p", [T, H, P])
    hg_p = dout("hg_p", [H, P, P])
    conv_p = dout("conv_p", [2, 2 * DFF])
    k_s = dout("k_s", [TS, H, P])
    v_s = dout("v_s", [TS, H, P])
    hg_s = dout("hg_s", [2, H, P, P])
    conv_s = dout("conv_s", [2, 2, 2 * DFF])

    MAIN_BYTES = 216896
    MAIN = st.enter_context(nc.sbuf_tensor("main", [P, MAIN_BYTES // 4], F32))
    arena = {"off": 0, "peak": 0}

    def sb(name, shape, dt=F32):
        n = 1
        for s_ in shape[1:]:
            n *= s_
        nbytes = n * (4 if dt is F32 else 2)
        nb_al = (nbytes + 31) // 32 * 32
        off = arena["off"]
        assert off + nb_al <= MAIN_BYTES, ("SBUF arena overflow", name, off, nb_al)
        arena["off"] = off + nb_al
        arena["peak"] = max(arena["peak"], arena["off"])
        v = MAIN[:, off // 4:(off + nb_al) // 4]
        if dt is not F32:
            v = v.bitcast(dt)
        v = v[0:shape[0], 0:n]
        if len(shape) == 3:
            v = v.rearrange("p (a b) -> p a b", b=shape[2])
        elif len(shape) == 4:
            v = v.rearrange("p (a b c) -> p a b c", b=shape[2], c=shape[3])
        return v

    def ps(name, shape, dt=F32):
        return st.enter_context(nc.psum_tensor(name, list(shape), dt))

    out_events = []
    scratch = {"off": 160000}

    def sbs(name, shape, dt=F32):
        save = arena["off"]
        arena["off"] = scratch["off"]
        v = sb(name, shape, dt)
        scratch["off"] = arena["off"]
        arena["off"] = save
        return v

    PS = [ps("ps%d" % i, [P, 512], F32) for i in range(7)]
    PT = ps("pt", [P, 1024], BF16)

    ones = sbs("ones", [P, 132])
    ident = sb("ident", [P, P])
    identb = sb("identb", [P, P], BF16)
    hmask = sb("hmask", [P, P])
    rmask = sb("rmask", [P, 512], BF16)
    pg.add("pool", lambda e: e.memset(ones[:], 1.0), w=[tk("ones")])
    pg.add("pool", lambda e: e.affine_select(out=ident[:], in_=ones[:, 0:P], pattern=[[-1, P]],
                                              compare_op=ALU.is_equal, fill=0.0, base=0,
                                              channel_multiplier=1),
           r=[tk("ones")], w=[tk("ident")])
    pg.add("dve", lambda e: e.tensor_copy(out=identb[:], in_=ident[:]), r=[tk("ident")], w=[tk("identb")])
    pg.add("pool", lambda e: e.affine_select(out=hmask[:], in_=ones[:, 0:P], pattern=[[1, P]],
                                              compare_op=ALU.is_ge, fill=0.0, base=0,
                                              channel_multiplier=-1),
           r=[tk("ones")], w=[tk("hmask")])
    pg.add("pool", lambda e: e.memset(hmask[0:64, 64:128], 0.0), w=[tk("hmask")])
    pg.add("pool", lambda e: e.memset(rmask[:], 1.0), w=[tk("rmask")])
    pg.add("pool", lambda e: e.memset(rmask[:].rearrange("p (a b) -> p a b", b=64)[:, :, 0:1], 0.0),
           w=[tk("rmask")])

    lbrows = sbs("lbrows", [16, P])
    lgc = sbs("lgc", [P, 16])
    lbv = sb("lbv", [P, H])
    omlv = sb("omlv", [P, H])
    tmp8 = sbs("tmp8", [P, H])
    pg.add("sp", lambda e: e.dma_start(out=lbrows[:], in_=lb_logits), w=[tk("lbrows")], dma="lbrows")
    pg.add("pe", lambda e: e.transpose(out=PS[0][:, 0:16], in_=lbrows[:], identity=ident[0:16, 0:16]),
           r=[tk("lbrows"), tk("ident")], w=[tk("ps", 0)])
    pg.add("dve", lambda e: e.tensor_copy(out=lgc[:], in_=PS[0][:, 0:16]), r=[tk("ps", 0)], w=[tk("lgc")])
    pg.add("dve", lambda e: e.tensor_tensor(out=tmp8[:], in0=lgc[:, 8:16], in1=lgc[:, 0:8], op=ALU.subtract),
           r=[tk("lgc")], w=[tk("tmp8")])
    pg.add("act", lambda e: e.activation(out=tmp8[:], in_=tmp8[:], func=AF.Exp), r=[tk("tmp8")], w=[tk("tmp8")])
    pg.add("dve", lambda e: e.tensor_scalar(out=tmp8[:], in0=tmp8[:], scalar1=1.0, scalar2=None, op0=ALU.add),
           r=[tk("tmp8")], w=[tk("tmp8")])
    pg.add("dve", lambda e: e.reciprocal(out=lbv[:], in_=tmp8[:]), r=[tk("tmp8")], w=[tk("lbv")])
    pg.add("dve", lambda e: e.tensor_scalar(out=omlv[:], in0=lbv[:], scalar1=-1.0, scalar2=1.0,
                                             op0=ALU.mult, op1=ALU.add),
           r=[tk("lbv")], w=[tk("omlv")])

    lamin = sbs("lamin", [P, 4, 64])
    lamt = sbs("lamt", [P, 2, 64])
    lams = sbs("lams", [P, 2])
    neglam = sb("neglam", [P, 1])
    for i, src in enumerate((lq1, lk1, lq2, lk2)):
        pg.add("sp", lambda e, i=i, src=src: e.dma_start(out=lamin[:, i, :], in_=src.to_broadcast([P, 64])),
               w=[tk("lamin", i)], dma=("lamin", i))
    pg.add("dve", lambda e: e.tensor_tensor(out=lamt[:], in0=lamin[:, 0::2, :], in1=lamin[:, 1::2, :], op=ALU.mult),
           r=[tk("lamin", i) for i in range(4)], w=[tk("lamt")])
    pg.add("dve", lambda e: e.tensor_reduce(out=lams[:], in_=lamt[:], axis=AX.X, op=ALU.add),
           r=[tk("lamt")], w=[tk("lams")])
    pg.add("act", lambda e: e.activation(out=lams[:], in_=lams[:], func=AF.Exp), r=[tk("lams")], w=[tk("lams")])
    pg.add("dve", lambda e: e.tensor_tensor(out=neglam[:], in0=lams[:, 1:2], in1=lams[:, 0:1], op=ALU.subtract),
           r=[tk("lams")], w=[tk("neglam")])
    pg.add("dve", lambda e: e.tensor_scalar(out=neglam[:], in0=neglam[:], scalar1=-LAM_INIT, scalar2=None, op0=ALU.add),
           r=[tk("neglam")], w=[tk("neglam")])

    hgg = sb("hgg", [P, 1])
    sgg = sb("sgg", [P, 1])
    pg.add("sp", lambda e: e.dma_start(out=hgg[:], in_=hg_norm_g), w=[tk("hgg")], dma="hgg")
    pg.add("sp", lambda e: e.dma_start(out=sgg[:], in_=subln_g), w=[tk("sgg")], dma="sgg")
    pg.add("dve", lambda e: e.tensor_scalar(out=sgg[:], in0=sgg[:], scalar1=1.0 - LAM_INIT, scalar2=None, op0=ALU.mult),
           r=[tk("sgg")], w=[tk("sgg")])

    cvrows = sbs("cvrows", [44, 8, P])
    cv = sb("cv", [P, 8, 44])
    for r_ in range(8):
        srow = conv_wb[r_, :] if r_ < 4 else state_conv[r_ - 4, :]
        pg.add("sp", lambda e, r_=r_, srow=srow: e.dma_start(out=cvrows[:, r_, :],
                                                   in_=srow.rearrange("(b p) -> b p", p=P)),
               w=[tk("cvrows", r_)], dma=("cvrows", r_))
        pg.add("pe", lambda e, r_=r_: e.transpose(out=PS[1][:, r_ * 44:(r_ + 1) * 44], in_=cvrows[:, r_, :],
                                                   identity=ident[0:44, 0:44]),
               r=[tk("cvrows", r_), tk("ident")], w=[tk("ps", 1)])
    pg.add("dve", lambda e: e.tensor_copy(out=cv[:].rearrange("p a b -> p (a b)"), in_=PS[1][:, 0:352]),
           r=[tk("ps", 1)], w=[tk("cv")])

    hT2 = sb("hT2", [P, 8, HTW], BF16)
    xT = hT2
    big1_off = arena["off"]
    yT = sb("yT", [P, 16, NTOK], BF16)
    arena["off"] = big1_off
    ACCT = sb("acc", [P, 17, D], F32)
    epsc = sb("epsc", [P, 1])
    stt = [sb("stt%d" % i, [P, 12]) for i in range(3)]
    mv = [sb("mv%d" % i, [P, 4]) for i in range(3)]
    pg.add("pool", lambda e: e.memset(epsc[:], LN_EPS), w=[tk("epsc")])
    mark = arena["off"]

    xb = [sb("xb%d" % i, [P, D], BF16) for i in range(2)]
    for i in range(17):
        src = x_p[i * P:(i + 1) * P, :] if i < 16 else x_s
        b = i % 2
        pg.add("pool", lambda e, b=b, src=src: e.dma_start(out=xb[b][:], in_=src), w=[tk("xb", b)], dma=("xb", b))
        for dc in range(8):
            pg.add("pe", lambda e, b=b, dc=dc: e.transpose(out=PT[:, dc * P:(dc + 1) * P],
                                                            in_=xb[b][:, dc * P:(dc + 1) * P], identity=identb[:]),
                   r=[tk("xb", b), tk("identb")], w=[tk("pt")])
        eng = "act" if i % 2 else "dve"
        if eng == "act":
            pg.add("act", lambda e, i=i: e.copy(out=xT[:, :, i * P:(i + 1) * P],
                                                 in_=PT[:].rearrange("p (a b) -> p a b", b=P)),
                   r=[tk("pt")], w=[tk("xT", i)])
        else:
            pg.add("dve", lambda e, i=i: e.tensor_copy(out=xT[:, :, i * P:(i + 1) * P],
                                                        in_=PT[:].rearrange("p (a b) -> p a b", b=P)),
                   r=[tk("pt")], w=[tk("xT", i)])

    chk(0)
    pg.barrier()
    arena["off"] = mark
    wh = [sb("wh%d" % i, [P, 8, 7 * P], BF16) for i in range(2)]
    WOFF = [0, 1, 4, 5, 2, 3, 6]
    hq_sb = [sb("hq%d" % i, [P, 512]) for i in range(2)]
    sigp = [sb("sigp%d" % i, [P, 512]) for i in range(2)]
    sign = [sb("sign%d" % i, [P, 512]) for i in range(2)]
    fbuf = sigp
    bbuf = [sb("bbuf%d" % i, [P, 512]) for i in range(2)]
    ebuf = [sb("ebuf%d" % i, [P, 512]) for i in range(2)]
    enbuf = bbuf
    qdz = [sb("qdz%d" % i, [P, 4, 3, 64], BF16) for i in range(2)]
    kdT = [sb("kdT%d" % i, [P, 512], BF16) for i in range(2)]
    qT = [sb("qT%d" % i, [P, 2, 512], BF16) for i in range(2)]
    vh = [sb("vh%d" % i, [P, 4, P], BF16) for i in range(2)]
    sog = [sb("sog%d" % i, [P, 4, P]) for i in range(2)]
    tm32 = [sb("tm32_%d" % i, [P, 512]) for i in range(2)]
    qTs = [sb("qTs%d" % i, [P, 512], BF16) for i in range(2)]
    kdtm = [sb("kdtm%d" % i, [P, 2, P], BF16) for i in range(2)]
    atm = [sb("atm%d" % i, [P, P], BF16) for i in range(2)]
    s32 = [sb("s32_%d" % i, [P, P]) for i in range(3)]
    sbf = [sb("sbf%d" % i, [P, P], BF16) for i in range(3)]
    t1b = [sb("t1b%d" % i, [P, P]) for i in range(2)]
    junk = sb("junk", [P, P])
    ssb = [sb("ssb%d" % i, [P, 2]) for i in range(4)]
    yh = [sb("yh%d" % i, [P, P], BF16) for i in range(2)]
    ktp = sb("ktp", [P, T], BF16)
    vp = sb("vp", [P, 16, 136], BF16)
    kts1 = sb("kts", [P, PAST], BF16)
    vs1_ = sb("vs", [P, 16, 136], BF16)
    ktn = [sb("ktn%d" % i, [P, SQ], BF16) for i in range(2)]
    vn = [sb("vn%d" % i, [SQ, 136], BF16) for i in range(2)]
    kc1 = sb("kc", [P, 16, P], BF16)
    kc = [kc1, kc1]
    pT = [sb("pT%d" % i, [P, 2, 256], BF16) for i in range(2)]
    rden = [sb("rden%d" % i, [P, 4]) for i in range(2)]
    t1a = [sb("t1a%d" % i, [P, P]) for i in range(2)]
    abuf = [sb("abuf%d" % i, [P, P]) for i in range(2)]
    ya = [sb("ya%d" % i, [P, P], BF16) for i in range(2)]

    for i in range(2):
        pg.add("pool", lambda e, i=i: e.memset(qT[i][:], 0.0), w=[tk("qT", i)])
        pg.add("pool", lambda e, i=i: e.memset(qdz[i][:, :, 1, :], 0.0), w=[tk("qdz", i)])
        pg.add("pool", lambda e, i=i: e.memset(vn[i][:, 128:130], 1.0), w=[tk("vn1", i)])
    pg.add("pool", lambda e: e.memset(vs1_[:, :, 128:130], 1.0), w=[tk("vs1", 0)])
    pg.add("pool", lambda e: e.memset(vp[:, :, 128:130], 1.0), w=[tk("vp1")])

    class Ring:
        def __init__(self, n):
            self.n = n
            self.i = -1

        def next(self):
            self.i = (self.i + 1) % self.n
            return self.i

    r_kvst = Ring(2)
    r_kdtm = Ring(2)
    r_atm = Ring(2)
    r_s = Ring(3)
    r_t1b = Ring(2)
    r_ss = Ring(4)
    r_yh = Ring(2)
    r_pT = Ring(2)
    r_sc = Ring(2)
    r_rden = Ring(2)
    r_t1a = Ring(2)
    r_ab = Ring(2)
    r_ya = Ring(2)
    r_pa = Ring(2)

    PA = [PS[0], PS[1]]
    PH = PS[2]
    SC = [PS[3], PS[4]]
    AC = [PS[5], PS[6]]
    PAI = [0, 1]
    SCI = [3, 4]
    ACI = [5, 6]

    def evac_eng(i):
        return "act" if i % 2 else "dve"

    groups = []
    for g in range(4):
        groups.append((g * 512, 512, [(g * 512 + k * P, P, "p", None) for k in range(4)]))
    groups.append((T, TS, [(T, SQ, "s", 0), (T + SQ, SQ, "s", 1)]))

    def xtoks(t0, n):
        return [tk("xT", i) for i in range(t0 // P, (t0 + n + P - 1) // P)]

    def run_interleaved(gens):
        gens = [g for g in gens if g is not None]
        while gens:
            for g in list(gens):
                try:
                    next(g)
                except StopIteration:
                    gens.remove(g)

    def rstd_ops(si, R):
        pg.add("act", lambda e: e.activation(out=ssb[si][0:R, 1:2], in_=ssb[si][0:R, 0:1], func=AF.Ln,
                                             scale=1.0 / P, bias=epsc[0:R, :]),
               r=[tk("ssb", si), tk("epsc")], w=[tk("ssb", si)])
        pg.add("act", lambda e: e.activation(out=ssb[si][0:R, 1:2], in_=ssb[si][0:R, 1:2], func=AF.Exp, scale=-0.5),
               r=[tk("ssb", si)], w=[tk("ssb", si)])

    def gen_attend(h, qbuf, qc0, nq, qrows, keytiles, out_cols):
        nqt = nq // qrows
        acc_v = [AC[i][:].rearrange("p (m c) -> p m c", m=2) for i in range(nqt)]
        last_for_qt = {}
        first_for_qt = {}
        for ki, kt in enumerate(keytiles):
            for iq in range(nqt):
                if iq * qrows >= kt[3]:
                    last_for_qt[iq] = ki
                    first_for_qt.setdefault(iq, ki)

        def scores(ki):
            kt_ap, v_ap, ns, q0, diag, rtoks = keytiles[ki]
            sci = r_sc.next()
            scv = SC[sci][:].rearrange("p (m c) -> p m c", m=2)
            for m in range(2):
                pg.add("pe", lambda e, m=m: e.matmul(
                    scv[0:ns, m, q0:nq], lhsT=kt_ap, rhs=qT[qbuf][:, m, qc0 + q0:qc0 + nq],
                    start=(m == 0), stop=(m == 1)),
                    r=rtoks + [tk("qT", qbuf)], w=[tk("ps", SCI[sci])])
            return sci, scv

        nxt = scores(0)
        for ki, (kt_ap, v_ap, ns, q0, diag, rtoks) in enumerate(keytiles):
            sci, scv = nxt
            if ki + 1 < len(keytiles):
                nxt = scores(ki + 1)
            pi = r_pT.next()
            pg.add("act", lambda e: e.activation(out=pT[pi][0:ns, :, q0:nq], in_=scv[0:ns, :, q0:nq], func=AF.Exp),
                   r=[tk("ps", SCI[sci])], w=[tk("pT", pi)])
            if diag is not None:
                pg.add("pool", lambda e: e.memset(pT[pi][64:128, :, diag * P:diag * P + 64], 0.0), w=[tk("pT", pi)])
            yield
            for iq in range(nqt):
                if iq * qrows < q0:
                    continue
                for m in range(2):
                    pg.add("pe", lambda e, iq=iq, m=m: e.matmul(
                        acc_v[iq][0:qrows, m, 0:129], lhsT=pT[pi][0:ns, m, iq * qrows:(iq + 1) * qrows],
                        rhs=v_ap[:, 0:129], start=(ki == first_for_qt[iq] and m == 0),
                        stop=(ki == last_for_qt[iq] and m == 1)),
                        r=rtoks + [tk("pT", pi)], w=[tk("ps", ACI[iq])])
            yield
        for iq in range(nqt):
            yield from attend_finalize(h, acc_v[iq], tk("ps", ACI[iq]), qrows, out_cols[iq])

    def attend_finalize(h, av, atok, R, c0):
        if True:
            ri = r_rden.next()
            pg.add("dve", lambda e: e.reciprocal(out=rden[ri][0:R, 0:2], in_=av[0:R, :, 128]),
                   r=[atok], w=[tk("rden", ri)])
            pg.add("dve", lambda e: e.tensor_tensor(out=rden[ri][0:R, 2:3], in0=rden[ri][0:R, 1:2],
                                                    in1=neglam[0:R, :], op=ALU.mult),
                   r=[tk("rden", ri), tk("neglam")], w=[tk("rden", ri)])
            ti = r_t1a.next()
            pg.add("act", lambda e: e.activation(out=t1a[ti][0:R, :], in_=av[0:R, 1, 0:128], func=AF.Copy,
                                                 scale=rden[ri][0:R, 2:3]),
                   r=[atok, tk("rden", ri)], w=[tk("t1a", ti)])
            ai = r_ab.next()
            pg.add("dve", lambda e: e.scalar_tensor_tensor(
                out=abuf[ai][0:R, :], in0=av[0:R, 0, 0:128], scalar=rden[ri][0:R, 0:1], in1=t1a[ti][0:R, :],
                op0=ALU.mult, op1=ALU.add),
                r=[atok, tk("rden", ri), tk("t1a", ti)], w=[tk("abuf", ai)])
            yield
            si = r_ss.next()
            pg.add("act", lambda e: e.activation(out=junk[0:R, :], in_=abuf[ai][0:R, :], func=AF.Square,
                                                 accum_out=ssb[si][0:R, 0:1]),
                   r=[tk("abuf", ai)], w=[tk("junk"), tk("ssb", si)])
            rstd_ops(si, R)
            yi = r_ya.next()
            pg.add("dve", lambda e: e.tensor_scalar(out=ya[yi][0:R, :], in0=abuf[ai][0:R, :],
                                                    scalar1=ssb[si][0:R, 1:2], scalar2=None, op0=ALU.mult),
                   r=[tk("abuf", ai), tk("ssb", si)], w=[tk("ya", yi)])
            yield
            pg.add("pe", lambda e: e.transpose(out=PT[:, 0:R], in_=ya[yi][0:R, :], identity=identb[0:R, 0:R]),
                   r=[tk("ya", yi), tk("identb")], w=[tk("pt")])
            pg.add("act", lambda e: e.activation(out=yT[:, 8 + h, c0:c0 + R], in_=PT[:, 0:R], func=AF.Copy,
                                                 scale=sgg[:, 0:1]),
                   r=[tk("pt"), tk("sgg")], w=[tk("yT", 8 + h, c0 // P)])
            yield

    def gen_attend_sample(h, qbuf, qc0, keytiles, out_col):
        R = SQ
        av = AC[0][:].rearrange("p (m c) -> p m c", m=2)
        atok = tk("ps", ACI[0])
        steps = [keytiles[i:i + 2] for i in range(0, 16, 2)] + [keytiles[16:17]]
        nsteps = len(steps)

        def scores(si_):
            tiles_ = steps[si_]
            sci = r_sc.next()
            scv = SC[sci][:, 0:256].rearrange("p (a c) -> p a c", c=64)
            for t_, (kt_ap, v_ap, ns, q0, diag, rtoks) in enumerate(tiles_):
                for m in range(2):
                    pg.add("pe", lambda e, t_=t_, m=m, kt_ap=kt_ap, ns=ns: e.matmul(
                        scv[0:ns, t_ * 2 + m, :], lhsT=kt_ap, rhs=qT[qbuf][:, m, qc0:qc0 + R],
                        start=(t_ == 0 and m == 0), stop=(t_ == len(tiles_) - 1 and m == 1)),
                        r=rtoks + [tk("qT", qbuf)], w=[tk("ps", SCI[sci])])
            return sci, scv

        nxt = scores(0)
        for si_, tiles_ in enumerate(steps):
            sci, scv = nxt
            if si_ + 1 < nsteps:
                nxt = scores(si_ + 1)
            ns = tiles_[0][2]
            nb_ = 2 * len(tiles_)
            pi = r_pT.next()
            ptv = pT[pi][:].rearrange("p m c -> p (m c)")[:, 0:256].rearrange("p (a c) -> p a c", c=64)
            pg.add("act", lambda e: e.activation(out=ptv[0:ns, 0:nb_, :], in_=scv[0:ns, 0:nb_, :], func=AF.Exp),
                   r=[tk("ps", SCI[sci])], w=[tk("pT", pi)])
            yield
            for t_, (kt_ap, v_ap, ns_, q0, diag, rtoks) in enumerate(tiles_):
                for m in range(2):
                    first = (si_ == 0 and t_ == 0 and m == 0)
                    last = (si_ == nsteps - 1 and t_ == len(tiles_) - 1 and m == 1)
                    pg.add("pe", lambda e, t_=t_, m=m, v_ap=v_ap, ns_=ns_, first=first, last=last: e.matmul(
                        av[0:R, m, 0:129], lhsT=ptv[0:ns_, t_ * 2 + m, :], rhs=v_ap[:, 0:129], start=first, stop=last),
                        r=rtoks + [tk("pT", pi)], w=[atok])
            yield
        yield from attend_finalize(h, av, atok, R, out_col)

    def load_cache(h, sq):
        pg.add("pool", lambda e: e.dma_start(out=kc1[:], in_=cache_k[sq, :, h, :].rearrange("(j p) e -> p j e", p=P)),
               w=[tk("kc", 0)], dma=("kc", 0))
        pg.add("pool", lambda e: e.dma_start(out=vs1_[:, 0:16, 0:128],
                                             in_=cache_v[sq, :, h, :].rearrange("(j p) e -> p j e", p=P)),
               w=[tk("vs", 0)], dma=("vs", 0))

    def gen_cache_T():
        for half in range(2):
            for jj in range(8):
                j = half * 8 + jj
                pg.add("pe", lambda e, j=j, jj=jj: e.transpose(out=PT[:, jj * P:(jj + 1) * P], in_=kc1[:, j, :],
                                                               identity=identb[:]),
                       r=[tk("kc", 0), tk("identb")], w=[tk("pt")])
            pg.add("dve", lambda e, half=half: e.tensor_copy(out=kts1[:, half * 1024:(half + 1) * 1024], in_=PT[:]),
                   r=[tk("pt")], w=[tk("kts", 0)])
            yield

    def load_weights(h):
        wb = h % 2
        for j in range(7):
            off = WOFF[j] * D + h * P
            pg.add("pool", lambda e, j=j, off=off: e.dma_start(
                out=wh[wb][:, :, j * P:(j + 1) * P], in_=w_in[:, off:off + P].rearrange("(c p) n -> p c n", p=P)),
                w=[tk("wh", wb, j)], dma=("wh", wb, j))

    def gen_inproj(h, gi):
        wb = h % 2
        W = wh[wb]
        gb = (5 * h + gi) % 2
        t0, NT, tiles = groups[gi]
        xt = xtoks(t0, NT)
        for j in range(4):
            pi = r_pa.next()
            ptok = tk("ps", PAI[pi])
            for dc in range(8):
                pg.add("pe", lambda e, dc=dc: e.matmul(
                    PA[pi][:, 0:NT], lhsT=W[:, dc, j * P:(j + 1) * P], rhs=xT[:, dc, t0:t0 + NT],
                    start=(dc == 0), stop=(dc == 7)),
                    r=[tk("wh", wb, j)] + xt, w=[ptok])
                if dc == 99:
                    yield
            src = PA[pi][:, 0:NT]
            if j == 0:
                pg.add("dve", lambda e: e.tensor_copy(out=hq_sb[gb][:, 0:NT], in_=src), r=[ptok], w=[tk("hq", gb)])
            elif j == 1:
                pg.add("act", lambda e: e.activation(out=sign[gb][:, 0:NT], in_=src, func=AF.Exp, scale=-1.0),
                       r=[ptok], w=[tk("sign", gb)])
            elif j == 2:
                pg.add("act", lambda e: e.activation(out=qTs[gb][:, 0:NT], in_=src, func=AF.Copy, scale=DA_SCALE),
                       r=[ptok], w=[tk("qTs", gb)])
                for m in range(2):
                    pg.add("pool", lambda e, m=m: e.tensor_copy(
                        out=qT[gb][m * 64:(m + 1) * 64, m, 0:NT], in_=qTs[gb][m * 64:(m + 1) * 64, 0:NT]),
                        r=[tk("qTs", gb)], w=[tk("qT", gb)])
            else:
                if gi < 4:
                    pg.add("dve", lambda e: e.tensor_copy(out=ktp[:, t0:t0 + NT], in_=src), r=[ptok], w=[tk("ktp", gi)])
                else:
                    for sq in range(2):
                        pg.add("dve", lambda e, sq=sq: e.tensor_copy(out=ktn[sq][:, :], in_=PA[pi][:, sq * SQ:(sq + 1) * SQ]),
                               r=[ptok], w=[tk("ktn", sq)])
            yield
        sp_, sn_ = sigp[gb][:, 0:NT], sign[gb][:, 0:NT]
        pg.add("dve", lambda e: e.tensor_scalar(out=sp_, in0=sn_, scalar1=1.0, scalar2=None, op0=ALU.add),
               r=[tk("sign", gb)], w=[tk("sigp", gb)])
        for q_ in range(0, NT, 128):
            pg.add("dve", lambda e, q_=q_: e.reciprocal(out=sigp[gb][:, q_:q_ + 128], in_=sigp[gb][:, q_:q_ + 128]),
                   r=[tk("sigp", gb)], w=[tk("sigp", gb)])
            yield
        pg.add("dve", lambda e: e.tensor_tensor(out=sn_, in0=sn_, in1=sp_, op=ALU.mult),
               r=[tk("sign", gb), tk("sigp", gb)], w=[tk("sign", gb)])
        pg.add("dve", lambda e: e.tensor_scalar(out=sp_, in0=sp_, scalar1=omlv[:, h:h + 1], scalar2=lbv[:, h:h + 1],
                                                op0=ALU.mult, op1=ALU.add),
               r=[tk("sigp", gb), tk("omlv"), tk("lbv")], w=[tk("sigp", gb)])
        yield
        pg.add("act", lambda e: e.activation(out=sp_, in_=sp_, func=AF.Ln), r=[tk("sigp", gb)], w=[tk("sigp", gb)])
        pg.add("dve", lambda e: e.tensor_tensor_scan(out=bbuf[gb][:, 0:NT], data0=rmask[:, 0:NT], data1=sp_, initial=0.0,
                                                     op0=ALU.mult, op1=ALU.add),
               r=[tk("sigp", gb), tk("rmask")], w=[tk("bbuf", gb)])
        pg.add("act", lambda e: e.activation(out=ebuf[gb][:, 0:NT], in_=bbuf[gb][:, 0:NT], func=AF.Exp),
               r=[tk("bbuf", gb)], w=[tk("ebuf", gb)])
        pg.add("act", lambda e: e.activation(out=bbuf[gb][:, 0:NT], in_=bbuf[gb][:, 0:NT], func=AF.Exp, scale=-1.0),
               r=[tk("bbuf", gb)], w=[tk("bbuf", gb)])
        yield
        nch_g = NT // 64
        pg.add("dve", lambda e: e.tensor_tensor(
            out=qdz[gb][:, 0:nch_g // 2, 0::2, :],
            in0=hq_sb[gb][:, 0:NT].rearrange("p (a b c) -> p a b c", b=2, c=64),
            in1=ebuf[gb][:, 0:NT].rearrange("p (a b c) -> p a b c", b=2, c=64), op=ALU.mult),
            r=[tk("hq", gb), tk("ebuf", gb)], w=[tk("qdz", gb)])
        pg.add("dve", lambda e: e.scalar_tensor_tensor(
            out=kdT[gb][:, 0:NT], in0=sn_, scalar=omlv[:, h:h + 1], in1=bbuf[gb][:, 0:NT],
            op0=ALU.mult, op1=ALU.mult),
            r=[tk("sign", gb), tk("omlv"), tk("bbuf", gb)], w=[tk("kdT", gb)])
        yield
        for li, (tt0, R, kind, sq) in enumerate(tiles):
            pi = r_pa.next()
            ptok = tk("ps", PAI[pi])
            for dc in range(8):
                pg.add("pe", lambda e, dc=dc: e.matmul(
                    PA[pi][0:R, :], lhsT=xT[:, dc, tt0:tt0 + R], rhs=W[:, dc, 3 * P:7 * P],
                    start=(dc == 0), stop=(dc == 7)),
                    r=[tk("wh", wb, j) for j in (3, 4, 5, 6)] + xtoks(tt0, R), w=[ptok])
                if dc == 99:
                    yield
            ki = r_kvst.next()
            ttok = tk("tm32", ki)
            tmv = tm32[ki][:].rearrange("p (a b) -> p a b", b=P)
            pg.add("act", lambda e: e.copy(out=tm32[ki][0:R, :], in_=PA[pi][0:R, :]), r=[ptok], w=[ttok])
            if kind == "p":
                kd, vd = k_p[tt0:tt0 + R, h, :], v_p[tt0:tt0 + R, h, :]
            else:
                kd, vd = k_s[sq * SQ:(sq + 1) * SQ, h, :], v_s[sq * SQ:(sq + 1) * SQ, h, :]
            pg.add("sp", lambda e: e.dma_start(out=kd, in_=tmv[0:R, 0, :]), r=[ttok], dma=("tm32", ki))
            pg.add("sp", lambda e: e.dma_start(out=vd, in_=tmv[0:R, 3, :]), r=[ttok], dma=("tm32", ki))
            pg.add("pool", lambda e: e.tensor_copy(out=vh[gb][0:R, li, :], in_=tmv[0:R, 1, :]), r=[ttok], w=[tk("vh", gb, li)])
            pg.add("act", lambda e: e.activation(out=sog[gb][0:R, li, :], in_=tmv[0:R, 2, :], func=AF.Exp, scale=-1.0),
                   r=[ttok], w=[tk("sog", gb, li)])
            if kind == "p":
                ti = tt0 // P
                pg.add("pool", lambda e: e.tensor_copy(out=vp[:, ti, 0:128], in_=tmv[:, 3, :]), r=[ttok], w=[tk("vp", ti)])
            else:
                pg.add("pool", lambda e: e.tensor_copy(out=vn[sq][0:SQ, 0:128], in_=tmv[0:SQ, 3, :]), r=[ttok], w=[tk("vsn", sq)])
            so_ = sog[gb][0:R, li, :]
            pg.add("dve", lambda e: e.tensor_scalar(out=so_, in0=so_, scalar1=1.0, scalar2=None, op0=ALU.add),
                   r=[tk("sog", gb, li)], w=[tk("sog", gb, li)])
            pg.add("dve", lambda e: e.reciprocal(out=so_, in_=so_), r=[tk("sog", gb, li)], w=[tk("sog", gb, li)])
            yield

    hstate = {"s_cur": None}

    def gen_hgrn(h, gi):
        gb = (5 * h + gi) % 2
        t0, NT, tiles = groups[gi]
        phv = PH[:].rearrange("p (a b) -> p a b", b=P)
        ptok = tk("ps", 2)
        for li, (tt0, R, kind, sq) in enumerate(tiles):
            nch = R // 64
            c0 = tt0 - t0
            lt = c0 // P
            if kind == "p" and tt0 == 0:
                s_cur = r_s.next()
                pg.add("pool", lambda e: e.memset(s32[s_cur][:], 0.0), w=[tk("s32", s_cur)])
                pg.add("pool", lambda e: e.memset(sbf[s_cur][:], 0.0), w=[tk("sbf", s_cur)])
                hstate["s_cur"] = s_cur
            if kind == "s":
                s_cur = r_s.next()
                pg.add("sp", lambda e: e.dma_start(out=s32[s_cur][:], in_=state_hgrn[sq, h, :, :]),
                       w=[tk("s32", s_cur)], dma=("s32", s_cur))
                pg.add("pool", lambda e: e.tensor_copy(out=sbf[s_cur][:], in_=s32[s_cur][:]),
                       r=[tk("s32", s_cur)], w=[tk("sbf", s_cur)])
                hstate["s_cur"] = s_cur
            s_cur = hstate["s_cur"]
            pg.add("pe", lambda e: e.transpose(out=PT[0:R, 0:P], in_=kdT[gb][:, c0:c0 + R], identity=identb[:]),
                   r=[tk("kdT", gb), tk("identb")], w=[tk("pt")])
            kmi = r_kdtm.next()
            for c in range(nch):
                pg.add("dve", lambda e, c=c: e.tensor_copy(out=kdtm[kmi][c * 64:(c + 1) * 64, c, :],
                                                           in_=PT[c * 64:(c + 1) * 64, 0:P]),
                       r=[tk("pt")], w=[tk("kdtm", kmi)])
            if nch == 2:
                qd_mov = qdz[gb][:, lt, 0::2, :]
                att_out = phv[0:R, 0, 0:R].rearrange("p (a b) -> p a b", b=64)
            else:
                qd_mov = qdz[gb][:, 0, 2 * sq, :]
                att_out = phv[0:R, 0, 0:R]
            pg.add("pe", lambda e: e.matmul(att_out, lhsT=kdT[gb][:, c0:c0 + R], rhs=qd_mov, start=True, stop=True),
                   r=[tk("kdT", gb), tk("qdz", gb)], w=[ptok])
            ai = r_atm.next()
            pg.add("dve", lambda e: e.tensor_tensor(out=atm[ai][0:R, 0:R], in0=phv[0:R, 0, 0:R], in1=hmask[0:R, 0:R],
                                                    op=ALU.mult),
                   r=[ptok, tk("hmask")], w=[tk("atm", ai)])
            yield
            for c in range(nch):
                if nch == 2:
                    lhs = kdtm[kmi][:, c, :]
                    rhs = vh[gb][:, li, :]
                else:
                    lhs = kdtm[kmi][0:64, 0, :]
                    rhs = vh[gb][0:64, li, :]
                pg.add("pe", lambda e, c=c, lhs=lhs, rhs=rhs: e.matmul(phv[:, 1 + c, :], lhsT=lhs, rhs=rhs,
                                                                     start=True, stop=True),
                       r=[tk("kdtm", kmi), tk("vh", gb, li)], w=[ptok])
            s_before = [s_cur]
            for c in range(nch):
                ti = r_t1b.next()
                sc_ = s_cur
                pg.add("dve", lambda e, c=c, ti=ti, sc_=sc_: e.tensor_tensor(out=t1b[ti][:], in0=phv[:, 1 + c, :],
                                                                           in1=s32[sc_][:], op=ALU.add),
                       r=[ptok, tk("s32", sc_)], w=[tk("t1b", ti)])
                s_new = r_s.next()
                acol = c0 + c * 64 + 63
                pg.add("dve", lambda e, ti=ti, s_new=s_new, acol=acol: e.tensor_scalar(
                    out=s32[s_new][:], in0=t1b[ti][:], scalar1=ebuf[gb][:, acol:acol + 1], scalar2=None, op0=ALU.mult),
                    r=[tk("t1b", ti), tk("ebuf", gb)], w=[tk("s32", s_new)])
                pg.add("pool", lambda e, ti=ti, s_new=s_new, acol=acol: e.tensor_scalar(
                    out=sbf[s_new][:], in0=t1b[ti][:], scalar1=ebuf[gb][:, acol:acol + 1], scalar2=1.0,
                    op0=ALU.mult, op1=ALU.mult),
                    r=[tk("t1b", ti), tk("ebuf", gb)], w=[tk("sbf", s_new)])
                s_cur = s_new
                s_before.append(s_cur)
            hstate["s_cur"] = s_cur
            if kind == "p" and tt0 == T - P:
                pg.add("sp", lambda e: e.dma_start(out=hg_p[h, :, :], in_=s32[s_cur][:]),
                       r=[tk("s32", s_cur)], dma=("s32", s_cur))
            if kind == "s":
                pg.add("sp", lambda e: e.dma_start(out=hg_s[sq, h, :, :], in_=s32[s_cur][:]),
                       r=[tk("s32", s_cur)], dma=("s32", s_cur))
            yield
            pg.add("pe", lambda e: e.matmul(phv[0:R, 3, :], lhsT=atm[ai][0:R, 0:R], rhs=vh[gb][0:R, li, :],
                                            start=True, stop=False),
                   r=[tk("atm", ai), tk("vh", gb, li)], w=[ptok])
            for c in range(nch):
                if nch == 2:
                    lhs = qdz[gb][:, lt, :, :].rearrange("p a b -> p (a b)")[:, c * 64:c * 64 + P]
                else:
                    lhs = qdz[gb][:, 0, 2 * sq, :]
                sb_i = s_before[c]
                pg.add("pe", lambda e, lhs=lhs, sb_i=sb_i, c=c: e.matmul(
                    phv[0:R, 3, :], lhsT=lhs, rhs=sbf[sb_i][:], start=False, stop=(c == nch - 1)),
                    r=[tk("qdz", gb), tk("sbf", sb_i)], w=[ptok])
            si = r_ss.next()
            pg.add("act", lambda e: e.activation(out=junk[0:R, :], in_=phv[0:R, 3, :], func=AF.Square,
                                                 accum_out=ssb[si][0:R, 0:1]),
                   r=[ptok], w=[tk("junk"), tk("ssb", si)])
            rstd_ops(si, R)
            yield
            yi = r_yh.next()
            pg.add("dve", lambda e: e.scalar_tensor_tensor(
                out=yh[yi][0:R, :], in0=phv[0:R, 3, :], scalar=ssb[si][0:R, 1:2], in1=sog[gb][0:R, li, :],
                op0=ALU.mult, op1=ALU.mult),
                r=[ptok, tk("ssb", si), tk("sog", gb, li)], w=[tk("yh", yi)])
            yield
            pg.add("pe", lambda e: e.transpose(out=PT[:, 0:R], in_=yh[yi][0:R, :], identity=identb[0:R, 0:R]),
                   r=[tk("yh", yi), tk("identb")], w=[tk("pt")])
            pg.add("act", lambda e: e.activation(out=yT[:, h, tt0:tt0 + R], in_=PT[:, 0:R], func=AF.Copy,
                                                 scale=hgg[:, 0:1]),
                   r=[tk("pt"), tk("hgg")], w=[tk("yT", h, tt0 // P)])
            yield

    def gen_attn(h, gi):
        gb = (5 * h + gi) % 2
        t0, NT, tiles = groups[gi]
        if gi < 4:
            for qg in range(2):
                qt0 = t0 + qg * 256
                nkt = qt0 // P + 2
                keytiles = []
                for j in range(nkt):
                    if j < qt0 // P:
                        q0, diag = 0, None
                    else:
                        q0, diag = (j - qt0 // P) * P, j - qt0 // P
                    keytiles.append((ktp[:, j * P:(j + 1) * P], vp[:, j, :], P, q0, diag,
                                     [tk("ktp", j // 4), tk("vp", j), tk("vp1")]))
                yield from gen_attend(h, gb, qg * 256, 256, P, keytiles, [qt0, qt0 + P])
        else:
            for sq in range(2):
                if sq == 1:
                    load_cache(h, 1)
                yield from gen_cache_T()
                keytiles = []
                for j in range(16):
                    keytiles.append((kts1[:, j * P:(j + 1) * P], vs1_[:, j, :], P, 0, None,
                                     [tk("kts", 0), tk("vs", 0), tk("vs1", 0)]))
                keytiles.append((ktn[sq][:, :], vn[sq][0:SQ, :], SQ, 0, None,
                                 [tk("ktn", sq), tk("vsn", sq), tk("vn1", sq)]))
                yield from gen_attend_sample(h, gb, sq * SQ, keytiles, T + sq * SQ)

    for i in range(2):
        pg.add("pool", lambda e, i=i: e.memset(kdtm[i][:], 0.0), w=[tk("kdtm", i)])
    load_weights(0)
    tail = []
    for h in range(H):
        if h + 1 < H:
            load_weights(h + 1)
        run_interleaved(tail + [gen_inproj(h, 0)])
        load_cache(h, 0)
        for gi in range(4):
            run_interleaved([gen_attn(h, gi), gen_hgrn(h, gi), gen_inproj(h, gi + 1)])
        tail = [gen_attn(h, 4), gen_hgrn(h, 4)]
    run_interleaved(tail)
    pg.barrier()
    chk(200)

    arena["off"] = mark
    mT = sb("mT", [P, 8, NTOK], BF16)
    mark2 = arena["off"]
    w2 = [sb("w2_%d" % i, [P, 8, 4, P], BF16) for i in range(2)]
    s1b = [sb("s1b%d" % i, [P, 512]) for i in range(2)]
    s2b = [sb("s2b%d" % i, [P, 512]) for i in range(2)]
    m1b = [sb("m1b%d" % i, [P, 512]) for i in range(2)]
    m2b = [sb("m2b%d" % i, [P, 512]) for i in range(2)]
    bankset = [[0, 1, 3, 4], [5, 6, 2, 0]]
    it = 0
    for nb in range(8):
        b2 = nb % 2
        srcs = [w_in[:, 7 * D + nb * P:7 * D + (nb + 1) * P], w_in[:, 8 * D + nb * P:8 * D + (nb + 1) * P],
                w_br_hg[:, nb * P:(nb + 1) * P], w_br_da[:, nb * P:(nb + 1) * P]]
        for j in range(4):
            pg.add("pool", lambda e, b2=b2, j=j, src=srcs[j]: e.dma_start(
                out=w2[b2][:, :, j, :], in_=src.rearrange("(c p) n -> p c n", p=P)),
                w=[tk("w2", b2, j)], dma=("w2", b2, j))
        for gi, (t0, NT, tiles) in enumerate(groups):
            banks = [0, 1, 3, 4] if it % 2 == 0 else [5, 6, 2, 1]
            if it % 2 == 1:
                banks = [5, 6, 2, 0]
            it += 1
            gb = gi % 2
            xt = xtoks(t0, NT)
            ytk_hg = [tk("yT", hh, i) for hh in range(8) for i in range(t0 // P, (t0 + NT + P - 1) // P)]
            ytk_da = [tk("yT", 8 + hh, i) for hh in range(8) for i in range(t0 // P, (t0 + NT + P - 1) // P)]
            for j in range(4):
                bk = banks[j]
                for c in range(8):
                    if j < 2:
                        rhs = xT[:, c, t0:t0 + NT]
                        rt = xt
                    elif j == 2:
                        rhs = yT[:, c, t0:t0 + NT]
                        rt = ytk_hg
                    else:
                        rhs = yT[:, 8 + c, t0:t0 + NT]
                        rt = ytk_da
                    pg.add("pe", lambda e, bk=bk, b2=b2, j=j, c=c, rhs=rhs, NT=NT: e.matmul(
                        PS[bk][:, 0:NT], lhsT=w2[b2][:, c, j, :], rhs=rhs, start=(c == 0), stop=(c == 7)),
                        r=[tk("w2", b2, j)] + rt, w=[tk("ps", bk)])
            pg.add("act", lambda e, gb=gb, bk=banks[0], NT=NT: e.activation(out=s1b[gb][:, 0:NT], in_=PS[bk][:, 0:NT], func=AF.Sigmoid),
                   r=[tk("ps", banks[0])], w=[tk("s1b", gb)])
            pg.add("act", lambda e, gb=gb, bk=banks[1], NT=NT: e.activation(out=s2b[gb][:, 0:NT], in_=PS[bk][:, 0:NT], func=AF.Sigmoid),
                   r=[tk("ps", banks[1])], w=[tk("s2b", gb)])
            pg.add("dve", lambda e, gb=gb, bk=banks[2], NT=NT: e.tensor_tensor(out=m1b[gb][:, 0:NT], in0=PS[bk][:, 0:NT],
                                                                               in1=s1b[gb][:, 0:NT], op=ALU.mult),
                   r=[tk("ps", banks[2]), tk("s1b", gb)], w=[tk("m1b", gb)])
            pg.add("dve", lambda e, gb=gb, bk=banks[3], NT=NT: e.tensor_tensor(out=m2b[gb][:, 0:NT], in0=PS[bk][:, 0:NT],
                                                                               in1=s2b[gb][:, 0:NT], op=ALU.mult),
                   r=[tk("ps", banks[3]), tk("s2b", gb)], w=[tk("m2b", gb)])
            pg.add("pool", lambda e, gb=gb, nb=nb, t0=t0, NT=NT: e.tensor_tensor(
                out=mT[:, nb, t0:t0 + NT], in0=m1b[gb][:, 0:NT], in1=m2b[gb][:, 0:NT], op=ALU.add),
                r=[tk("m1b", gb), tk("m2b", gb)],
                w=[tk("mT", i) for i in range(t0 // P, (t0 + NT + P - 1) // P)])

    chk(201)
    pg.barrier()
    arena["off"] = mark2
    wout = sb("wout", [P, 8, D], BF16)
    g1b = sb("g1b", [P, D])
    b1b = sb("b1b", [P, D])
    xf = [sb("xf%d" % i, [P, D]) for i in range(3)]
    rb = [sb("rb%d" % i, [P, D]) for i in range(3)]
    hb = [sb("hb%d" % i, [P, D], BF16) for i in range(3)]

    pg.add("pool", lambda e: e.dma_start(out=wout[:], in_=w_out.rearrange("(c p) n -> p c n", p=P)),
           w=[tk("wout")], dma="wout")
    pg.add("sp", lambda e: e.dma_start(out=g1b[:], in_=ln1_g.to_broadcast([P, D])), w=[tk("g1b")], dma="g1b")
    pg.add("sp", lambda e: e.dma_start(out=b1b[:], in_=ln1_b.to_broadcast([P, D])), w=[tk("b1b")], dma="b1b")
    pg.add("dve", lambda e: e.tensor_scalar(out=g1b[:], in0=g1b[:], scalar1=ALPHA, scalar2=None, op0=ALU.mult),
           r=[tk("g1b")], w=[tk("g1b")])
    pg.add("dve", lambda e: e.tensor_scalar(out=b1b[:], in0=b1b[:], scalar1=ALPHA, scalar2=None, op0=ALU.mult),
           r=[tk("b1b")], w=[tk("b1b")])
    pg.add("pool", lambda e: e.memset(hT2[:, :, 0:2], 0.0), w=[tk("hTz")])
    pg.add("pool", lambda e: e.memset(hT2[:, :, 2050:2052], 0.0), w=[tk("hTz")])
    pg.add("pool", lambda e: e.memset(hT2[:, :, 2116:2118], 0.0), w=[tk("hTz")])

    def layer_norm_tile(src_ap, bi, gtile, btile, gtok, btok, out_ap, out_tok, src_tok):
        for hf_ in range(2):
            pg.add("dve", lambda e, hf_=hf_: e.bn_stats(out=stt[bi][:, hf_ * 6:(hf_ + 1) * 6], in_=src_ap[:, hf_ * 512:(hf_ + 1) * 512]),
                   r=[src_tok], w=[tk("stt", bi)])
        pg.add("dve", lambda e: e.bn_aggr(out=mv[bi][:, 0:2], in_=stt[bi][:]), r=[tk("stt", bi)], w=[tk("mv", bi)])
        pg.add("act", lambda e: e.activation(out=mv[bi][:, 2:3], in_=mv[bi][:, 1:2], func=AF.Ln, bias=epsc[:, :]),
               r=[tk("mv", bi), tk("epsc")], w=[tk("mv", bi)])
        pg.add("act", lambda e: e.activation(out=mv[bi][:, 2:3], in_=mv[bi][:, 2:3], func=AF.Exp, scale=-0.5),
               r=[tk("mv", bi)], w=[tk("mv", bi)])
        yield
        pg.add("dve", lambda e: e.tensor_scalar(out=src_ap, in0=src_ap, scalar1=mv[bi][:, 0:1], scalar2=mv[bi][:, 2:3],
                                                op0=ALU.subtract, op1=ALU.mult),
               r=[src_tok, tk("mv", bi)], w=[src_tok])
        pg.add("dve", lambda e: e.tensor_tensor(out=src_ap, in0=src_ap, in1=gtile[:], op=ALU.mult),
               r=[src_tok, gtok], w=[src_tok])
        yield
        pg.add("pool", lambda e: e.tensor_tensor(out=out_ap, in0=src_ap, in1=btile[:], op=ALU.add),
               r=[src_tok, btok], w=[out_tok])
        yield

    def run_staggered(make_gen, n, depth):
        active = []
        nxt = 0
        while active or nxt < n:
            if nxt < n and len(active) < depth:
                active.append(make_gen(nxt))
                nxt += 1
            for g in list(active):
                try:
                    next(g)
                except StopIteration:
                    active.remove(g)

    def gen_2b(i):
        bi = i % 3
        src = x_p[i * P:(i + 1) * P, :] if i < 16 else x_s
        pg.add("sp", lambda e, bi=bi, src=src: e.dma_start(out=xf[bi][:], in_=src), w=[tk("xf", bi)], dma=("xf", bi))
        for hf_ in range(2):
            bk = ([3, 4] if i % 2 == 0 else [5, 6])[hf_]
            for c in range(8):
                pg.add("pe", lambda e, bk=bk, c=c, i=i, hf_=hf_: e.matmul(
                    PS[bk][:, :], lhsT=mT[:, c, i * P:(i + 1) * P], rhs=wout[:, c, hf_ * 512:(hf_ + 1) * 512],
                    start=(c == 0), stop=(c == 7)),
                    r=[tk("mT", i), tk("wout")], w=[tk("ps", bk)])
            pg.add("dve", lambda e, bk=bk, bi=bi, hf_=hf_: e.scalar_tensor_tensor(
                out=rb[bi][:, hf_ * 512:(hf_ + 1) * 512], in0=xf[bi][:, hf_ * 512:(hf_ + 1) * 512], scalar=ALPHA,
                in1=PS[bk][:, :], op0=ALU.mult, op1=ALU.add),
                r=[tk("xf", bi), tk("ps", bk)], w=[tk("rb", bi)])
            yield
        yield from layer_norm_tile(rb[bi][:], bi, g1b, b1b, tk("g1b"), tk("b1b"), ACCT[:, i, :], tk("acc", i), tk("rb", bi))
        pg.add("act", lambda e, bi=bi, i=i: e.activation(out=hb[bi][:], in_=ACCT[:, i, :], func=AF.Copy, scale=1.0 / ALPHA),
               r=[tk("acc", i)], w=[tk("hb", bi)])
        yield
        for c in range(8):
            pg.add("pe", lambda e, bi=bi, c=c: e.transpose(out=PT[:, c * P:(c + 1) * P], in_=hb[bi][:, c * P:(c + 1) * P],
                                                           identity=identb[:]),
                   r=[tk("hb", bi), tk("identb")], w=[tk("pt")])
        if i < 16:
            pg.add("dve", lambda e, i=i: e.tensor_copy(out=hT2[:, :, 2 + i * P:2 + (i + 1) * P],
                                                       in_=PT[:].rearrange("p (a b) -> p a b", b=P)),
                   r=[tk("pt")], w=[tk("hT", i)])
        else:
            pg.add("dve", lambda e: e.tensor_copy(
                out=hT2[:, :, 2052:2184].rearrange("p a (s c) -> p a s c", c=66)[:, :, :, 0:64],
                in_=PT[:].rearrange("p (a s c) -> p a s c", s=2, c=64)),
                r=[tk("pt")], w=[tk("hT", 16)])
        yield

    run_staggered(gen_2b, 17, 3)

    chk(202)
    pg.barrier()
    arena["off"] = mark
    wup = [sb("wup%d" % i, [P, 8, 2, P], BF16) for i in range(2 * JG)]
    wdn = [sb("wdn%d" % i, [P, D], BF16) for i in range(2 * JG)]
    Ab = [sb("A%d" % i, [P, AW], BF16) for i in range(2 * JG)]
    cwork = [[sb("cw%d_%d" % (i, k), [P, 256]) for k in range(5)] for i in range(2)]
    co = sb("co", [P, 3, 2, 44])
    cot = sb("cot", [88, 3, P])
    r_cw = Ring(2)

    cgroups = [(256 * g, 258, 256 * g, "p") for g in range(8)] + [(2050, 132, 2048, "s")]

    def hT_toks(c0, n):
        toks = [tk("hTz")]
        for i in range(17):
            lo = 2 + i * P if i < 16 else 2052
            hi = lo + P if i < 16 else 2184
            if c0 < hi and c0 + n > lo:
                toks.append(tk("hT", i))
        return toks

    njg = (NJ + JG - 1) // JG
    jbufs = {}

    def gen_up(jg):
        js = list(range(jg * JG, min(NJ, (jg + 1) * JG)))
        bufs = []
        jbufs[jg] = bufs
        for jj, j in enumerate(js):
            bidx = (jg % 2) * JG + jj
            bufs.append(bidx)
            for vg in range(2):
                off = vg * DFF + j * P
                pg.add("pool", lambda e, bidx=bidx, vg=vg, off=off: e.dma_start(
                    out=wup[bidx][:, :, vg, :], in_=w_up[:, off:off + P].rearrange("(c p) n -> p c n", p=P)),
                    w=[tk("wup", bidx, vg)], dma=("wup", bidx, vg))
            pg.add("pool", lambda e, bidx=bidx, j=j: e.dma_start(out=wdn[bidx][:], in_=w_down[j * P:(j + 1) * P, :]),
                   w=[tk("wdn", bidx)], dma=("wdn", bidx))

            for (hc0, NC, ac0, kind) in cgroups:
                NO = NC - 2
                ht = hT_toks(hc0, NC)
                cwi = r_cw.next()
                cw = cwork[cwi]
                res = []
                for vg in range(2):
                    bk = [3, 4][vg] if (cwi == 0) else [5, 6][vg]
                    for c in range(8):
                        pg.add("pe", lambda e, bk=bk, bidx=bidx, vg=vg, c=c, hc0=hc0, NC=NC, kind=kind: e.matmul(
                            PS[bk][:, 0:NC], lhsT=wup[bidx][:, c, vg, :], rhs=hT2[:, c, hc0:hc0 + NC],
                            start=(c == 0), stop=(c == 7)),
                            r=[tk("wup", bidx, vg)] + ht, w=[tk("ps", bk)])
                    blk = vg * NJ + j
                    if kind == "s":
                        pg.add("dve", lambda e, bk=bk, blk=blk: e.tensor_copy(
                            out=PS[bk][:, 0:132].rearrange("p (s c) -> p s c", c=66)[:, :, 0:2],
                            in_=cv[:, 4:8, blk].rearrange("p (s r) -> p s r", r=2)),
                            r=[tk("cv")], w=[tk("ps", bk)])
                    t1 = cw[vg * 2]
                    t2 = cw[vg * 2 + 1]
                    pg.add("act", lambda e, bk=bk, t1=t1, blk=blk, NC=NC, NO=NO: e.activation(
                        out=t1[:, 0:NO], in_=PS[bk][:, 2:NC], func=AF.Identity, scale=cv[:, 2, blk:blk + 1],
                        bias=cv[:, 3, blk:blk + 1]),
                        r=[tk("ps", bk), tk("cv")], w=[tk("cw", cwi, vg * 2)])
                    pg.add("dve", lambda e, bk=bk, t1=t1, t2=t2, blk=blk, NC=NC, NO=NO: e.scalar_tensor_tensor(
                        out=t2[:, 0:NO], in0=PS[bk][:, 1:NC - 1], scalar=cv[:, 1, blk:blk + 1], in1=t1[:, 0:NO],
                        op0=ALU.mult, op1=ALU.add),
                        r=[tk("ps", bk), tk("cv"), tk("cw", cwi, vg * 2)], w=[tk("cw", cwi, vg * 2 + 1)])
                    pg.add("dve", lambda e, bk=bk, t1=t1, t2=t2, blk=blk, NC=NC, NO=NO: e.scalar_tensor_tensor(
                        out=t1[:, 0:NO], in0=PS[bk][:, 0:NO], scalar=cv[:, 0, blk:blk + 1], in1=t2[:, 0:NO],
                        op0=ALU.mult, op1=ALU.add),
                        r=[tk("ps", bk), tk("cv"), tk("cw", cwi, vg * 2 + 1)], w=[tk("cw", cwi, vg * 2)])
                    res.append(t1)
                    if kind == "p" and hc0 == 256 * 7:
                        pg.add("act", lambda e, bk=bk, blk=blk, NC=NC: e.copy(out=co[:, 0, :, blk], in_=PS[bk][:, NC - 2:NC]),
                               r=[tk("ps", bk)], w=[tk("co")])
                    if kind == "s":
                        pg.add("act", lambda e, bk=bk, blk=blk: e.copy(
                            out=co[:, 1:3, :, blk], in_=PS[bk][:, 0:132].rearrange("p (s c) -> p s c", c=66)[:, :, 64:66]),
                            r=[tk("ps", bk)], w=[tk("co")])
                sg = cw[4]
                pg.add("act", lambda e, sg=sg, g_=res[1], NO=NO: e.activation(out=sg[:, 0:NO], in_=g_[:, 0:NO], func=AF.Silu),
                       r=[tk("cw", cwi, 2)], w=[tk("cw", cwi, 4)])
                if kind == "p":
                    atoks = [tk("A", bidx, ac0 // P), tk("A", bidx, ac0 // P + 1)]
                else:
                    atoks = [tk("A", bidx, 16)]
                if kind == "p":
                    pg.add("pool", lambda e, bidx=bidx, ac0=ac0, NO=NO, sg=sg, v_=res[0]: e.tensor_tensor(
                        out=Ab[bidx][:, ac0:ac0 + NO], in0=v_[:, 0:NO], in1=sg[:, 0:NO], op=ALU.mult),
                        r=[tk("cw", cwi, 0), tk("cw", cwi, 4)], w=atoks)
                else:
                    pg.add("pool", lambda e, bidx=bidx, sg=sg, v_=res[0]: e.tensor_tensor(
                        out=Ab[bidx][:, 2048:2176].rearrange("p (s c) -> p s c", c=64),
                        in0=v_[:, 0:132].rearrange("p (s c) -> p s c", c=66)[:, :, 0:64],
                        in1=sg[:, 0:132].rearrange("p (s c) -> p s c", c=66)[:, :, 0:64], op=ALU.mult),
                        r=[tk("cw", cwi, 0), tk("cw", cwi, 4)], w=atoks)
                yield

    def gen_down(jg):
        bufs = jbufs[jg]
        for i in range(17):
            for hf_ in range(2):
                bk = [0, 1][hf_]
                for jj, bidx in enumerate(bufs):
                    lhs = Ab[bidx][:, i * P:(i + 1) * P]
                    pg.add("pe", lambda e, bk=bk, lhs=lhs, bidx=bidx, hf_=hf_, jj=jj, n=len(bufs): e.matmul(
                        PS[bk][:, :], lhsT=lhs, rhs=wdn[bidx][:, hf_ * 512:(hf_ + 1) * 512],
                        start=(jj == 0), stop=(jj == n - 1)),
                        r=[tk("A", bidx, i), tk("wdn", bidx)], w=[tk("ps", bk)])
                pg.add("dve", lambda e, bk=bk, i=i, hf_=hf_: e.tensor_tensor(
                    out=ACCT[:, i, hf_ * 512:(hf_ + 1) * 512], in0=PS[bk][:, :], in1=ACCT[:, i, hf_ * 512:(hf_ + 1) * 512],
                    op=ALU.add),
                    r=[tk("ps", bk), tk("acc", i)], w=[tk("acc", i)])
                yield

    run_interleaved([gen_up(0)])
    for jg in range(njg):
        run_interleaved([gen_down(jg), gen_up(jg + 1) if jg + 1 < njg else None])

    chk(203)
    for inst in range(3):
        pg.add("pe", lambda e, inst=inst: e.transpose(out=PS[2][0:88, 0:P], in_=co[:, inst, :, :].rearrange("p a b -> p (a b)"),
                                                      identity=ident[:]),
               r=[tk("co"), tk("ident")], w=[tk("ps", 2)])
        pg.add("dve", lambda e, inst=inst: e.tensor_copy(out=cot[:, inst, :], in_=PS[2][0:88, 0:P]),
               r=[tk("ps", 2)], w=[tk("cot", inst)])
        for r_ in range(2):
            dst = conv_p[r_, :] if inst == 0 else conv_s[inst - 1, r_, :]
            out_events.append(pg.add("sp", lambda e, inst=inst, r_=r_, dst=dst: e.dma_start(
                out=dst.rearrange("(b p) -> b p", p=P), in_=cot[r_ * 44:(r_ + 1) * 44, inst, :]),
                r=[tk("cot", inst)], dma=("cot", inst)))

    g2b = sb("g2b", [P, D])
    b2b = sb("b2b", [P, D])
    yo = [sb("yo%d" % i, [P, D]) for i in range(2)]
    pg.add("sp", lambda e: e.dma_start(out=g2b[:], in_=ln2_g.to_broadcast([P, D])), w=[tk("g2b")], dma="g2b")
    pg.add("sp", lambda e: e.dma_start(out=b2b[:], in_=ln2_b.to_broadcast([P, D])), w=[tk("b2b")], dma="b2b")
    def gen_3b(i):
        bi = i % 2
        yield from layer_norm_tile(ACCT[:, i, :], bi, g2b, b2b, tk("g2b"), tk("b2b"), yo[bi][:], tk("yo", bi), tk("acc", i))
        dst = y_p[i * P:(i + 1) * P, :] if i < 16 else y_s
        out_events.append(pg.add("sp", lambda e, bi=bi, dst=dst: e.dma_start(out=dst, in_=yo[bi][:]),
                                 r=[tk("yo", bi)], dma=("yo", bi)))
        yield

    run_staggered(gen_3b, 17, 2)

    pg.barrier()


_NC_CACHE = {}


def kernel(x_prompt, x_sample, cache_k, cache_v, state_hgrn, state_ffn_conv, w_in, hg_lb_logits,
           hg_norm_g, da_lambda_q1, da_lambda_k1, da_lambda_q2, da_lambda_k2, da_subln_g,
           w_br_hg, w_br_da, w_out, ln1_g, ln1_b, w_up, conv_w, conv_b, w_down, ln2_g, ln2_b):
    f = lambda a: np.ascontiguousarray(np.asarray(a, dtype=np.float32))
    if "nc" not in _NC_CACHE:
        _NC_CACHE["nc"] = build_nc()
    nc = _NC_CACHE["nc"]
    shared = {
        "w_in": f(w_in[0]),
        "lb_logits": f(hg_lb_logits).reshape(16, 128),
        "hg_norm_g": f(hg_norm_g).reshape(128, 1),
        "lq1": f(da_lambda_q1), "lk1": f(da_lambda_k1), "lq2": f(da_lambda_q2), "lk2": f(da_lambda_k2),
        "subln_g": f(da_subln_g).reshape(128, 1),
        "w_br_hg": f(w_br_hg[0]), "w_br_da": f(w_br_da[0]), "w_out": f(w_out[0]),
        "ln1_g": f(ln1_g), "ln1_b": f(ln1_b),
        "w_up": f(w_up[0]),
        "conv_wb": f(np.concatenate([np.asarray(conv_w[0]), np.asarray(conv_b)], axis=0)),
        "w_down": f(w_down[0]),
        "ln2_g": f(ln2_g), "ln2_b": f(ln2_b),
    }
    in_maps = []
    for c in range(8):
        m = dict(shared)
        m["x_p"] = f(x_prompt[c])
        m["x_s"] = f(x_sample[2 * c:2 * c + 2]).reshape(TS, D)
        m["cache_k"] = f(cache_k[0, 2 * c:2 * c + 2])
        m["cache_v"] = f(cache_v[0, 2 * c:2 * c + 2])
        m["state_hgrn"] = f(state_hgrn[0, 2 * c:2 * c + 2])
        m["state_conv"] = f(state_ffn_conv[0, 2 * c:2 * c + 2]).reshape(4, 2 * DFF)
        in_maps.append(m)
    res = run_bass_kernel_spmd(nc, in_maps, core_ids=list(range(8)))
    R = res.results
    y_prompt = np.stack([R[c]["y_p"] for c in range(8)], axis=0)
    y_sample = np.concatenate([R[c]["y_s"].reshape(2, SQ, D) for c in range(8)], axis=0)
    k_prompt = np.stack([R[c]["k_p"] for c in range(8)], axis=0)[None]
    v_prompt = np.stack([R[c]["v_p"] for c in range(8)], axis=0)[None]
    hgrn_prompt = np.stack([R[c]["hg_p"] for c in range(8)], axis=0)[None]
    conv_prompt = np.stack([R[c]["conv_p"] for c in range(8)], axis=0)[None]
    k_sample = np.concatenate([R[c]["k_s"].reshape(2, SQ, H, P) for c in range(8)], axis=0)[None]
    v_sample = np.concatenate([R[c]["v_s"].reshape(2, SQ, H, P) for c in range(8)], axis=0)[None]
    hgrn_sample = np.concatenate([R[c]["hg_s"] for c in range(8)], axis=0)[None]
    conv_sample = np.concatenate([R[c]["conv_s"] for c in range(8)], axis=0)[None]
    return (y_prompt.astype(np.float32), y_sample.astype(np.float32), k_prompt.astype(np.float32),
            v_prompt.astype(np.float32), hgrn_prompt.astype(np.float32), conv_prompt.astype(np.float32),
            k_sample.astype(np.float32), v_sample.astype(np.float32), hgrn_sample.astype(np.float32),
            conv_sample.astype(np.float32))
```

```python
import math
from collections import defaultdict
from contextlib import ExitStack

import numpy as np
import concourse.bass as bass
import concourse.mybir as mybir
from concourse.bass_utils import run_bass_kernel_spmd

F32 = mybir.dt.float32
BF16 = mybir.dt.bfloat16
ALU = mybir.AluOpType
AF = mybir.ActivationFunctionType
AX = mybir.AxisListType

P = 128
D = 1024
T = 2048
SQ = 64
TS = 2 * SQ
NTOK = T + TS
H = 8
DFF = 2816
NJ = DFF // P
PAST = 2048
LN_EPS = 1e-5
ALPHA = 2.0 ** 0.25
LAM_INIT = 0.8 - 0.6 * math.exp(-0.3 * 0)
DA_SCALE = 64 ** -0.5
HTW = 2184
AW = 2180
JG = 3

SAME_ENGINE_SYNC = True
EXCL_PSUM = True


class Tok:
    __slots__ = ("w", "r", "excl")

    def __init__(self):
        self.w = None
        self.r = {}
        self.excl = False


class _Rec:
    def __getattr__(self, name):
        return lambda *a, **k: (name, a, k)


_REC = _Rec()


class Op:
    __slots__ = ("fn", "deps", "inc", "dma")

    def __init__(self, fn, deps, dma):
        self.fn = fn(_REC) if fn is not None else None
        self.deps = deps
        self.inc = False
        self.dma = dma


class Prog:
    ENGS = ("pe", "act", "dve", "pool", "sp")

    def __init__(self):
        self.ops = {e: [] for e in self.ENGS}
        self.slots = {}
        self.tk = defaultdict(Tok)

    def t(self, *key):
        t = self.tk[key]
        if key[0] in ("ps", "pt"):
            t.excl = True
        return t

    def add(self, eng, fn, r=(), w=(), dma=None):
        if EXCL_PSUM:
            xr = [t for t in r if t.excl]
            if xr:
                r = [t for t in r if not t.excl]
                w = list(w) + [t for t in xr if t not in w]
        ops = self.ops[eng]
        idx = len(ops)
        deps = {}

        def need(k, v):
            if k[0] == "e" and k[1] == eng and (eng == "pe" or not SAME_ENGINE_SYNC) and dma is None:
                return
            if deps.get(k, -1) < v:
                deps[k] = v

        for t in r:
            if t.w is not None:
                need(*t.w)
        for t in w:
            if t.w is not None:
                need(*t.w)
            for k, v in t.r.items():
                need(k, v)
        for k, v in deps.items():
            if k[0] == "e":
                self.ops[k[1]][v].inc = True
        if dma is not None:
            cnt = self.slots.get(dma, 0) + 16
            self.slots[dma] = cnt
            ev = (("d", dma), cnt)
        else:
            ev = (("e", eng), idx)
        for t in w:
            t.w = ev
            t.r = {}
        for t in r:
            if t.w is not ev:
                if t.r.get(ev[0], -1) < ev[1]:
                    t.r[ev[0]] = ev[1]
        ops.append(Op(fn, deps, dma))
        return ev

    def barrier(self):
        last = {}
        for e in self.ENGS:
            for idx in range(len(self.ops[e]) - 1, -1, -1):
                op = self.ops[e][idx]
                if op.fn is not None and op.dma is None:
                    last[e] = idx
                    break
        for e in self.ENGS:
            deps = {}
            for e2, idx in last.items():
                if e2 != e:
                    deps[("e", e2)] = idx
                    self.ops[e2][idx].inc = True
            for k, cnt in self.slots.items():
                deps[("d", k)] = cnt
            self.ops[e].append(Op(None, deps, None))
        for t in self.tk.values():
            t.w = None
            t.r = {}

    def emit(self, nc, st):
        sems = {e: st.enter_context(nc.semaphore("s_" + e)) for e in self.ENGS}
        dsem = {k: st.enter_context(nc.semaphore("d%d" % i)) for i, k in enumerate(self.slots)}
        incval = {}
        for e in self.ENGS:
            c = 0
            vals = []
            for op in self.ops[e]:
                if op.inc:
                    c += 1
                vals.append(c)
            incval[e] = vals
        block = st.enter_context(nc.Block())

        def run(e, engine):
            waited = {}
            for op in self.ops[e]:
                for k, v in op.deps.items():
                    if k[0] == "e":
                        sem = sems[k[1]]
                        val = incval[k[1]][v]
                    else:
                        sem = dsem[k[1]]
                        val = v
                    if waited.get(k, 0) < val:
                        engine.wait_ge(sem, val)
                        waited[k] = val
                if op.fn is None:
                    continue
                ins = getattr(engine, op.fn[0])(*op.fn[1], **op.fn[2])
                if op.dma is not None:
                    ins.then_inc(dsem[op.dma], 16)
                elif op.inc:
                    ins.then_inc(sems[e], 1)

        @block.tensor
        def _(eng):
            run("pe", eng)

        @block.scalar
        def _(eng):
            run("act", eng)

        @block.vector
        def _(eng):
            run("dve", eng)

        @block.gpsimd
        def _(eng):
            run("pool", eng)

        @block.sync
        def _(eng):
            run("sp", eng)


class _Stop(Exception):
    pass


STOP = None
EXPT = 0


def build_nc():
    nc = bass.Bass("TRN2", target_bir_lowering=False, dynamic_dma_scratch_size=12288)
    pg = Prog()
    tk = pg.t
    st = ExitStack()

    def chk(n):
        if STOP is not None and STOP == n:
            raise _Stop()

    try:
        _build_body(nc, pg, tk, st, chk)
    except _Stop:
        pg.barrier()
    pg.emit(nc, st)
    st.close()
    return nc


def _build_body(nc, pg, tk, st, chk):

    def din(name, shape):
        return nc.dram_tensor(name, list(shape), F32, kind="ExternalInput").ap()

    def dout(name, shape):
        return nc.dram_tensor(name, list(shape), F32, kind="ExternalOutput").ap()

    x_p = din("x_p", [T, D])
    x_s = din("x_s", [TS, D])
    cache_k = din("cache_k", [2, PAST, H, P])
    cache_v = din("cache_v", [2, PAST, H, P])
    state_hgrn = din("state_hgrn", [2, H, P, P])
    state_conv = din("state_conv", [4, 2 * DFF])
    w_in = din("w_in", [D, 9 * D])
    lb_logits = din("lb_logits", [16, P])
    hg_norm_g = din("hg_norm_g", [P, 1])
    lq1 = din("lq1", [1, 64])
    lk1 = din("lk1", [1, 64])
    lq2 = din("lq2", [1, 64])
    lk2 = din("lk2", [1, 64])
    subln_g = din("subln_g", [P, 1])
    w_br_hg = din("w_br_hg", [D, D])
    w_br_da = din("w_br_da", [D, D])
    w_out = din("w_out", [D, D])
    ln1_g = din("ln1_g", [1, D])
    ln1_b = din("ln1_b", [1, D])
    w_up = din("w_up", [D, 2 * DFF])
    conv_wb = din("conv_wb", [4, 2 * DFF])
    w_down = din("w_down", [DFF, D])
    ln2_g = din("ln2_g", [1, D])
    ln2_b = din("ln2_b", [1, D])

    y_p = dout("y_p", [T, D])
    y_s = dout("y_s", [TS, D])
    k_p = dout("k_p", [T, H, P])
    v_p = dout("v_p", [T, H, P])
    hg_p = dout("hg_p", [H, P, P])
    conv_p = dout("conv_p", [2, 2 * DFF])
    k_s = dout("k_s", [TS, H, P])
    v_s = dout("v_s", [TS, H, P])
    hg_s = dout("hg_s", [2, H, P, P])
    conv_s = dout("conv_s", [2, 2, 2 * DFF])

    MAIN_BYTES = 216896
    MAIN = st.enter_context(nc.sbuf_tensor("main", [P, MAIN_BYTES // 4], F32))
    arena = {"off": 0, "peak": 0}

    def sb(name, shape, dt=F32):
        n = 1
        for s_ in shape[1:]:
            n *= s_
        nbytes = n * (4 if dt is F32 else 2)
        nb_al = (nbytes + 31) // 32 * 32
        off = arena["off"]
        assert off + nb_al <= MAIN_BYTES, ("SBUF arena overflow", name, off, nb_al)
        arena["off"] = off + nb_al
        arena["peak"] = max(arena["peak"], arena["off"])
        v = MAIN[:, off // 4:(off + nb_al) // 4]
        if dt is not F32:
            v = v.bitcast(dt)
        v = v[0:shape[0], 0:n]
        if len(shape) == 3:
            v = v.rearrange("p (a b) -> p a b", b=shape[2])
        elif len(shape) == 4:
            v = v.rearrange("p (a b c) -> p a b c", b=shape[2], c=shape[3])
        return v

    def ps(name, shape, dt=F32):
        return st.enter_context(nc.psum_tensor(name, list(shape), dt))

    out_events = []
    scratch = {"off": 160000}

    def sbs(name, shape, dt=F32):
        save = arena["off"]
        arena["off"] = scratch["off"]
        v = sb(name, shape, dt)
        scratch["off"] = arena["off"]
        arena["off"] = save
        return v

    PS = [ps("ps%d" % i, [P, 512], F32) for i in range(7)]
    PT = ps("pt", [P, 1024], BF16)

    ones = sbs("ones", [P, 132])
    ident = sb("ident", [P, P])
    identb = sb("identb", [P, P], BF16)
    hmask = sb("hmask", [P, P])
    rmask = sb("rmask", [P, 512], BF16)
    pg.add("pool", lambda e: e.memset(ones[:], 1.0), w=[tk("ones")])
    pg.add("pool", lambda e: e.affine_select(out=ident[:], in_=ones[:, 0:P], pattern=[[-1, P]],
                                              compare_op=ALU.is_equal, fill=0.0, base=0,
                                              channel_multiplier=1),
           r=[tk("ones")], w=[tk("ident")])
    pg.add("dve", lambda e: e.tensor_copy(out=identb[:], in_=ident[:]), r=[tk("ident")], w=[tk("identb")])
    pg.add("pool", lambda e: e.affine_select(out=hmask[:], in_=ones[:, 0:P], pattern=[[1, P]],
                                              compare_op=ALU.is_ge, fill=0.0, base=0,
                                              channel_multiplier=-1),
           r=[tk("ones")], w=[tk("hmask")])
    pg.add("pool", lambda e: e.memset(hmask[0:64, 64:128], 0.0), w=[tk("hmask")])
    pg.add("pool", lambda e: e.memset(rmask[:], 1.0), w=[tk("rmask")])
    pg.add("pool", lambda e: e.memset(rmask[:].rearrange("p (a b) -> p a b", b=64)[:, :, 0:1], 0.0),
           w=[tk("rmask")])

    lbrows = sbs("lbrows", [16, P])
    lgc = sbs("lgc", [P, 16])
    lbv = sb("lbv", [P, H])
    omlv = sb("omlv", [P, H])
    tmp8 = sbs("tmp8", [P, H])
    pg.add("sp", lambda e: e.dma_start(out=lbrows[:], in_=lb_logits), w=[tk("lbrows")], dma="lbrows")
    pg.add("pe", lambda e: e.transpose(out=PS[0][:, 0:16], in_=lbrows[:], identity=ident[0:16, 0:16]),
           r=[tk("lbrows"), tk("ident")], w=[tk("ps", 0)])
    pg.add("dve", lambda e: e.tensor_copy(out=lgc[:], in_=PS[0][:, 0:16]), r=[tk("ps", 0)], w=[tk("lgc")])
    pg.add("dve", lambda e: e.tensor_tensor(out=tmp8[:], in0=lgc[:, 8:16], in1=lgc[:, 0:8], op=ALU.subtract),
           r=[tk("lgc")], w=[tk("tmp8")])
    pg.add("act", lambda e: e.activation(out=tmp8[:], in_=tmp8[:], func=AF.Exp), r=[tk("tmp8")], w=[tk("tmp8")])
    pg.add("dve", lambda e: e.tensor_scalar(out=tmp8[:], in0=tmp8[:], scalar1=1.0, scalar2=None, op0=ALU.add),
           r=[tk("tmp8")], w=[tk("tmp8")])
    pg.add("dve", lambda e: e.reciprocal(out=lbv[:], in_=tmp8[:]), r=[tk("tmp8")], w=[tk("lbv")])
    pg.add("dve", lambda e: e.tensor_scalar(out=omlv[:], in0=lbv[:], scalar1=-1.0, scalar2=1.0,
                                             op0=ALU.mult, op1=ALU.add),
           r=[tk("lbv")], w=[tk("omlv")])

    lamin = sbs("lamin", [P, 4, 64])
    lamt = sbs("lamt", [P, 2, 64])
    lams = sbs("lams", [P, 2])
    neglam = sb("neglam", [P, 1])
    for i, src in enumerate((lq1, lk1, lq2, lk2)):
        pg.add("sp", lambda e, i=i, src=src: e.dma_start(out=lamin[:, i, :], in_=src.to_broadcast([P, 64])),
               w=[tk("lamin", i)], dma=("lamin", i))
    pg.add("dve", lambda e: e.tensor_tensor(out=lamt[:], in0=lamin[:, 0::2, :], in1=lamin[:, 1::2, :], op=ALU.mult),
           r=[tk("lamin", i) for i in range(4)], w=[tk("lamt")])
    pg.add("dve", lambda e: e.tensor_reduce(out=lams[:], in_=lamt[:], axis=AX.X, op=ALU.add),
           r=[tk("lamt")], w=[tk("lams")])
    pg.add("act", lambda e: e.activation(out=lams[:], in_=lams[:], func=AF.Exp), r=[tk("lams")], w=[tk("lams")])
    pg.add("dve", lambda e: e.tensor_tensor(out=neglam[:], in0=lams[:, 1:2], in1=lams[:, 0:1], op=ALU.subtract),
           r=[tk("lams")], w=[tk("neglam")])
    pg.add("dve", lambda e: e.tensor_scalar(out=neglam[:], in0=neglam[:], scalar1=-LAM_INIT, scalar2=None, op0=ALU.add),
           r=[tk("neglam")], w=[tk("neglam")])

    hgg = sb("hgg", [P, 1])
    sgg = sb("sgg", [P, 1])
    pg.add("sp", lambda e: e.dma_start(out=hgg[:], in_=hg_norm_g), w=[tk("hgg")], dma="hgg")
    pg.add("sp", lambda e: e.dma_start(out=sgg[:], in_=subln_g), w=[tk("sgg")], dma="sgg")
    pg.add("dve", lambda e: e.tensor_scalar(out=sgg[:], in0=sgg[:], scalar1=1.0 - LAM_INIT, scalar2=None, op0=ALU.mult),
           r=[tk("sgg")], w=[tk("sgg")])

    cvrows = sbs("cvrows", [44, 8, P])
    cv = sb("cv", [P, 8, 44])
    for r_ in range(8):
        srow = conv_wb[r_, :] if r_ < 4 else state_conv[r_ - 4, :]
        pg.add("sp", lambda e, r_=r_, srow=srow: e.dma_start(out=cvrows[:, r_, :],
                                                   in_=srow.rearrange("(b p) -> b p", p=P)),
               w=[tk("cvrows", r_)], dma=("cvrows", r_))
        pg.add("pe", lambda e, r_=r_: e.transpose(out=PS[1][:, r_ * 44:(r_ + 1) * 44], in_=cvrows[:, r_, :],
                                                   identity=ident[0:44, 0:44]),
               r=[tk("cvrows", r_), tk("ident")], w=[tk("ps", 1)])
    pg.add("dve", lambda e: e.tensor_copy(out=cv[:].rearrange("p a b -> p (a b)"), in_=PS[1][:, 0:352]),
           r=[tk("ps", 1)], w=[tk("cv")])

    hT2 = sb("hT2", [P, 8, HTW], BF16)
    xT = hT2
    big1_off = arena["off"]
    yT = sb("yT", [P, 16, NTOK], BF16)
    arena["off"] = big1_off
    ACCT = sb("acc", [P, 17, D], F32)
    epsc = sb("epsc", [P, 1])
    stt = [sb("stt%d" % i, [P, 12]) for i in range(3)]
    mv = [sb("mv%d" % i, [P, 4]) for i in range(3)]
    pg.add("pool", lambda e: e.memset(epsc[:], LN_EPS), w=[tk("epsc")])
    mark = arena["off"]

    xb = [sb("xb%d" % i, [P, D], BF16) for i in range(2)]
    for i in range(17):
        src = x_p[i * P:(i + 1) * P, :] if i < 16 else x_s
        b = i % 2
        pg.add("pool", lambda e, b=b, src=src: e.dma_start(out=xb[b][:], in_=src), w=[tk("xb", b)], dma=("xb", b))
        for dc in range(8):
            pg.add("pe", lambda e, b=b, dc=dc: e.transpose(out=PT[:, dc * P:(dc + 1) * P],
                                                            in_=xb[b][:, dc * P:(dc + 1) * P], identity=identb[:]),
                   r=[tk("xb", b), tk("identb")], w=[tk("pt")])
        eng = "act" if i % 2 else "dve"
        if eng == "act":
            pg.add("act", lambda e, i=i: e.copy(out=xT[:, :, i * P:(i + 1) * P],
                                                 in_=PT[:].rearrange("p (a b) -> p a b", b=P)),
                   r=[tk("pt")], w=[tk("xT", i)])
        else:
            pg.add("dve", lambda e, i=i: e.tensor_copy(out=xT[:, :, i * P:(i + 1) * P],
                                                        in_=PT[:].rearrange("p (a b) -> p a b", b=P)),
                   r=[tk("pt")], w=[tk("xT", i)])

    chk(0)
    pg.barrier()
    arena["off"] = mark
    wh = [sb("wh%d" % i, [P, 8, 7 * P], BF16) for i in range(2)]
    WOFF = [0, 1, 4, 5, 2, 3, 6]
    hq_sb = [sb("hq%d" % i, [P, 512]) for i in range(2)]
    sigp = [sb("sigp%d" % i, [P, 512]) for i in range(2)]
    sign = [sb("sign%d" % i, [P, 512]) for i in range(2)]
    fbuf = sigp
    bbuf = [sb("bbuf%d" % i, [P, 512]) for i in range(2)]
    ebuf = [sb("ebuf%d" % i, [P, 512]) for i in range(2)]
    enbuf = bbuf
    qdz = [sb("qdz%d" % i, [P, 4, 3, 64], BF16) for i in range(2)]
    kdT = [sb("kdT%d" % i, [P, 512], BF16) for i in range(2)]
    qT = [sb("qT%d" % i, [P, 2, 512], BF16) for i in range(2)]
    vh = [sb("vh%d" % i, [P, 4, P], BF16) for i in range(2)]
    sog = [sb("sog%d" % i, [P, 4, P]) for i in range(2)]
    tm32 = [sb("tm32_%d" % i, [P, 512]) for i in range(2)]
    qTs = [sb("qTs%d" % i, [P, 512], BF16) for i in range(2)]
    kdtm = [sb("kdtm%d" % i, [P, 2, P], BF16) for i in range(2)]
    atm = [sb("atm%d" % i, [P, P], BF16) for i in range(2)]
    s32 = [sb("s32_%d" % i, [P, P]) for i in range(3)]
    sbf = [sb("sbf%d" % i, [P, P], BF16) for i in range(3)]
    t1b = [sb("t1b%d" % i, [P, P]) for i in range(2)]
    junk = sb("junk", [P, P])
    ssb = [sb("ssb%d" % i, [P, 2]) for i in range(4)]
    yh = [sb("yh%d" % i, [P, P], BF16) for i in range(2)]
    ktp = sb("ktp", [P, T], BF16)
    vp = sb("vp", [P, 16, 136], BF16)
    kts1 = sb("kts", [P, PAST], BF16)
    vs1_ = sb("vs", [P, 16, 136], BF16)
    ktn = [sb("ktn%d" % i, [P, SQ], BF16) for i in range(2)]
    vn = [sb("vn%d" % i, [SQ, 136], BF16) for i in range(2)]
    kc1 = sb("kc", [P, 16, P], BF16)
    kc = [kc1, kc1]
    pT = [sb("pT%d" % i, [P, 2, 256], BF16) for i in range(2)]
    rden = [sb("rden%d" % i, [P, 4]) for i in range(2)]
    t1a = [sb("t1a%d" % i, [P, P]) for i in range(2)]
    abuf = [sb("abuf%d" % i, [P, P]) for i in range(2)]
    ya = [sb("ya%d" % i, [P, P], BF16) for i in range(2)]

    for i in range(2):
        pg.add("pool", lambda e, i=i: e.memset(qT[i][:], 0.0), w=[tk("qT", i)])
        pg.add("pool", lambda e, i=i: e.memset(qdz[i][:, :, 1, :], 0.0), w=[tk("qdz", i)])
        pg.add("pool", lambda e, i=i: e.memset(vn[i][:, 128:130], 1.0), w=[tk("vn1", i)])
    pg.add("pool", lambda e: e.memset(vs1_[:, :, 128:130], 1.0), w=[tk("vs1", 0)])
    pg.add("pool", lambda e: e.memset(vp[:, :, 128:130], 1.0), w=[tk("vp1")])

    class Ring:
        def __init__(self, n):
            self.n = n
            self.i = -1

        def next(self):
            self.i = (self.i + 1) % self.n
            return self.i

    r_kvst = Ring(2)
    r_kdtm = Ring(2)
    r_atm = Ring(2)
    r_s = Ring(3)
    r_t1b = Ring(2)
    r_ss = Ring(4)
    r_yh = Ring(2)
    r_pT = Ring(2)
    r_sc = Ring(2)
    r_rden = Ring(2)
    r_t1a = Ring(2)
    r_ab = Ring(2)
    r_ya = Ring(2)
    r_pa = Ring(2)

    PA = [PS[0], PS[1]]
    PH = PS[2]
    SC = [PS[3], PS[4]]
    AC = [PS[5], PS[6]]
    PAI = [0, 1]
    SCI = [3, 4]
    ACI = [5, 6]

    def evac_eng(i):
        return "act" if i % 2 else "dve"

    groups = []
    for g in range(4):
        groups.append((g * 512, 512, [(g * 512 + k * P, P, "p", None) for k in range(4)]))
    groups.append((T, TS, [(T, SQ, "s", 0), (T + SQ, SQ, "s", 1)]))

    def xtoks(t0, n):
        return [tk("xT", i) for i in range(t0 // P, (t0 + n + P - 1) // P)]

    def run_interleaved(gens):
        gens = [g for g in gens if g is not None]
        while gens:
            for g in list(gens):
                try:
                    next(g)
                except StopIteration:
                    gens.remove(g)

    def rstd_ops(si, R):
        pg.add("act", lambda e: e.activation(out=ssb[si][0:R, 1:2], in_=ssb[si][0:R, 0:1], func=AF.Ln,
                                             scale=1.0 / P, bias=epsc[0:R, :]),
               r=[tk("ssb", si), tk("epsc")], w=[tk("ssb", si)])
        pg.add("act", lambda e: e.activation(out=ssb[si][0:R, 1:2], in_=ssb[si][0:R, 1:2], func=AF.Exp, scale=-0.5),
               r=[tk("ssb", si)], w=[tk("ssb", si)])

    def gen_attend(h, qbuf, qc0, nq, qrows, keytiles, out_cols):
        nqt = nq // qrows
        acc_v = [AC[i][:].rearrange("p (m c) -> p m c", m=2) for i in range(nqt)]
        last_for_qt = {}
        first_for_qt = {}
        for ki, kt in enumerate(keytiles):
            for iq in range(nqt):
                if iq * qrows >= kt[3]:
                    last_for_qt[iq] = ki
                    first_for_qt.setdefault(iq, ki)

        def scores(ki):
            kt_ap, v_ap, ns, q0, diag, rtoks = keytiles[ki]
            sci = r_sc.next()
            scv = SC[sci][:].rearrange("p (m c) -> p m c", m=2)
            for m in range(2):
                pg.add("pe", lambda e, m=m: e.matmul(
                    scv[0:ns, m, q0:nq], lhsT=kt_ap, rhs=qT[qbuf][:, m, qc0 + q0:qc0 + nq],
                    start=(m == 0), stop=(m == 1)),
                    r=rtoks + [tk("qT", qbuf)], w=[tk("ps", SCI[sci])])
            return sci, scv

        nxt = scores(0)
        for ki, (kt_ap, v_ap, ns, q0, diag, rtoks) in enumerate(keytiles):
            sci, scv = nxt
            if ki + 1 < len(keytiles):
                nxt = scores(ki + 1)
            pi = r_pT.next()
            pg.add("act", lambda e: e.activation(out=pT[pi][0:ns, :, q0:nq], in_=scv[0:ns, :, q0:nq], func=AF.Exp),
                   r=[tk("ps", SCI[sci])], w=[tk("pT", pi)])
            if diag is not None:
                pg.add("pool", lambda e: e.memset(pT[pi][64:128, :, diag * P:diag * P + 64], 0.0), w=[tk("pT", pi)])
            yield
            for iq in range(nqt):
                if iq * qrows < q0:
                    continue
                for m in range(2):
                    pg.add("pe", lambda e, iq=iq, m=m: e.matmul(
                        acc_v[iq][0:qrows, m, 0:129], lhsT=pT[pi][0:ns, m, iq * qrows:(iq + 1) * qrows],
                        rhs=v_ap[:, 0:129], start=(ki == first_for_qt[iq] and m == 0),
                        stop=(ki == last_for_qt[iq] and m == 1)),
                        r=rtoks + [tk("pT", pi)], w=[tk("ps", ACI[iq])])
            yield
        for iq in range(nqt):
            yield from attend_finalize(h, acc_v[iq], tk("ps", ACI[iq]), qrows, out_cols[iq])

    def attend_finalize(h, av, atok, R, c0):
        if True:
            ri = r_rden.next()
            pg.add("dve", lambda e: e.reciprocal(out=rden[ri][0:R, 0:2], in_=av[0:R, :, 128]),
                   r=[atok], w=[tk("rden", ri)])
            pg.add("dve", lambda e: e.tensor_tensor(out=rden[ri][0:R, 2:3], in0=rden[ri][0:R, 1:2],
                                                    in1=neglam[0:R, :], op=ALU.mult),
                   r=[tk("rden", ri), tk("neglam")], w=[tk("rden", ri)])
            ti = r_t1a.next()
            pg.add("act", lambda e: e.activation(out=t1a[ti][0:R, :], in_=av[0:R, 1, 0:128], func=AF.Copy,
                                                 scale=rden[ri][0:R, 2:3]),
                   r=[atok, tk("rden", ri)], w=[tk("t1a", ti)])
            ai = r_ab.next()
            pg.add("dve", lambda e: e.scalar_tensor_tensor(
                out=abuf[ai][0:R, :], in0=av[0:R, 0, 0:128], scalar=rden[ri][0:R, 0:1], in1=t1a[ti][0:R, :],
                op0=ALU.mult, op1=ALU.add),
                r=[atok, tk("rden", ri), tk("t1a", ti)], w=[tk("abuf", ai)])
            yield
            si = r_ss.next()
            pg.add("act", lambda e: e.activation(out=junk[0:R, :], in_=abuf[ai][0:R, :], func=AF.Square,
                                                 accum_out=ssb[si][0:R, 0:1]),
                   r=[tk("abuf", ai)], w=[tk("junk"), tk("ssb", si)])
            rstd_ops(si, R)
            yi = r_ya.next()
            pg.add("dve", lambda e: e.tensor_scalar(out=ya[yi][0:R, :], in0=abuf[ai][0:R, :],
                                                    scalar1=ssb[si][0:R, 1:2], scalar2=None, op0=ALU.mult),
                   r=[tk("abuf", ai), tk("ssb", si)], w=[tk("ya", yi)])
            yield
            pg.add("pe", lambda e: e.transpose(out=PT[:, 0:R], in_=ya[yi][0:R, :], identity=identb[0:R, 0:R]),
                   r=[tk("ya", yi), tk("identb")], w=[tk("pt")])
            pg.add("act", lambda e: e.activation(out=yT[:, 8 + h, c0:c0 + R], in_=PT[:, 0:R], func=AF.Copy,
                                                 scale=sgg[:, 0:1]),
                   r=[tk("pt"), tk("sgg")], w=[tk("yT", 8 + h, c0 // P)])
            yield

    def gen_attend_sample(h, qbuf, qc0, keytiles, out_col):
        R = SQ
        av = AC[0][:].rearrange("p (m c) -> p m c", m=2)
        atok = tk("ps", ACI[0])
        steps = [keytiles[i:i + 2] for i in range(0, 16, 2)] + [keytiles[16:17]]
        nsteps = len(steps)

        def scores(si_):
            tiles_ = steps[si_]
            sci = r_sc.next()
            scv = SC[sci][:, 0:256].rearrange("p (a c) -> p a c", c=64)
            for t_, (kt_ap, v_ap, ns, q0, diag, rtoks) in enumerate(tiles_):
                for m in range(2):
                    pg.add("pe", lambda e, t_=t_, m=m, kt_ap=kt_ap, ns=ns: e.matmul(
                        scv[0:ns, t_ * 2 + m, :], lhsT=kt_ap, rhs=qT[qbuf][:, m, qc0:qc0 + R],
                        start=(t_ == 0 and m == 0), stop=(t_ == len(tiles_) - 1 and m == 1)),
                        r=rtoks + [tk("qT", qbuf)], w=[tk("ps", SCI[sci])])
            return sci, scv

        nxt = scores(0)
        for si_, tiles_ in enumerate(steps):
            sci, scv = nxt
            if si_ + 1 < nsteps:
                nxt = scores(si_ + 1)
            ns = tiles_[0][2]
            nb_ = 2 * len(tiles_)
            pi = r_pT.next()
            ptv = pT[pi][:].rearrange("p m c -> p (m c)")[:, 0:256].rearrange("p (a c) -> p a c", c=64)
            pg.add("act", lambda e: e.activation(out=ptv[0:ns, 0:nb_, :], in_=scv[0:ns, 0:nb_, :], func=AF.Exp),
                   r=[tk("ps", SCI[sci])], w=[tk("pT", pi)])
            yield
            for t_, (kt_ap, v_ap, ns_, q0, diag, rtoks) in enumerate(tiles_):
                for m in range(2):
                    first = (si_ == 0 and t_ == 0 and m == 0)
                    last = (si_ == nsteps - 1 and t_ == len(tiles_) - 1 and m == 1)
                    pg.add("pe", lambda e, t_=t_, m=m, v_ap=v_ap, ns_=ns_, first=first, last=last: e.matmul(
                        av[0:R, m, 0:129], lhsT=ptv[0:ns_, t_ * 2 + m, :], rhs=v_ap[:, 0:129], start=first, stop=last),
                        r=rtoks + [tk("pT", pi)], w=[atok])
            yield
        yield from attend_finalize(h, av, atok, R, out_col)

    def load_cache(h, sq):
        pg.add("pool", lambda e: e.dma_start(out=kc1[:], in_=cache_k[sq, :, h, :].rearrange("(j p) e -> p j e", p=P)),
               w=[tk("kc", 0)], dma=("kc", 0))
        pg.add("pool", lambda e: e.dma_start(out=vs1_[:, 0:16, 0:128],
                                             in_=cache_v[sq, :, h, :].rearrange("(j p) e -> p j e", p=P)),
               w=[tk("vs", 0)], dma=("vs", 0))

    def gen_cache_T():
        for half in range(2):
            for jj in range(8):
                j = half * 8 + jj
                pg.add("pe", lambda e, j=j, jj=jj: e.transpose(out=PT[:, jj * P:(jj + 1) * P], in_=kc1[:, j, :],
                                                               identity=identb[:]),
                       r=[tk("kc", 0), tk("identb")], w=[tk("pt")])
            pg.add("dve", lambda e, half=half: e.tensor_copy(out=kts1[:, half * 1024:(half + 1) * 1024], in_=PT[:]),
                   r=[tk("pt")], w=[tk("kts", 0)])
            yield

    def load_weights(h):
        wb = h % 2
        for j in range(7):
            off = WOFF[j] * D + h * P
            pg.add("pool", lambda e, j=j, off=off: e.dma_start(
                out=wh[wb][:, :, j * P:(j + 1) * P], in_=w_in[:, off:off + P].rearrange("(c p) n -> p c n", p=P)),
                w=[tk("wh", wb, j)], dma=("wh", wb, j))

    def gen_inproj(h, gi):
        wb = h % 2
        W = wh[wb]
        gb = (5 * h + gi) % 2
        t0, NT, tiles = groups[gi]
        xt = xtoks(t0, NT)
        for j in range(4):
            pi = r_pa.next()
            ptok = tk("ps", PAI[pi])
            for dc in range(8):
                pg.add("pe", lambda e, dc=dc: e.matmul(
                    PA[pi][:, 0:NT], lhsT=W[:, dc, j * P:(j + 1) * P], rhs=xT[:, dc, t0:t0 + NT],
                    start=(dc == 0), stop=(dc == 7)),
                    r=[tk("wh", wb, j)] + xt, w=[ptok])
                if dc == 99:
                    yield
            src = PA[pi][:, 0:NT]
            if j == 0:
                pg.add("dve", lambda e: e.tensor_copy(out=hq_sb[gb][:, 0:NT], in_=src), r=[ptok], w=[tk("hq", gb)])
            elif j == 1:
                pg.add("act", lambda e: e.activation(out=sign[gb][:, 0:NT], in_=src, func=AF.Exp, scale=-1.0),
                       r=[ptok], w=[tk("sign", gb)])
            elif j == 2:
                pg.add("act", lambda e: e.activation(out=qTs[gb][:, 0:NT], in_=src, func=AF.Copy, scale=DA_SCALE),
                       r=[ptok], w=[tk("qTs", gb)])
                for m in range(2):
                    pg.add("pool", lambda e, m=m: e.tensor_copy(
                        out=qT[gb][m * 64:(m + 1) * 64, m, 0:NT], in_=qTs[gb][m * 64:(m + 1) * 64, 0:NT]),
                        r=[tk("qTs", gb)], w=[tk("qT", gb)])
            else:
                if gi < 4:
                    pg.add("dve", lambda e: e.tensor_copy(out=ktp[:, t0:t0 + NT], in_=src), r=[ptok], w=[tk("ktp", gi)])
                else:
                    for sq in range(2):
                        pg.add("dve", lambda e, sq=sq: e.tensor_copy(out=ktn[sq][:, :], in_=PA[pi][:, sq * SQ:(sq + 1) * SQ]),
                               r=[ptok], w=[tk("ktn", sq)])
            yield
        sp_, sn_ = sigp[gb][:, 0:NT], sign[gb][:, 0:NT]
        pg.add("dve", lambda e: e.tensor_scalar(out=sp_, in0=sn_, scalar1=1.0, scalar2=None, op0=ALU.add),
               r=[tk("sign", gb)], w=[tk("sigp", gb)])
        for q_ in range(0, NT, 128):
            pg.add("dve", lambda e, q_=q_: e.reciprocal(out=sigp[gb][:, q_:q_ + 128], in_=sigp[gb][:, q_:q_ + 128]),
                   r=[tk("sigp", gb)], w=[tk("sigp", gb)])
            yield
        pg.add("dve", lambda e: e.tensor_tensor(out=sn_, in0=sn_, in1=sp_, op=ALU.mult),
               r=[tk("sign", gb), tk("sigp", gb)], w=[tk("sign", gb)])
        pg.add("dve", lambda e: e.tensor_scalar(out=sp_, in0=sp_, scalar1=omlv[:, h:h + 1], scalar2=lbv[:, h:h + 1],
                                                op0=ALU.mult, op1=ALU.add),
               r=[tk("sigp", gb), tk("omlv"), tk("lbv")], w=[tk("sigp", gb)])
        yield
        pg.add("act", lambda e: e.activation(out=sp_, in_=sp_, func=AF.Ln), r=[tk("sigp", gb)], w=[tk("sigp", gb)])
        pg.add("dve", lambda e: e.tensor_tensor_scan(out=bbuf[gb][:, 0:NT], data0=rmask[:, 0:NT], data1=sp_, initial=0.0,
                                                     op0=ALU.mult, op1=ALU.add),
               r=[tk("sigp", gb), tk("rmask")], w=[tk("bbuf", gb)])
        pg.add("act", lambda e: e.activation(out=ebuf[gb][:, 0:NT], in_=bbuf[gb][:, 0:NT], func=AF.Exp),
               r=[tk("bbuf", gb)], w=[tk("ebuf", gb)])
        pg.add("act", lambda e: e.activation(out=bbuf[gb][:, 0:NT], in_=bbuf[gb][:, 0:NT], func=AF.Exp, scale=-1.0),
               r=[tk("bbuf", gb)], w=[tk("bbuf", gb)])
        yield
        nch_g = NT // 64
        pg.add("dve", lambda e: e.tensor_tensor(
            out=qdz[gb][:, 0:nch_g // 2, 0::2, :],
            in0=hq_sb[gb][:, 0:NT].rearrange("p (a b c) -> p a b c", b=2, c=64),
            in1=ebuf[gb][:, 0:NT].rearrange("p (a b c) -> p a b c", b=2, c=64), op=ALU.mult),
            r=[tk("hq", gb), tk("ebuf", gb)], w=[tk("qdz", gb)])
        pg.add("dve", lambda e: e.scalar_tensor_tensor(
            out=kdT[gb][:, 0:NT], in0=sn_, scalar=omlv[:, h:h + 1], in1=bbuf[gb][:, 0:NT],
            op0=ALU.mult, op1=ALU.mult),
            r=[tk("sign", gb), tk("omlv"), tk("bbuf", gb)], w=[tk("kdT", gb)])
        yield
        for li, (tt0, R, kind, sq) in enumerate(tiles):
            pi = r_pa.next()
            ptok = tk("ps", PAI[pi])
            for dc in range(8):
                pg.add("pe", lambda e, dc=dc: e.matmul(
                    PA[pi][0:R, :], lhsT=xT[:, dc, tt0:tt0 + R], rhs=W[:, dc, 3 * P:7 * P],
                    start=(dc == 0), stop=(dc == 7)),
                    r=[tk("wh", wb, j) for j in (3, 4, 5, 6)] + xtoks(tt0, R), w=[ptok])
                if dc == 99:
                    yield
            ki = r_kvst.next()
            ttok = tk("tm32", ki)
            tmv = tm32[ki][:].rearrange("p (a b) -> p a b", b=P)
            pg.add("act", lambda e: e.copy(out=tm32[ki][0:R, :], in_=PA[pi][0:R, :]), r=[ptok], w=[ttok])
            if kind == "p":
                kd, vd = k_p[tt0:tt0 + R, h, :], v_p[tt0:tt0 + R, h, :]
            else:
                kd, vd = k_s[sq * SQ:(sq + 1) * SQ, h, :], v_s[sq * SQ:(sq + 1) * SQ, h, :]
            pg.add("sp", lambda e: e.dma_start(out=kd, in_=tmv[0:R, 0, :]), r=[ttok], dma=("tm32", ki))
            pg.add("sp", lambda e: e.dma_start(out=vd, in_=tmv[0:R, 3, :]), r=[ttok], dma=("tm32", ki))
            pg.add("pool", lambda e: e.tensor_copy(out=vh[gb][0:R, li, :], in_=tmv[0:R, 1, :]), r=[ttok], w=[tk("vh", gb, li)])
            pg.add("act", lambda e: e.activation(out=sog[gb][0:R, li, :], in_=tmv[0:R, 2, :], func=AF.Exp, scale=-1.0),
                   r=[ttok], w=[tk("sog", gb, li)])
            if kind == "p":
                ti = tt0 // P
                pg.add("pool", lambda e: e.tensor_copy(out=vp[:, ti, 0:128], in_=tmv[:, 3, :]), r=[ttok], w=[tk("vp", ti)])
            else:
                pg.add("pool", lambda e: e.tensor_copy(out=vn[sq][0:SQ, 0:128], in_=tmv[0:SQ, 3, :]), r=[ttok], w=[tk("vsn", sq)])
            so_ = sog[gb][0:R, li, :]
            pg.add("dve", lambda e: e.tensor_scalar(out=so_, in0=so_, scalar1=1.0, scalar2=None, op0=ALU.add),
                   r=[tk("sog", gb, li)], w=[tk("sog", gb, li)])
            pg.add("dve", lambda e: e.reciprocal(out=so_, in_=so_), r=[tk("sog", gb, li)], w=[tk("sog", gb, li)])
            yield

    hstate = {"s_cur": None}

    def gen_hgrn(h, gi):
        gb = (5 * h + gi) % 2
        t0, NT, tiles = groups[gi]
        phv = PH[:].rearrange("p (a b) -> p a b", b=P)
        ptok = tk("ps", 2)
        for li, (tt0, R, kind, sq) in enumerate(tiles):
            nch = R // 64
            c0 = tt0 - t0
            lt = c0 // P
            if kind == "p" and tt0 == 0:
                s_cur = r_s.next()
                pg.add("pool", lambda e: e.memset(s32[s_cur][:], 0.0), w=[tk("s32", s_cur)])
                pg.add("pool", lambda e: e.memset(sbf[s_cur][:], 0.0), w=[tk("sbf", s_cur)])
                hstate["s_cur"] = s_cur
            if kind == "s":
                s_cur = r_s.next()
                pg.add("sp", lambda e: e.dma_start(out=s32[s_cur][:], in_=state_hgrn[sq, h, :, :]),
                       w=[tk("s32", s_cur)], dma=("s32", s_cur))
                pg.add("pool", lambda e: e.tensor_copy(out=sbf[s_cur][:], in_=s32[s_cur][:]),
                       r=[tk("s32", s_cur)], w=[tk("sbf", s_cur)])
                hstate["s_cur"] = s_cur
            s_cur = hstate["s_cur"]
            pg.add("pe", lambda e: e.transpose(out=PT[0:R, 0:P], in_=kdT[gb][:, c0:c0 + R], identity=identb[:]),
                   r=[tk("kdT", gb), tk("identb")], w=[tk("pt")])
            kmi = r_kdtm.next()
            for c in range(nch):
                pg.add("dve", lambda e, c=c: e.tensor_copy(out=kdtm[kmi][c * 64:(c + 1) * 64, c, :],
                                                           in_=PT[c * 64:(c + 1) * 64, 0:P]),
                       r=[tk("pt")], w=[tk("kdtm", kmi)])
            if nch == 2:
                qd_mov = qdz[gb][:, lt, 0::2, :]
                att_out = phv[0:R, 0, 0:R].rearrange("p (a b) -> p a b", b=64)
            else:
                qd_mov = qdz[gb][:, 0, 2 * sq, :]
                att_out = phv[0:R, 0, 0:R]
            pg.add("pe", lambda e: e.matmul(att_out, lhsT=kdT[gb][:, c0:c0 + R], rhs=qd_mov, start=True, stop=True),
                   r=[tk("kdT", gb), tk("qdz", gb)], w=[ptok])
            ai = r_atm.next()
            pg.add("dve", lambda e: e.tensor_tensor(out=atm[ai][0:R, 0:R], in0=phv[0:R, 0, 0:R], in1=hmask[0:R, 0:R],
                                                    op=ALU.mult),
                   r=[ptok, tk("hmask")], w=[tk("atm", ai)])
            yield
            for c in range(nch):
                if nch == 2:
                    lhs = kdtm[kmi][:, c, :]
                    rhs = vh[gb][:, li, :]
                else:
                    lhs = kdtm[kmi][0:64, 0, :]
                    rhs = vh[gb][0:64, li, :]
                pg.add("pe", lambda e, c=c, lhs=lhs, rhs=rhs: e.matmul(phv[:, 1 + c, :], lhsT=lhs, rhs=rhs,
                                                                     start=True, stop=True),
                       r=[tk("kdtm", kmi), tk("vh", gb, li)], w=[ptok])
            s_before = [s_cur]
            for c in range(nch):
                ti = r_t1b.next()
                sc_ = s_cur
                pg.add("dve", lambda e, c=c, ti=ti, sc_=sc_: e.tensor_tensor(out=t1b[ti][:], in0=phv[:, 1 + c, :],
                                                                           in1=s32[sc_][:], op=ALU.add),
                       r=[ptok, tk("s32", sc_)], w=[tk("t1b", ti)])
                s_new = r_s.next()
                acol = c0 + c * 64 + 63
                pg.add("dve", lambda e, ti=ti, s_new=s_new, acol=acol: e.tensor_scalar(
                    out=s32[s_new][:], in0=t1b[ti][:], scalar1=ebuf[gb][:, acol:acol + 1], scalar2=None, op0=ALU.mult),
                    r=[tk("t1b", ti), tk("ebuf", gb)], w=[tk("s32", s_new)])
                pg.add("pool", lambda e, ti=ti, s_new=s_new, acol=acol: e.tensor_scalar(
                    out=sbf[s_new][:], in0=t1b[ti][:], scalar1=ebuf[gb][:, acol:acol + 1], scalar2=1.0,
                    op0=ALU.mult, op1=ALU.mult),
                    r=[tk("t1b", ti), tk("ebuf", gb)], w=[tk("sbf", s_new)])
                s_cur = s_new
                s_before.append(s_cur)
            hstate["s_cur"] = s_cur
            if kind == "p" and tt0 == T - P:
                pg.add("sp", lambda e: e.dma_start(out=hg_p[h, :, :], in_=s32[s_cur][:]),
                       r=[tk("s32", s_cur)], dma=("s32", s_cur))
            if kind == "s":
                pg.add("sp", lambda e: e.dma_start(out=hg_s[sq, h, :, :], in_=s32[s_cur][:]),
                       r=[tk("s32", s_cur)], dma=("s32", s_cur))
            yield
            pg.add("pe", lambda e: e.matmul(phv[0:R, 3, :], lhsT=atm[ai][0:R, 0:R], rhs=vh[gb][0:R, li, :],
                                            start=True, stop=False),
                   r=[tk("atm", ai), tk("vh", gb, li)], w=[ptok])
            for c in range(nch):
                if nch == 2:
                    lhs = qdz[gb][:, lt, :, :].rearrange("p a b -> p (a b)")[:, c * 64:c * 64 + P]
                else:
                    lhs = qdz[gb][:, 0, 2 * sq, :]
                sb_i = s_before[c]
                pg.add("pe", lambda e, lhs=lhs, sb_i=sb_i, c=c: e.matmul(
                    phv[0:R, 3, :], lhsT=lhs, rhs=sbf[sb_i][:], start=False, stop=(c == nch - 1)),
                    r=[tk("qdz", gb), tk("sbf", sb_i)], w=[ptok])
            si = r_ss.next()
            pg.add("act", lambda e: e.activation(out=junk[0:R, :], in_=phv[0:R, 3, :], func=AF.Square,
                                                 accum_out=ssb[si][0:R, 0:1]),
                   r=[ptok], w=[tk("junk"), tk("ssb", si)])
            rstd_ops(si, R)
            yield
            yi = r_yh.next()
            pg.add("dve", lambda e: e.scalar_tensor_tensor(
                out=yh[yi][0:R, :], in0=phv[0:R, 3, :], scalar=ssb[si][0:R, 1:2], in1=sog[gb][0:R, li, :],
                op0=ALU.mult, op1=ALU.mult),
                r=[ptok, tk("ssb", si), tk("sog", gb, li)], w=[tk("yh", yi)])
            yield
            pg.add("pe", lambda e: e.transpose(out=PT[:, 0:R], in_=yh[yi][0:R, :], identity=identb[0:R, 0:R]),
                   r=[tk("yh", yi), tk("identb")], w=[tk("pt")])
            pg.add("act", lambda e: e.activation(out=yT[:, h, tt0:tt0 + R], in_=PT[:, 0:R], func=AF.Copy,
                                                 scale=hgg[:, 0:1]),
                   r=[tk("pt"), tk("hgg")], w=[tk("yT", h, tt0 // P)])
            yield

    def gen_attn(h, gi):
        gb = (5 * h + gi) % 2
        t0, NT, tiles = groups[gi]
        if gi < 4:
            for qg in range(2):
                qt0 = t0 + qg * 256
                nkt = qt0 // P + 2
                keytiles = []
                for j in range(nkt):
                    if j < qt0 // P:
                        q0, diag = 0, None
                    else:
                        q0, diag = (j - qt0 // P) * P, j - qt0 // P
                    keytiles.append((ktp[:, j * P:(j + 1) * P], vp[:, j, :], P, q0, diag,
                                     [tk("ktp", j // 4), tk("vp", j), tk("vp1")]))
                yield from gen_attend(h, gb, qg * 256, 256, P, keytiles, [qt0, qt0 + P])
        else:
            for sq in range(2):
                if sq == 1:
                    load_cache(h, 1)
                yield from gen_cache_T()
                keytiles = []
                for j in range(16):
                    keytiles.append((kts1[:, j * P:(j + 1) * P], vs1_[:, j, :], P, 0, None,
                                     [tk("kts", 0), tk("vs", 0), tk("vs1", 0)]))
                keytiles.append((ktn[sq][:, :], vn[sq][0:SQ, :], SQ, 0, None,
                                 [tk("ktn", sq), tk("vsn", sq), tk("vn1", sq)]))
                yield from gen_attend_sample(h, gb, sq * SQ, keytiles, T + sq * SQ)

    for i in range(2):
        pg.add("pool", lambda e, i=i: e.memset(kdtm[i][:], 0.0), w=[tk("kdtm", i)])
    load_weights(0)
    tail = []
    for h in range(H):
        if h + 1 < H:
            load_weights(h + 1)
        run_interleaved(tail + [gen_inproj(h, 0)])
        load_cache(h, 0)
        for gi in range(4):
            run_interleaved([gen_attn(h, gi), gen_hgrn(h, gi), gen_inproj(h, gi + 1)])
        tail = [gen_attn(h, 4), gen_hgrn(h, 4)]
    run_interleaved(tail)
    pg.barrier()
    chk(200)

    arena["off"] = mark
    mT = sb("mT", [P, 8, NTOK], BF16)
    mark2 = arena["off"]
    w2 = [sb("w2_%d" % i, [P, 8, 4, P], BF16) for i in range(2)]
    s1b = [sb("s1b%d" % i, [P, 512]) for i in range(2)]
    s2b = [sb("s2b%d" % i, [P, 512]) for i in range(2)]
    m1b = [sb("m1b%d" % i, [P, 512]) for i in range(2)]
    m2b = [sb("m2b%d" % i, [P, 512]) for i in range(2)]
    bankset = [[0, 1, 3, 4], [5, 6, 2, 0]]
    it = 0
    def issue_w2(nb):
        b2 = nb % 2
        srcs = [w_in[:, 7 * D + nb * P:7 * D + (nb + 1) * P], w_in[:, 8 * D + nb * P:8 * D + (nb + 1) * P],
                w_br_hg[:, nb * P:(nb + 1) * P], w_br_da[:, nb * P:(nb + 1) * P]]
        for j in range(4):
            pg.add("pool", lambda e, b2=b2, j=j, src=srcs[j]: e.dma_start(
                out=w2[b2][:, :, j, :], in_=src.rearrange("(c p) n -> p c n", p=P)),
                w=[tk("w2", b2, j)], dma=("w2", b2, j))

    issue_w2(0)
    for nb in range(8):
        b2 = nb % 2
        if nb + 1 < 8:
            issue_w2(nb + 1)
        for gi, (t0, NT, tiles) in enumerate(groups):
            banks = [0, 1, 3, 4] if it % 2 == 0 else [5, 6, 2, 1]
            if it % 2 == 1:
                banks = [5, 6, 2, 0]
            it += 1
            gb = gi % 2
            xt = xtoks(t0, NT)
            ytk_hg = [tk("yT", hh, i) for hh in range(8) for i in range(t0 // P, (t0 + NT + P - 1) // P)]
            ytk_da = [tk("yT", 8 + hh, i) for hh in range(8) for i in range(t0 // P, (t0 + NT + P - 1) // P)]
            for j in range(4):
                bk = banks[j]
                for c in range(8):
                    if j < 2:
                        rhs = xT[:, c, t0:t0 + NT]
                        rt = xt
                    elif j == 2:
                        rhs = yT[:, c, t0:t0 + NT]
                        rt = ytk_hg
                    else:
                        rhs = yT[:, 8 + c, t0:t0 + NT]
                        rt = ytk_da
                    pg.add("pe", lambda e, bk=bk, b2=b2, j=j, c=c, rhs=rhs, NT=NT: e.matmul(
                        PS[bk][:, 0:NT], lhsT=w2[b2][:, c, j, :], rhs=rhs, start=(c == 0), stop=(c == 7)),
                        r=[tk("w2", b2, j)] + rt, w=[tk("ps", bk)])
            pg.add("act", lambda e, gb=gb, bk=banks[0], NT=NT: e.activation(out=s1b[gb][:, 0:NT], in_=PS[bk][:, 0:NT], func=AF.Sigmoid),
                   r=[tk("ps", banks[0])], w=[tk("s1b", gb)])
            pg.add("act", lambda e, gb=gb, bk=banks[1], NT=NT: e.activation(out=s2b[gb][:, 0:NT], in_=PS[bk][:, 0:NT], func=AF.Sigmoid),
                   r=[tk("ps", banks[1])], w=[tk("s2b", gb)])
            pg.add("dve", lambda e, gb=gb, bk=banks[2], NT=NT: e.tensor_tensor(out=m1b[gb][:, 0:NT], in0=PS[bk][:, 0:NT],
                                                                               in1=s1b[gb][:, 0:NT], op=ALU.mult),
                   r=[tk("ps", banks[2]), tk("s1b", gb)], w=[tk("m1b", gb)])
            pg.add("dve", lambda e, gb=gb, bk=banks[3], NT=NT: e.tensor_tensor(out=m2b[gb][:, 0:NT], in0=PS[bk][:, 0:NT],
                                                                               in1=s2b[gb][:, 0:NT], op=ALU.mult),
                   r=[tk("ps", banks[3]), tk("s2b", gb)], w=[tk("m2b", gb)])
            pg.add("pool", lambda e, gb=gb, nb=nb, t0=t0, NT=NT: e.tensor_tensor(
                out=mT[:, nb, t0:t0 + NT], in0=m1b[gb][:, 0:NT], in1=m2b[gb][:, 0:NT], op=ALU.add),
                r=[tk("m1b", gb), tk("m2b", gb)],
                w=[tk("mT", i) for i in range(t0 // P, (t0 + NT + P - 1) // P)])

    chk(201)
    pg.barrier()
    arena["off"] = mark2
    wout = sb("wout", [P, 8, D], BF16)
    g1b = sb("g1b", [P, D])
    b1b = sb("b1b", [P, D])
    xf = [sb("xf%d" % i, [P, D]) for i in range(3)]
    rb = [sb("rb%d" % i, [P, D]) for i in range(3)]
    hb = [sb("hb%d" % i, [P, D], BF16) for i in range(3)]

    pg.add("pool", lambda e: e.dma_start(out=wout[:], in_=w_out.rearrange("(c p) n -> p c n", p=P)),
           w=[tk("wout")], dma="wout")
    pg.add("sp", lambda e: e.dma_start(out=g1b[:], in_=ln1_g.to_broadcast([P, D])), w=[tk("g1b")], dma="g1b")
    pg.add("sp", lambda e: e.dma_start(out=b1b[:], in_=ln1_b.to_broadcast([P, D])), w=[tk("b1b")], dma="b1b")
    pg.add("dve", lambda e: e.tensor_scalar(out=g1b[:], in0=g1b[:], scalar1=ALPHA, scalar2=None, op0=ALU.mult),
           r=[tk("g1b")], w=[tk("g1b")])
    pg.add("dve", lambda e: e.tensor_scalar(out=b1b[:], in0=b1b[:], scalar1=ALPHA, scalar2=None, op0=ALU.mult),
           r=[tk("b1b")], w=[tk("b1b")])
    pg.add("pool", lambda e: e.memset(hT2[:, :, 0:2], 0.0), w=[tk("hTz")])
    pg.add("pool", lambda e: e.memset(hT2[:, :, 2050:2052], 0.0), w=[tk("hTz")])
    pg.add("pool", lambda e: e.memset(hT2[:, :, 2116:2118], 0.0), w=[tk("hTz")])

    def layer_norm_tile(src_ap, bi, gtile, btile, gtok, btok, out_ap, out_tok, src_tok):
        for hf_ in range(2):
            pg.add("dve", lambda e, hf_=hf_: e.bn_stats(out=stt[bi][:, hf_ * 6:(hf_ + 1) * 6], in_=src_ap[:, hf_ * 512:(hf_ + 1) * 512]),
                   r=[src_tok], w=[tk("stt", bi)])
        pg.add("dve", lambda e: e.bn_aggr(out=mv[bi][:, 0:2], in_=stt[bi][:]), r=[tk("stt", bi)], w=[tk("mv", bi)])
        pg.add("act", lambda e: e.activation(out=mv[bi][:, 2:3], in_=mv[bi][:, 1:2], func=AF.Ln, bias=epsc[:, :]),
               r=[tk("mv", bi), tk("epsc")], w=[tk("mv", bi)])
        pg.add("act", lambda e: e.activation(out=mv[bi][:, 2:3], in_=mv[bi][:, 2:3], func=AF.Exp, scale=-0.5),
               r=[tk("mv", bi)], w=[tk("mv", bi)])
        yield
        pg.add("dve", lambda e: e.tensor_scalar(out=src_ap, in0=src_ap, scalar1=mv[bi][:, 0:1], scalar2=mv[bi][:, 2:3],
                                                op0=ALU.subtract, op1=ALU.mult),
               r=[src_tok, tk("mv", bi)], w=[src_tok])
        pg.add("dve", lambda e: e.tensor_tensor(out=src_ap, in0=src_ap, in1=gtile[:], op=ALU.mult),
               r=[src_tok, gtok], w=[src_tok])
        yield
        pg.add("pool", lambda e: e.tensor_tensor(out=out_ap, in0=src_ap, in1=btile[:], op=ALU.add),
               r=[src_tok, btok], w=[out_tok])
        yield

    def run_staggered(make_gen, n, depth):
        active = []
        nxt = 0
        while active or nxt < n:
            if nxt < n and len(active) < depth:
                active.append(make_gen(nxt))
                nxt += 1
            for g in list(active):
                try:
                    next(g)
                except StopIteration:
                    active.remove(g)

    def gen_2b(i):
        bi = i % 3
        src = x_p[i * P:(i + 1) * P, :] if i < 16 else x_s
        pg.add("sp", lambda e, bi=bi, src=src: e.dma_start(out=xf[bi][:], in_=src), w=[tk("xf", bi)], dma=("xf", bi))
        for hf_ in range(2):
            bk = ([3, 4] if i % 2 == 0 else [5, 6])[hf_]
            for c in range(8):
                pg.add("pe", lambda e, bk=bk, c=c, i=i, hf_=hf_: e.matmul(
                    PS[bk][:, :], lhsT=mT[:, c, i * P:(i + 1) * P], rhs=wout[:, c, hf_ * 512:(hf_ + 1) * 512],
                    start=(c == 0), stop=(c == 7)),
                    r=[tk("mT", i), tk("wout")], w=[tk("ps", bk)])
            pg.add("dve", lambda e, bk=bk, bi=bi, hf_=hf_: e.scalar_tensor_tensor(
                out=rb[bi][:, hf_ * 512:(hf_ + 1) * 512], in0=xf[bi][:, hf_ * 512:(hf_ + 1) * 512], scalar=ALPHA,
                in1=PS[bk][:, :], op0=ALU.mult, op1=ALU.add),
                r=[tk("xf", bi), tk("ps", bk)], w=[tk("rb", bi)])
            yield
        yield from layer_norm_tile(rb[bi][:], bi, g1b, b1b, tk("g1b"), tk("b1b"), ACCT[:, i, :], tk("acc", i), tk("rb", bi))
        pg.add("act", lambda e, bi=bi, i=i: e.activation(out=hb[bi][:], in_=ACCT[:, i, :], func=AF.Copy, scale=1.0 / ALPHA),
               r=[tk("acc", i)], w=[tk("hb", bi)])
        yield
        for c in range(8):
            pg.add("pe", lambda e, bi=bi, c=c: e.transpose(out=PT[:, c * P:(c + 1) * P], in_=hb[bi][:, c * P:(c + 1) * P],
                                                           identity=identb[:]),
                   r=[tk("hb", bi), tk("identb")], w=[tk("pt")])
        if i < 16:
            pg.add("dve", lambda e, i=i: e.tensor_copy(out=hT2[:, :, 2 + i * P:2 + (i + 1) * P],
                                                       in_=PT[:].rearrange("p (a b) -> p a b", b=P)),
                   r=[tk("pt")], w=[tk("hT", i)])
        else:
            pg.add("dve", lambda e: e.tensor_copy(
                out=hT2[:, :, 2052:2184].rearrange("p a (s c) -> p a s c", c=66)[:, :, :, 0:64],
                in_=PT[:].rearrange("p (a s c) -> p a s c", s=2, c=64)),
                r=[tk("pt")], w=[tk("hT", 16)])
        yield

    run_staggered(gen_2b, 17, 3)

    chk(202)
    pg.barrier()
    arena["off"] = mark
    wup = [sb("wup%d" % i, [P, 8, 2, P], BF16) for i in range(2 * JG)]
    wdn = [sb("wdn%d" % i, [P, D], BF16) for i in range(2 * JG)]
    Ab = [sb("A%d" % i, [P, AW], BF16) for i in range(2 * JG)]
    cwork = [[sb("cw%d_%d" % (i, k), [P, 256]) for k in range(5)] for i in range(2)]
    co = sb("co", [P, 3, 2, 44])
    cot = sb("cot", [88, 3, P])
    r_cw = Ring(2)

    cgroups = [(256 * g, 258, 256 * g, "p") for g in range(8)] + [(2050, 132, 2048, "s")]

    def hT_toks(c0, n):
        toks = [tk("hTz")]
        for i in range(17):
            lo = 2 + i * P if i < 16 else 2052
            hi = lo + P if i < 16 else 2184
            if c0 < hi and c0 + n > lo:
                toks.append(tk("hT", i))
        return toks

    njg = (NJ + JG - 1) // JG
    jbufs = {}

    def issue_wup(jg):
        for jj, j in enumerate(range(jg * JG, min(NJ, (jg + 1) * JG))):
            bidx = (jg % 2) * JG + jj
            for vg in range(2):
                off = vg * DFF + j * P
                pg.add("pool", lambda e, bidx=bidx, vg=vg, off=off: e.dma_start(
                    out=wup[bidx][:, :, vg, :], in_=w_up[:, off:off + P].rearrange("(c p) n -> p c n", p=P)),
                    w=[tk("wup", bidx, vg)], dma=("wup", bidx, vg))

    def issue_wdn(jg):
        for jj, j in enumerate(range(jg * JG, min(NJ, (jg + 1) * JG))):
            bidx = (jg % 2) * JG + jj
            pg.add("pool", lambda e, bidx=bidx, j=j: e.dma_start(out=wdn[bidx][:], in_=w_down[j * P:(j + 1) * P, :]),
                   w=[tk("wdn", bidx)], dma=("wdn", bidx))

    issue_wup(0)

    def gen_up(jg):
        js = list(range(jg * JG, min(NJ, (jg + 1) * JG)))
        bufs = []
        jbufs[jg] = bufs
        issue_wdn(jg)
        if jg + 1 < njg:
            issue_wup(jg + 1)
        for jj, j in enumerate(js):
            bidx = (jg % 2) * JG + jj
            bufs.append(bidx)

            for (hc0, NC, ac0, kind) in cgroups:
                NO = NC - 2
                ht = hT_toks(hc0, NC)
                cwi = r_cw.next()
                cw = cwork[cwi]
                res = []
                for vg in range(2):
                    bk = [3, 4][vg] if (cwi == 0) else [5, 6][vg]
                    for c in range(8):
                        pg.add("pe", lambda e, bk=bk, bidx=bidx, vg=vg, c=c, hc0=hc0, NC=NC, kind=kind: e.matmul(
                            PS[bk][:, 0:NC], lhsT=wup[bidx][:, c, vg, :], rhs=hT2[:, c, hc0:hc0 + NC],
                            start=(c == 0), stop=(c == 7)),
                            r=[tk("wup", bidx, vg)] + ht, w=[tk("ps", bk)])
                    blk = vg * NJ + j
                    if kind == "s":
                        pg.add("dve", lambda e, bk=bk, blk=blk: e.tensor_copy(
                            out=PS[bk][:, 0:132].rearrange("p (s c) -> p s c", c=66)[:, :, 0:2],
                            in_=cv[:, 4:8, blk].rearrange("p (s r) -> p s r", r=2)),
                            r=[tk("cv")], w=[tk("ps", bk)])
                    t1 = cw[vg * 2]
                    t2 = cw[vg * 2 + 1]
                    pg.add("act", lambda e, bk=bk, t1=t1, blk=blk, NC=NC, NO=NO: e.activation(
                        out=t1[:, 0:NO], in_=PS[bk][:, 2:NC], func=AF.Identity, scale=cv[:, 2, blk:blk + 1],
                        bias=cv[:, 3, blk:blk + 1]),
                        r=[tk("ps", bk), tk("cv")], w=[tk("cw", cwi, vg * 2)])
                    pg.add("dve", lambda e, bk=bk, t1=t1, t2=t2, blk=blk, NC=NC, NO=NO: e.scalar_tensor_tensor(
                        out=t2[:, 0:NO], in0=PS[bk][:, 1:NC - 1], scalar=cv[:, 1, blk:blk + 1], in1=t1[:, 0:NO],
                        op0=ALU.mult, op1=ALU.add),
                        r=[tk("ps", bk), tk("cv"), tk("cw", cwi, vg * 2)], w=[tk("cw", cwi, vg * 2 + 1)])
                    pg.add("dve", lambda e, bk=bk, t1=t1, t2=t2, blk=blk, NC=NC, NO=NO: e.scalar_tensor_tensor(
                        out=t1[:, 0:NO], in0=PS[bk][:, 0:NO], scalar=cv[:, 0, blk:blk + 1], in1=t2[:, 0:NO],
                        op0=ALU.mult, op1=ALU.add),
                        r=[tk("ps", bk), tk("cv"), tk("cw", cwi, vg * 2 + 1)], w=[tk("cw", cwi, vg * 2)])
                    res.append(t1)
                    if kind == "p" and hc0 == 256 * 7:
                        pg.add("act", lambda e, bk=bk, blk=blk, NC=NC: e.copy(out=co[:, 0, :, blk], in_=PS[bk][:, NC - 2:NC]),
                               r=[tk("ps", bk)], w=[tk("co")])
                    if kind == "s":
                        pg.add("act", lambda e, bk=bk, blk=blk: e.copy(
                            out=co[:, 1:3, :, blk], in_=PS[bk][:, 0:132].rearrange("p (s c) -> p s c", c=66)[:, :, 64:66]),
                            r=[tk("ps", bk)], w=[tk("co")])
                sg = cw[4]
                pg.add("act", lambda e, sg=sg, g_=res[1], NO=NO: e.activation(out=sg[:, 0:NO], in_=g_[:, 0:NO], func=AF.Silu),
                       r=[tk("cw", cwi, 2)], w=[tk("cw", cwi, 4)])
                if kind == "p":
                    atoks = [tk("A", bidx, ac0 // P), tk("A", bidx, ac0 // P + 1)]
                else:
                    atoks = [tk("A", bidx, 16)]
                if kind == "p":
                    pg.add("pool", lambda e, bidx=bidx, ac0=ac0, NO=NO, sg=sg, v_=res[0]: e.tensor_tensor(
                        out=Ab[bidx][:, ac0:ac0 + NO], in0=v_[:, 0:NO], in1=sg[:, 0:NO], op=ALU.mult),
                        r=[tk("cw", cwi, 0), tk("cw", cwi, 4)], w=atoks)
                else:
                    pg.add("pool", lambda e, bidx=bidx, sg=sg, v_=res[0]: e.tensor_tensor(
                        out=Ab[bidx][:, 2048:2176].rearrange("p (s c) -> p s c", c=64),
                        in0=v_[:, 0:132].rearrange("p (s c) -> p s c", c=66)[:, :, 0:64],
                        in1=sg[:, 0:132].rearrange("p (s c) -> p s c", c=66)[:, :, 0:64], op=ALU.mult),
                        r=[tk("cw", cwi, 0), tk("cw", cwi, 4)], w=atoks)
                yield

    def gen_down(jg):
        bufs = jbufs[jg]
        for i in range(17):
            for hf_ in range(2):
                bk = [0, 1][hf_]
                for jj, bidx in enumerate(bufs):
                    lhs = Ab[bidx][:, i * P:(i + 1) * P]
                    pg.add("pe", lambda e, bk=bk, lhs=lhs, bidx=bidx, hf_=hf_, jj=jj, n=len(bufs): e.matmul(
                        PS[bk][:, :], lhsT=lhs, rhs=wdn[bidx][:, hf_ * 512:(hf_ + 1) * 512],
                        start=(jj == 0), stop=(jj == n - 1)),
                        r=[tk("A", bidx, i), tk("wdn", bidx)], w=[tk("ps", bk)])
                pg.add("dve", lambda e, bk=bk, i=i, hf_=hf_: e.tensor_tensor(
                    out=ACCT[:, i, hf_ * 512:(hf_ + 1) * 512], in0=PS[bk][:, :], in1=ACCT[:, i, hf_ * 512:(hf_ + 1) * 512],
                    op=ALU.add),
                    r=[tk("ps", bk), tk("acc", i)], w=[tk("acc", i)])
                yield

    run_interleaved([gen_up(0)])
    for jg in range(njg):
        run_interleaved([gen_down(jg), gen_up(jg + 1) if jg + 1 < njg else None])

    chk(203)
    for inst in range(3):
        pg.add("pe", lambda e, inst=inst: e.transpose(out=PS[2][0:88, 0:P], in_=co[:, inst, :, :].rearrange("p a b -> p (a b)"),
                                                      identity=ident[:]),
               r=[tk("co"), tk("ident")], w=[tk("ps", 2)])
        pg.add("dve", lambda e, inst=inst: e.tensor_copy(out=cot[:, inst, :], in_=PS[2][0:88, 0:P]),
               r=[tk("ps", 2)], w=[tk("cot", inst)])
        for r_ in range(2):
            dst = conv_p[r_, :] if inst == 0 else conv_s[inst - 1, r_, :]
            out_events.append(pg.add("sp", lambda e, inst=inst, r_=r_, dst=dst: e.dma_start(
                out=dst.rearrange("(b p) -> b p", p=P), in_=cot[r_ * 44:(r_ + 1) * 44, inst, :]),
                r=[tk("cot", inst)], dma=("cot", inst)))

    g2b = sb("g2b", [P, D])
    b2b = sb("b2b", [P, D])
    yo = [sb("yo%d" % i, [P, D]) for i in range(2)]
    pg.add("sp", lambda e: e.dma_start(out=g2b[:], in_=ln2_g.to_broadcast([P, D])), w=[tk("g2b")], dma="g2b")
    pg.add("sp", lambda e: e.dma_start(out=b2b[:], in_=ln2_b.to_broadcast([P, D])), w=[tk("b2b")], dma="b2b")
    def gen_3b(i):
        bi = i % 2
        yield from layer_norm_tile(ACCT[:, i, :], bi, g2b, b2b, tk("g2b"), tk("b2b"), yo[bi][:], tk("yo", bi), tk("acc", i))
        dst = y_p[i * P:(i + 1) * P, :] if i < 16 else y_s
        out_events.append(pg.add("sp", lambda e, bi=bi, dst=dst: e.dma_start(out=dst, in_=yo[bi][:]),
                                 r=[tk("yo", bi)], dma=("yo", bi)))
        yield

    run_staggered(gen_3b, 17, 2)

    pg.barrier()


_NC_CACHE = {}


def kernel(x_prompt, x_sample, cache_k, cache_v, state_hgrn, state_ffn_conv, w_in, hg_lb_logits,
           hg_norm_g, da_lambda_q1, da_lambda_k1, da_lambda_q2, da_lambda_k2, da_subln_g,
           w_br_hg, w_br_da, w_out, ln1_g, ln1_b, w_up, conv_w, conv_b, w_down, ln2_g, ln2_b):
    f = lambda a: np.ascontiguousarray(np.asarray(a, dtype=np.float32))
    if "nc" not in _NC_CACHE:
        _NC_CACHE["nc"] = build_nc()
    nc = _NC_CACHE["nc"]
    shared = {
        "w_in": f(w_in[0]),
        "lb_logits": f(hg_lb_logits).reshape(16, 128),
        "hg_norm_g": f(hg_norm_g).reshape(128, 1),
        "lq1": f(da_lambda_q1), "lk1": f(da_lambda_k1), "lq2": f(da_lambda_q2), "lk2": f(da_lambda_k2),
        "subln_g": f(da_subln_g).reshape(128, 1),
        "w_br_hg": f(w_br_hg[0]), "w_br_da": f(w_br_da[0]), "w_out": f(w_out[0]),
        "ln1_g": f(ln1_g), "ln1_b": f(ln1_b),
        "w_up": f(w_up[0]),
        "conv_wb": f(np.concatenate([np.asarray(conv_w[0]), np.asarray(conv_b)], axis=0)),
        "w_down": f(w_down[0]),
        "ln2_g": f(ln2_g), "ln2_b": f(ln2_b),
    }
    in_maps = []
    for c in range(8):
        m = dict(shared)
        m["x_p"] = f(x_prompt[c])
        m["x_s"] = f(x_sample[2 * c:2 * c + 2]).reshape(TS, D)
        m["cache_k"] = f(cache_k[0, 2 * c:2 * c + 2])
        m["cache_v"] = f(cache_v[0, 2 * c:2 * c + 2])
        m["state_hgrn"] = f(state_hgrn[0, 2 * c:2 * c + 2])
        m["state_conv"] = f(state_ffn_conv[0, 2 * c:2 * c + 2]).reshape(4, 2 * DFF)
        in_maps.append(m)
    res = run_bass_kernel_spmd(nc, in_maps, core_ids=list(range(8)))
    R = res.results
    y_prompt = np.stack([R[c]["y_p"] for c in range(8)], axis=0)
    y_sample = np.concatenate([R[c]["y_s"].reshape(2, SQ, D) for c in range(8)], axis=0)
    k_prompt = np.stack([R[c]["k_p"] for c in range(8)], axis=0)[None]
    v_prompt = np.stack([R[c]["v_p"] for c in range(8)], axis=0)[None]
    hgrn_prompt = np.stack([R[c]["hg_p"] for c in range(8)], axis=0)[None]
    conv_prompt = np.stack([R[c]["conv_p"] for c in range(8)], axis=0)[None]
    k_sample = np.concatenate([R[c]["k_s"].reshape(2, SQ, H, P) for c in range(8)], axis=0)[None]
    v_sample = np.concatenate([R[c]["v_s"].reshape(2, SQ, H, P) for c in range(8)], axis=0)[None]
    hgrn_sample = np.concatenate([R[c]["hg_s"] for c in range(8)], axis=0)[None]
    conv_sample = np.concatenate([R[c]["conv_s"] for c in range(8)], axis=0)[None]
    return (y_prompt.astype(np.float32), y_sample.astype(np.float32), k_prompt.astype(np.float32),
            v_prompt.astype(np.float32), hgrn_prompt.astype(np.float32), conv_prompt.astype(np.float32),
            k_sample.astype(np.float32), v_sample.astype(np.float32), hgrn_sample.astype(np.float32),
            conv_sample.astype(np.float32))
```
